# Optimizing a Trainium2 kernel written in Bass

```python
import math
import numpy as np
import jax
import jax.numpy as jnp
from jax import lax

D_MODEL = 2048
BATCH = 32
SEQ = 256
DEPTH = 2
DEC_BATCH = 2
DEC_SEQ = 4096
PAST_LEN = 256

GRID_W = 64
D_MIX = D_MODEL // 2
N_MOD = 6
S5_CH = 16
S5_G = D_MIX // S5_CH
S5_P = 64
RET_H = 8
RET_DK = D_MIX // RET_H
RET_DV = D_MIX // RET_H
RET_CHUNK = 128
MLA_DN = 128
MLA_DR = 64
MLA_DV = 128
MLA_H = D_MIX // MLA_DV
MLA_Q_LORA = 768
MLA_KV_LORA = 512
ROPE_BASE = 10000.0
Q_BLOCK = 128
D_FF = 11 * D_MODEL // 4
NORM_EPS = 1e-6
IN_WIDTHS = (D_MIX, D_MIX, D_MIX, D_MIX, D_MIX, MLA_Q_LORA, MLA_KV_LORA, MLA_DR, D_MODEL, D_MODEL, D_MODEL)

kernel_name = 'hybrid_prefix_s5_retention_mla_trunk_step'


def rmsnorm(x, g):
    xf = x.astype(jnp.float32)
    y = xf * lax.rsqrt(jnp.mean(xf * xf, axis=-1, keepdims=True) + NORM_EPS)
    return (y * g.astype(jnp.float32)).astype(x.dtype)


def adaln(cvec, w_ada, b_ada):
    m = jax.nn.silu(cvec) @ w_ada + b_ada
    return jnp.split(m, N_MOD, axis=-1)


def s5_discretise(lam_re, lam_im, log_dt):
    f32 = jnp.float32
    dt = jnp.exp(log_dt.astype(f32))[:, None]
    lr, li = lam_re.astype(f32), lam_im.astype(f32)
    ar, ai = lr * dt, li * dt
    mag = jnp.exp(ar)
    abar_re, abar_im = mag * jnp.cos(ai), mag * jnp.sin(ai)
    nr, ni = abar_re - 1.0, abar_im
    den = lr * lr + li * li
    coef_re = (nr * lr + ni * li) / den
    coef_im = (ni * lr - nr * li) / den
    return ar, ai, abar_re, abar_im, coef_re, coef_im


def s5_scan(abar_re, abar_im, b_re, b_im, reverse):
    a_re = jnp.broadcast_to(abar_re, b_re.shape)
    a_im = jnp.broadcast_to(abar_im, b_re.shape)

    def combine(e1, e2):
        a1r, a1i, b1r, b1i = e1
        a2r, a2i, b2r, b2i = e2
        return (a2r * a1r - a2i * a1i,
                a2r * a1i + a2i * a1r,
                a2r * b1r - a2i * b1i + b2r,
                a2r * b1i + a2i * b1r + b2i)

    _, _, s_re, s_im = lax.associative_scan(combine, (a_re, a_im, b_re, b_im), reverse=reverse, axis=1)
    return s_re, s_im


def s5_branch(u, lp, ctx_state):
    f32 = jnp.float32
    B_, L, _ = u.shape
    uf = u.astype(f32)
    ug = uf.reshape(B_, L, S5_G, S5_CH)
    bu_re = jnp.einsum('blgc,gpc->blgp', ug, lp['s5_b_re'].astype(f32))
    bu_im = jnp.einsum('blgc,gpc->blgp', ug, lp['s5_b_im'].astype(f32))
    tot_re, tot_im, finals = None, None, []
    for d in range(2):
        ar, ai, abr, abi, cr, ci = s5_discretise(lp['s5_lam_re'][d], lp['s5_lam_im'][d], lp['s5_log_dt'][d])
        b_re = cr * bu_re - ci * bu_im
        b_im = cr * bu_im + ci * bu_re
        s_re, s_im = s5_scan(abr, abi, b_re, b_im, reverse=(d == 1))
        if ctx_state is None:
            edge = -1 if d == 0 else 0
            finals.append(jnp.stack([s_re[:, edge], s_im[:, edge]], axis=-1))
        else:
            s0 = ctx_state[d].astype(f32)
            kk = (jnp.arange(1, L + 1) if d == 0 else jnp.arange(L, 0, -1)).astype(f32)[:, None, None]
            mag = jnp.exp(kk * ar)
            p_re, p_im = mag * jnp.cos(kk * ai), mag * jnp.sin(kk * ai)
            s0r, s0i = s0[..., 0][:, None], s0[..., 1][:, None]
            s_re = s_re + p_re * s0r - p_im * s0i
            s_im = s_im + p_re * s0i + p_im * s0r
        tot_re = s_re if tot_re is None else tot_re + s_re
        tot_im = s_im if tot_im is None else tot_im + s_im
    y = (jnp.einsum('blgp,gcp->blgc', tot_re, lp['s5_c_re'].astype(f32))
         - jnp.einsum('blgp,gcp->blgc', tot_im, lp['s5_c_im'].astype(f32)))
    y = y.reshape(B_, L, D_MIX) + lp['s5_d'].astype(f32) * uf
    y = jax.nn.gelu(y)
    y = y * jax.nn.sigmoid(y @ lp['s5_glu_w'].astype(f32) + lp['s5_glu_b'].astype(f32))
    return y.astype(u.dtype), finals


def retention_scan(q, k, v, log_gamma, s0):
    f32 = jnp.float32
    B_, L, H, _ = q.shape
    DV = v.shape[-1]
    nc = L // RET_CHUNK

    def chunks(t):
        return t.reshape(B_, nc, RET_CHUNK, H, t.shape[-1]).transpose(1, 0, 3, 2, 4)

    idx = jnp.arange(RET_CHUNK, dtype=f32)
    lg = log_gamma[:, None]
    diff = idx[:, None] - idx[None, :]
    decay_in = jnp.where(diff >= 0, jnp.exp(lg[:, :, None] * jnp.maximum(diff, 0.0)), 0.0)
    decay_q = jnp.exp(lg * (idx + 1.0))[..., None]
    decay_k = jnp.exp(lg * (RET_CHUNK - 1.0 - idx))[..., None]
    decay_s = jnp.exp(lg * RET_CHUNK)[..., None]

    def step(S, blk):
        qc, kc, vc = blk
        att = jnp.einsum('bhid,bhjd->bhij', qc, kc) * decay_in
        o = jnp.einsum('bhij,bhje->bhie', att, vc) + jnp.einsum('bhid,bhde->bhie', qc * decay_q, S)
        S = decay_s * S + jnp.einsum('bhjd,bhje->bhde', kc * decay_k, vc)
        return S, o

    S, o = lax.scan(step, s0, (chunks(q), chunks(k), chunks(v)))
    o = o.transpose(1, 0, 3, 2, 4).reshape(B_, L, H, DV)
    return o, S


def retention_branch(q, k, v, g, lp, ctx_state):
    f32 = jnp.float32
    B_, L, _ = q.shape
    qf = q.astype(f32).reshape(B_, L, RET_H, RET_DK)
    kf = k.astype(f32).reshape(B_, L, RET_H, RET_DK) * (RET_DK ** -0.5)
    vf = v.astype(f32).reshape(B_, L, RET_H, RET_DV)
    o_tot, finals = None, []
    for d in range(2):
        log_gamma = -jnp.exp(lp['ret_decay'][d].astype(f32))
        if ctx_state is None:
            s0 = jnp.zeros((B_, RET_H, RET_DK, RET_DV), f32)
        else:
            s0 = ctx_state[d].astype(f32)
        if d == 0:
            o, S = retention_scan(qf, kf, vf, log_gamma, s0)
        else:
            o, S = retention_scan(qf[:, ::-1], kf[:, ::-1], vf[:, ::-1], log_gamma, s0)
            o = o[:, ::-1]
        if ctx_state is None:
            finals.append(S)
        o_tot = o if o_tot is None else o_tot + o
    mu = jnp.mean(o_tot, axis=-1, keepdims=True)
    var = jnp.mean(jnp.square(o_tot - mu), axis=-1, keepdims=True)
    on = ((o_tot - mu) * lax.rsqrt(var + NORM_EPS)).reshape(B_, L, D_MIX)
    y = jax.nn.silu(g.astype(f32)) * (on * lp['ret_norm_g'].astype(f32))
    return y.astype(g.dtype), finals


def axial_rope_tables(n_tokens):
    f32 = jnp.float32
    rows = n_tokens // GRID_W
    row = jnp.broadcast_to(jnp.arange(rows, dtype=f32)[:, None], (rows, GRID_W)).reshape(-1)
    col = jnp.broadcast_to(jnp.arange(GRID_W, dtype=f32)[None, :], (rows, GRID_W)).reshape(-1)
    nf = MLA_DR // 4
    inv = ROPE_BASE ** (-jnp.arange(nf, dtype=f32) / nf)
    ang = jnp.concatenate([row[:, None] * inv, col[:, None] * inv], axis=-1)
    return jnp.cos(ang), jnp.sin(ang)


def apply_rope(x, cos, sin):
    xf = x.astype(jnp.float32)
    xp = xf.reshape(xf.shape[:-1] + (MLA_DR // 2, 2))
    x1, x2 = xp[..., 0], xp[..., 1]
    out = jnp.stack([x1 * cos - x2 * sin, x1 * sin + x2 * cos], axis=-1)
    return out.reshape(x.shape).astype(x.dtype)


def block_attention(q_nope, q_rope, k_nope, k_rope, v):
    B_, L, H, _ = q_nope.shape
    nb = L // Q_BLOCK
    scale = (MLA_DN + MLA_DR) ** -0.5

    def blk(qs):
        qn, qr = qs
        s = jnp.einsum('bqhd,bkhd->bhqk', qn, k_nope) + jnp.einsum('bqhd,bkd->bhqk', qr, k_rope)
        p = jax.nn.softmax(s.astype(jnp.float32) * scale, axis=-1).astype(v.dtype)
        return jnp.einsum('bhqk,bkhd->bqhd', p, v)

    qn_b = q_nope.reshape(B_, nb, Q_BLOCK, H, MLA_DN).transpose(1, 0, 2, 3, 4)
    qr_b = q_rope.reshape(B_, nb, Q_BLOCK, H, MLA_DR).transpose(1, 0, 2, 3, 4)
    o = lax.map(blk, (qn_b, qr_b))
    return o.transpose(1, 0, 2, 3, 4).reshape(B_, L, H, MLA_DV)


def mla_branch(c_q, c_kv, k_rope, lp, ctx_cache):
    B_, L, _ = c_q.shape
    q = (rmsnorm(c_q, lp['mla_q_norm']) @ lp['mla_w_uq']).reshape(B_, L, MLA_H, MLA_DN + MLA_DR)
    q_nope, q_rope = q[..., :MLA_DN], q[..., MLA_DN:]
    ckv = rmsnorm(c_kv, lp['mla_kv_norm'])
    if ctx_cache is None:
        key_ckv, key_rope = ckv, k_rope
        new_cache = (ckv, k_rope)
    else:
        cos, sin = axial_rope_tables(L)
        q_rope = apply_rope(q_rope, cos[:, None], sin[:, None])
        key_ckv = jnp.concatenate([ckv, ctx_cache[0].astype(ckv.dtype)], axis=1)
        key_rope = jnp.concatenate([apply_rope(k_rope, cos, sin), ctx_cache[1].astype(k_rope.dtype)], axis=1)
        new_cache = None
    S_ = key_ckv.shape[1]
    k_nope = (key_ckv @ lp['mla_w_uk']).reshape(B_, S_, MLA_H, MLA_DN)
    v = (key_ckv @ lp['mla_w_uv']).reshape(B_, S_, MLA_H, MLA_DV)
    o = block_attention(q_nope, q_rope, k_nope, key_rope, v)
    return o.reshape(B_, L, D_MIX), new_cache


def token_mixer(h, lp, ctx):
    z = h @ lp['w_in']
    splits = np.cumsum(IN_WIDTHS)[:-1].tolist()
    u, q, k, v, g, c_q, c_kv, k_rope, gate_a, gate_b, gate_c = jnp.split(z, splits, axis=-1)
    ya, s5_st = s5_branch(u, lp, None if ctx is None else ctx['s5'])
    yb, ret_st = retention_branch(q, k, v, g, lp, None if ctx is None else ctx['ret'])
    yc, mla_st = mla_branch(c_q, c_kv, k_rope, lp, None if ctx is None else ctx['mla'])
    wb = lp['w_branch']
    merged = (jax.nn.sigmoid(gate_a) * (ya @ wb[0])
              + jax.nn.sigmoid(gate_b) * (yb @ wb[1])
              + jax.nn.sigmoid(gate_c) * (yc @ wb[2]))
    return merged @ lp['w_out'], (s5_st, ret_st, mla_st)


def conv_ffn(h, lp):
    u = h @ lp['ffn_w_up']
    w = lp['ffn_conv_w']
    up = jnp.pad(u, ((0, 0), (1, 1), (0, 0)))
    u = up[:, :-2] * w[0] + up[:, 1:-1] * w[1] + up[:, 2:] * w[2] + lp['ffn_conv_b']
    val, gate = jnp.split(u, 2, axis=-1)
    return (jax.nn.silu(gate) * val) @ lp['ffn_w_down']


def trunk_layer(x, mods, lp, ctx):
    sh1, sc1, g1, sh2, sc2, g2 = mods
    ng = lp['norm_g']
    h = rmsnorm(x, ng[0]) * (1.0 + sc1) + sh1
    mo, st = token_mixer(h, lp, ctx)
    x = x + g1 * rmsnorm(mo, ng[1])
    h = rmsnorm(x, ng[2]) * (1.0 + sc2) + sh2
    x = x + g2 * rmsnorm(conv_ffn(h, lp), ng[3])
    return x, st


def setup_inputs(seed: int = 0) -> dict:
    f32 = jnp.float32
    key = jax.random.key(seed)
    keys = jax.random.split(key, 48)

    def nrm(i, shape, scale):
        return scale * jax.random.normal(keys[i], shape, f32)

    x_prompt = nrm(0, (BATCH, SEQ, D_MODEL), 1.0)
    x_sample = nrm(1, (DEC_BATCH, DEC_SEQ, D_MODEL), 1.0)
    cache_mla_ckv = nrm(2, (DEC_BATCH, DEPTH, PAST_LEN, MLA_KV_LORA), 1.0)
    cache_mla_krope = nrm(3, (DEC_BATCH, DEPTH, PAST_LEN, MLA_DR), 1.0)
    state_ret_fwd = nrm(4, (DEC_BATCH, DEPTH, RET_H, RET_DK, RET_DV), 0.5)
    state_ret_bwd = nrm(5, (DEC_BATCH, DEPTH, RET_H, RET_DK, RET_DV), 0.5)
    state_s5_fwd = nrm(6, (DEC_BATCH, DEPTH, S5_G, S5_P, 2), 0.3)
    state_s5_bwd = nrm(7, (DEC_BATCH, DEPTH, S5_G, S5_P, 2), 0.3)
    c = nrm(8, (DEC_BATCH, D_MODEL), 1.0)
    c_ctx = nrm(9, (D_MODEL,), 1.0)
    ada_w = nrm(10, (DEPTH, D_MODEL, N_MOD * D_MODEL), 0.5 * D_MODEL ** -0.5)
    ada_b = nrm(11, (DEPTH, N_MOD * D_MODEL), 0.01)
    norm_g = 1.0 + nrm(12, (DEPTH, 4, D_MODEL), 0.01)
    w_in = nrm(13, (DEPTH, D_MODEL, sum(IN_WIDTHS)), D_MODEL ** -0.5)
    s5_lam_re = -0.5 + nrm(14, (DEPTH, 2, S5_G, S5_P), 0.01)
    s5_lam_im = math.pi * jnp.arange(S5_P, dtype=f32) + nrm(15, (DEPTH, 2, S5_G, S5_P), 0.01)
    s5_log_dt = jax.random.uniform(keys[16], (DEPTH, 2, S5_G), f32, math.log(0.001), math.log(0.1))
    s5_b_re = nrm(17, (DEPTH, S5_G, S5_P, S5_CH), (2 * S5_CH) ** -0.5)
    s5_b_im = nrm(18, (DEPTH, S5_G, S5_P, S5_CH), (2 * S5_CH) ** -0.5)
    s5_c_re = nrm(19, (DEPTH, S5_G, S5_CH, S5_P), (2 * S5_P) ** -0.5)
    s5_c_im = nrm(20, (DEPTH, S5_G, S5_CH, S5_P), (2 * S5_P) ** -0.5)
    s5_d = nrm(21, (DEPTH, D_MIX), 1.0)
    s5_glu_w = nrm(22, (DEPTH, D_MIX, D_MIX), D_MIX ** -0.5)
    s5_glu_b = nrm(23, (DEPTH, D_MIX), 0.01)
    ret_base = jnp.log(-jnp.log1p(-(2.0 ** (-5.0 - jnp.arange(RET_H, dtype=f32)))))
    ret_decay = ret_base + nrm(24, (DEPTH, 2, RET_H), 0.01)
    ret_norm_g = 1.0 + nrm(25, (DEPTH, D_MIX), 0.01)
    mla_q_norm = 1.0 + nrm(26, (DEPTH, MLA_Q_LORA), 0.01)
    mla_kv_norm = 1.0 + nrm(27, (DEPTH, MLA_KV_LORA), 0.01)
    mla_w_uq = nrm(28, (DEPTH, MLA_Q_LORA, MLA_H * (MLA_DN + MLA_DR)), MLA_Q_LORA ** -0.5)
    mla_w_uk = nrm(29, (DEPTH, MLA_KV_LORA, MLA_H * MLA_DN), MLA_KV_LORA ** -0.5)
    mla_w_uv = nrm(30, (DEPTH, MLA_KV_LORA, MLA_H * MLA_DV), MLA_KV_LORA ** -0.5)
    w_branch = nrm(31, (DEPTH, 3, D_MIX, D_MODEL), D_MIX ** -0.5)
    w_out = nrm(32, (DEPTH, D_MODEL, D_MODEL), D_MODEL ** -0.5)
    ffn_w_up = nrm(33, (DEPTH, D_MODEL, 2 * D_FF), D_MODEL ** -0.5)
    ffn_conv_w = nrm(34, (DEPTH, 3, 2 * D_FF), 3 ** -0.5)
    ffn_conv_b = nrm(35, (DEPTH, 2 * D_FF), 0.01)
    ffn_w_down = nrm(36, (DEPTH, D_FF, D_MODEL), D_FF ** -0.5)
    return {'x_prompt': x_prompt, 'x_sample': x_sample,
            'cache_mla_ckv': cache_mla_ckv, 'cache_mla_krope': cache_mla_krope,
            'state_ret_fwd': state_ret_fwd, 'state_ret_bwd': state_ret_bwd,
            'state_s5_fwd': state_s5_fwd, 'state_s5_bwd': state_s5_bwd,
            'c': c, 'c_ctx': c_ctx, 'ada_w': ada_w, 'ada_b': ada_b, 'norm_g': norm_g, 'w_in': w_in,
            's5_lam_re': s5_lam_re, 's5_lam_im': s5_lam_im, 's5_log_dt': s5_log_dt,
            's5_b_re': s5_b_re, 's5_b_im': s5_b_im, 's5_c_re': s5_c_re, 's5_c_im': s5_c_im,
            's5_d': s5_d, 's5_glu_w': s5_glu_w, 's5_glu_b': s5_glu_b,
            'ret_decay': ret_decay, 'ret_norm_g': ret_norm_g,
            'mla_q_norm': mla_q_norm, 'mla_kv_norm': mla_kv_norm,
            'mla_w_uq': mla_w_uq, 'mla_w_uk': mla_w_uk, 'mla_w_uv': mla_w_uv,
            'w_branch': w_branch, 'w_out': w_out,
            'ffn_w_up': ffn_w_up, 'ffn_conv_w': ffn_conv_w, 'ffn_conv_b': ffn_conv_b, 'ffn_w_down': ffn_w_down}


def reference(x_prompt, x_sample, cache_mla_ckv, cache_mla_krope, state_ret_fwd, state_ret_bwd,
              state_s5_fwd, state_s5_bwd, c, c_ctx, ada_w, ada_b, norm_g, w_in,
              s5_lam_re, s5_lam_im, s5_log_dt, s5_b_re, s5_b_im, s5_c_re, s5_c_im,
              s5_d, s5_glu_w, s5_glu_b, ret_decay, ret_norm_g,
              mla_q_norm, mla_kv_norm, mla_w_uq, mla_w_uk, mla_w_uv,
              w_branch, w_out, ffn_w_up, ffn_conv_w, ffn_conv_b, ffn_w_down):
    xp, xs = x_prompt, x_sample
    ckv_out, krope_out, retf_out, retb_out, s5f_out, s5b_out = [], [], [], [], [], []
    for l in range(DEPTH):
        lp = {'norm_g': norm_g[l], 'w_in': w_in[l],
              's5_lam_re': s5_lam_re[l], 's5_lam_im': s5_lam_im[l], 's5_log_dt': s5_log_dt[l],
              's5_b_re': s5_b_re[l], 's5_b_im': s5_b_im[l], 's5_c_re': s5_c_re[l], 's5_c_im': s5_c_im[l],
              's5_d': s5_d[l], 's5_glu_w': s5_glu_w[l], 's5_glu_b': s5_glu_b[l],
              'ret_decay': ret_decay[l], 'ret_norm_g': ret_norm_g[l],
              'mla_q_norm': mla_q_norm[l], 'mla_kv_norm': mla_kv_norm[l],
              'mla_w_uq': mla_w_uq[l], 'mla_w_uk': mla_w_uk[l], 'mla_w_uv': mla_w_uv[l],
              'w_branch': w_branch[l], 'w_out': w_out[l],
              'ffn_w_up': ffn_w_up[l], 'ffn_conv_w': ffn_conv_w[l], 'ffn_conv_b': ffn_conv_b[l],
              'ffn_w_down': ffn_w_down[l]}
        mods_ctx = adaln(c_ctx, ada_w[l], ada_b[l])
        xp, (s5_st, ret_st, mla_st) = trunk_layer(xp, mods_ctx, lp, None)
        s5f_out.append(s5_st[0])
        s5b_out.append(s5_st[1])
        retf_out.append(ret_st[0])
        retb_out.append(ret_st[1])
        ckv_out.append(mla_st[0])
        krope_out.append(mla_st[1])
        mods_lat = [m[:, None, :] for m in adaln(c, ada_w[l], ada_b[l])]
        ctx = {'s5': (state_s5_fwd[:, l], state_s5_bwd[:, l]),
               'ret': (state_ret_fwd[:, l], state_ret_bwd[:, l]),
               'mla': (cache_mla_ckv[:, l], cache_mla_krope[:, l])}
        xs, _ = trunk_layer(xs, mods_lat, lp, ctx)
    new_mla_ckv = jnp.stack(ckv_out, axis=1)
    new_mla_krope = jnp.stack(krope_out, axis=1)
    new_ret_fwd = jnp.stack(retf_out, axis=1)
    new_ret_bwd = jnp.stack(retb_out, axis=1)
    new_s5_fwd = jnp.stack(s5f_out, axis=1)
    new_s5_bwd = jnp.stack(s5b_out, axis=1)
    return (xp, xs, new_mla_ckv, new_mla_krope, new_ret_fwd, new_ret_bwd, new_s5_fwd, new_s5_bwd)
```

```python
import math
import numpy as np
import concourse.bass as bass
import concourse.mybir as mybir
from concourse.bass_utils import run_bass_kernel_spmd

F32 = mybir.dt.float32
BF16 = mybir.dt.bfloat16
AF = mybir.ActivationFunctionType
ALU = mybir.AluOpType

RING = 8
DEBUG = False
EPS = 1e-6
D = 2048
DM = 1024
NL = 2
LP = 256
NPS = 4
LS = 4096
PAST = 256
DFF = 5632
ZROWS = 11584
R_U, R_Q, R_K, R_G, R_CQ, R_CKV, R_KR, R_GATE = 0, 1024, 2048, 3072, 4096, 4864, 5376, 5440
ATT_SCALE = (128 + 64) ** -0.5
RET_SCALE = 128 ** -0.5
MAGIC = 12582912.0
TWO_PI = 2.0 * math.pi


class Dep:
    __slots__ = ("w", "r")

    def __init__(self):
        self.w = {}
        self.r = {}


class T:
    __slots__ = ("t", "d")

    def __init__(self, t, d=None):
        self.t = t
        self.d = d if d is not None else Dep()


class Ctx:
    def __init__(self, nc):
        self.nc = nc
        self.enames = ["pe", "act", "dve", "pool", "sp"]
        self.ops = {e: [] for e in self.enames}
        self.cnt = {e: 0 for e in self.enames}
        self.sem = {e: nc.alloc_semaphore("s_" + e) for e in self.enames}
        self.waited = {e: {} for e in self.enames}
        self.ring = {q: [nc.alloc_semaphore("r_%s%d" % (q, i)) for i in range(RING)] for q in ("sp", "pool", "act")}
        self.dman = {q: 0 for q in self.ring}
        self.nalloc = 0
        self.ninstr = 0

    def sb(self, shape, dt, name=None):
        self.nalloc += 1
        return T(self.nc.alloc_sbuf_tensor(name or ("sb%d" % self.nalloc), list(shape), dt))

    def semof(self, key):
        if key[0] == "E":
            return self.sem[key[1]]
        return self.ring[key[1]][key[2]]

    def _collect(self, e, reads, writes, extra=None):
        need = {}
        toks = []
        for d in reads:
            toks += list(d.w.items())
        for d in writes:
            toks += list(d.w.items())
            toks += list(d.r.items())
        if extra:
            toks += extra
        for key, (val, src) in toks:
            if src == "pe" and e == "pe" and key[0] == "E":
                continue
            if self.waited[e].get(key, 0) >= val:
                continue
            if need.get(key, 0) < val:
                need[key] = val
        for key, val in need.items():
            self.waited[e][key] = val
        return [(self.semof(k), v) for k, v in need.items()]

    def op(self, e, fn, reads=(), writes=()):
        reads = [x.d if isinstance(x, T) else x for x in reads]
        writes = [x.d if isinstance(x, T) else x for x in writes]
        wl = self._collect(e, reads, writes)
        self.cnt[e] += 1
        n = self.cnt[e]
        key = ("E", e)
        sem = self.sem[e]

        def emit(h):
            for sm, v in wl:
                h.wait_ge(sm, v)
            fn(h).then_inc(sem, 1)
        self.ops[e].append(emit)
        self.ninstr += 1 + len(wl)
        for d in reads:
            d.r[key] = (n, e)
        for d in writes:
            d.w[key] = (n, e)

    def dma(self, q, out, in_, reads=(), writes=(), **kw):
        reads = [x.d if isinstance(x, T) else x for x in reads]
        writes = [x.d if isinstance(x, T) else x for x in writes]
        i = self.dman[q]
        self.dman[q] += 1
        ri = i % RING
        val = 16 * (i // RING + 1)
        prev = 16 * (i // RING)
        key = ("D", q, ri)
        extra = [(key, (prev, None))] if prev > 0 else None
        wl = self._collect(q, reads, writes, extra)
        sem = self.ring[q][ri]

        def emit(h):
            for sm, v in wl:
                h.wait_ge(sm, v)
            h.dma_start(out=out, in_=in_, **kw).then_inc(sem, 16)
        self.ops[q].append(emit)
        self.ninstr += 1 + len(wl)
        for d in reads:
            d.r[key] = (val, None)
        for d in writes:
            d.w[key] = (val, None)

    def finish(self):
        finals = []
        for q in self.ring:
            n = self.dman[q]
            for ri in range(RING):
                cntr = (n - ri + RING - 1) // RING if n > ri else 0
                if cntr > 0:
                    finals.append((self.ring[q][ri], 16 * cntr))
        efinal = [(self.sem[e], self.cnt[e]) for e in self.enames if self.cnt[e] > 0]
        ops = self.ops
        with self.nc.Block() as block:
            @block.tensor
            def _(h):
                for f in ops["pe"]:
                    f(h)

            @block.scalar
            def _(h):
                for f in ops["act"]:
                    f(h)

            @block.vector
            def _(h):
                for f in ops["dve"]:
                    f(h)

            @block.gpsimd
            def _(h):
                for f in ops["pool"]:
                    f(h)

            @block.sync
            def _(h):
                for f in ops["sp"]:
                    f(h)
                for sm, v in efinal:
                    h.wait_ge(sm, v)
                for sm, v in finals:
                    h.wait_ge(sm, v)


class Pool:
    def __init__(self, tiles):
        self.tiles = tiles
        self.i = 0

    def get(self):
        t = self.tiles[self.i % len(self.tiles)]
        self.i += 1
        return t


def pat(a):
    return a.tensor, a.offset, [list(x) for x in a.ap]


def bc_mid(a, n):
    t, o, p = pat(a)
    return bass.AP(t, o, [p[0], [0, n]] + p[1:])


def bc_last(a, n):
    t, o, p = pat(a)
    return bass.AP(t, o, p + [[0, n]])


def rev(a):
    t, o, p = pat(a)
    st, n = p[1]
    return bass.AP(t, o + st * (n - 1), [p[0], [-st, n]])


def build(debug_stop=None):
    nc = bass.Bass("TRN2", target_bir_lowering=False)
    c = Ctx(nc)

    def din(name, shape, dt=F32):
        return nc.dram_tensor(name, list(shape), dt, kind="ExternalInput").ap()

    def dout(name, shape, dt=F32):
        return nc.dram_tensor(name, list(shape), dt, kind="ExternalOutput").ap()

    def dscr(name, shape, dt):
        return T(nc.dram_tensor(name, list(shape), dt, kind="Internal").ap())

    xTp = din("xTp", [D, 1024])
    xTs = din("xTs", [D, LS])
    condT = din("condT", [128, 16, 2])
    ada_w = din("ada_w", [NL, D, 6 * D])
    ada_bT = din("ada_bT", [NL, 128, 96])
    norm_gT = din("norm_gT", [NL, 128, 4, 16])
    w_in = din("w_in", [NL, D, 12608])
    w_in_kr = din("w_in_kr", [NL, D, 64])
    s5_lam = din("s5_lam", [NL, 2, 3, 128, 32])
    s5_nb = din("s5_nb", [NL, 2, 128, 32, 32])
    s5_cb = din("s5_cb", [NL, 2, 128, 32, 32])
    s5_dT = din("s5_dT", [NL, 32, 32])
    s5_s0 = din("s5_s0", [NL, 2, 128, 32, 2])
    glu_w = din("glu_w", [NL, DM, DM])
    glu_bT = din("glu_bT", [NL, 128, 8])
    ret_dec = din("ret_dec", [NL, 128, 16])
    ret_ngT = din("ret_ngT", [NL, 128, 8])
    ret_s0 = din("ret_s0", [NL, 2, 8, 128, 128])
    ret_tab = din("ret_tab", [128, 6, 128])
    ret_col = din("ret_col", [128, 3])
    q_normT = din("q_normT", [NL, 128, 6])
    kv_normT = din("kv_normT", [NL, 128, 4])
    kv_norm_rep = din("kv_norm_rep", [NL, 128, 512])
    w_uq = din("w_uq", [NL, 768, 1536])
    w_uk = din("w_uk", [NL, 512, 1024])
    w_uv = din("w_uv", [NL, 512, 1024])
    rope_cs = din("rope_cs", [2, 32, LS])
    cache_ckvT = din("cache_ckvT", [NL, 512, PAST])
    cache_kr = din("cache_kr", [NL, 2, 32, PAST])
    w_branch = din("w_branch", [NL, 3, DM, D])
    w_out = din("w_out", [NL, D, D])
    w_up = din("w_up", [NL, D, 2 * DFF])
    conv_wT = din("conv_wT", [NL, 128, 88, 3])
    conv_bT = din("conv_bT", [NL, 128, 88])
    w_down = din("w_down", [NL, DFF, D])
    yTp = dout("yTp", [D, 1024])
    yTs = dout("yTs", [D, LS])
    ckv_o = dout("ckv_o", [NPS, NL, LP, 512])
    kr_o = dout("kr_o", [NPS, NL, LP, 64])
    ret_o = dout("ret_o", [2, NPS, NL, 8, 128, 128])
    s5_o = dout("s5_o", [2, NPS, NL, 2, 128, 32])
    if DEBUG:
        dbg_y = dout("dbg_y", [3 * DM, 1024], BF16)
        dbg_x1 = dout("dbg_x1", [D, 1024])
        dbg_x2 = dout("dbg_x2", [D, 1024])
    zT = dscr("zT", [ZROWS, LS], BF16)
    ktok = dscr("ktok", [LS, DM], BF16)
    vtok = dscr("vtok", [LS, DM], BF16)
    ygT = dscr("ygT", [DM, LS], BF16)
    yT = dscr("yT", [3 * DM, LS], BF16)
    x1b = dscr("x1b", [D, LS], F32)
    x2b = dscr("x2b", [D, LS], F32)
    fbuf = dscr("fbuf", [D, 1024], F32)
    qscr = dscr("qscr", [8, 192, LS], BF16)
    d_xTp, d_xTs = Dep(), Dep()
    d_out = Dep()

    wcA = [dscr("wcA%d" % l, [32, 128, 16 * 512], BF16) for l in range(NL)]
    wcC = [dscr("wcC%d" % l, [16, 128, 16 * 512], BF16) for l in range(NL)]
    wcU = [dscr("wcU%d" % l, [44, 128, 16 * 256], BF16) for l in range(NL)]
    wcD = [dscr("wcD%d" % l, [16, 128, 44 * 128], BF16) for l in range(NL)]

    def load_w(w, nk, ncols, src_ap, cache, bi, first):
        cview = cache.t[bi, :, 0:nk * ncols].rearrange("p (k n) -> p k n", k=nk)
        if first:
            c.dma("pool", w.t[:, 0:nk, 0:ncols], src_ap.rearrange("(k p) n -> p k n", p=128), writes=[w])
            c.dma("sp", cview, w.t[:, 0:nk, 0:ncols], reads=[w], writes=[cache])
        else:
            c.dma("pool", w.t[:, 0:nk, 0:ncols], cview, reads=[cache], writes=[w])

    def dump(name, ap, shape, dt, reads):
        if DEBUG:
            o_ = dout("dbg_" + name, shape, dt)
            c.dma("sp", o_, ap, reads=reads, writes=[d_out])

    PS = [T(nc.alloc_psum_tensor("psb%d" % i, [128, 512], F32)) for i in range(8)]
    ps_mm = Pool(PS[0:4])
    ps_aux = Pool(PS[6:8])
    ACC0, ACC1 = PS[4], PS[5]

    ident = c.sb([128, 128], F32, "ident")
    c.op("pool", lambda h: h.memset(ident.t[:, :], 0.0), writes=[ident])
    c.op("pool", lambda h: h.affine_select(ident.t[:, :], ident.t[:, :], pattern=[[-1, 128]], compare_op=ALU.not_equal,
                                           fill=1.0, base=0, channel_multiplier=1), reads=[ident], writes=[ident])
    ones_bf = c.sb([128, 128], BF16, "ones_bf")
    c.op("pool", lambda h: h.memset(ones_bf.t[:, :], 1.0), writes=[ones_bf])
    ones128 = c.sb([128, 128], BF16, "ones128")
    c.op("pool", lambda h: h.memset(ones128.t[:, :], 1.0 / 128.0), writes=[ones128])
    eps_t = c.sb([128, 1], F32, "eps_t")
    c.op("pool", lambda h: h.memset(eps_t.t[:, :], EPS), writes=[eps_t])

    def barrier():
        toks = [(("E", e), (c.cnt[e], e)) for e in c.enames if c.cnt[e] > 0]
        for q in c.ring:
            n = c.dman[q]
            for ri in range(RING):
                cntr = (n - ri + RING - 1) // RING if n > ri else 0
                if cntr > 0:
                    toks.append((("D", q, ri), (16 * cntr, None)))
        d = Dep()
        for k, v in toks:
            d.w[k] = v
        for e in c.enames:
            if e in ("sp",):
                wl = c._collect(e, [d], [])

                def emit(h, wl=wl):
                    for sm, v in wl:
                        h.wait_ge(sm, v)
                c.ops[e].append(emit)
            elif e == "pe":
                wl = c._collect(e, [d], [])

                def emit(h, wl=wl):
                    for sm, v in wl:
                        h.wait_ge(sm, v)
                c.ops[e].append(emit)
            else:
                wl = c._collect(e, [d], [])

                def emit(h, wl=wl):
                    for sm, v in wl:
                        h.wait_ge(sm, v)
                c.ops[e].append(emit)

    def mm(ps, ps_ap, lhsT, rhs, start, stop, reads):
        c.op("pe", lambda h: h.matmul(ps_ap, lhsT, rhs, start=start, stop=stop), reads=reads, writes=[ps])

    def act(out_t, out_ap, in_ap, func, reads, bias=None, scale=None):
        kw = {}
        if bias is not None:
            kw["bias"] = bias
        if scale is not None:
            kw["scale"] = scale
        c.op("act", lambda h: h.activation(out_ap, in_ap, func, **kw), reads=reads, writes=[out_t])

    def tt(out_t, out_ap, a, b, op, reads, eng="dve"):
        c.op(eng, lambda h: h.tensor_tensor(out_ap, a, b, op), reads=reads, writes=[out_t])

    def ts(out_t, out_ap, a, s1, s2, op0, op1, reads, eng="dve"):
        if op1 is None:
            c.op(eng, lambda h: h.tensor_scalar(out_ap, a, s1, None, op0), reads=reads, writes=[out_t])
        else:
            c.op(eng, lambda h: h.tensor_scalar(out_ap, a, s1, s2, op0, op1), reads=reads, writes=[out_t])

    def stt(out_t, out_ap, a, s, b, op0, op1, reads):
        c.op("dve", lambda h: h.scalar_tensor_tensor(out_ap, a, s, b, op0, op1), reads=reads, writes=[out_t])

    def recip(out_t, out_ap, a, reads):
        c.op("dve", lambda h: h.reciprocal(out_ap, a), reads=reads, writes=[out_t])

    def rstd_from_ssq(ps, ps_ap, out_t, out_ap, inv_n):
        act(out_t, out_ap, ps_ap, AF.Sqrt, [ps, eps_t], bias=eps_t.t[:, 0:1], scale=inv_n)
        recip(out_t, out_ap, out_ap, [out_t])

    ARENA_BYTES = 146 * 1024
    arena = nc.alloc_sbuf_tensor("arena", [128, ARENA_BYTES], mybir.dt.uint8)

    class Arena:
        def __init__(self):
            self.off = 0

        def reset(self):
            barrier()
            self.off = 0

        def tile(self, shape, dt):
            nb = int(np.prod(shape[1:])) * mybir.dt.size(dt)
            nb_al = (nb + 31) // 32 * 32
            assert self.off + nb_al <= ARENA_BYTES, ("arena overflow", self.off, nb_al)
            P = shape[0]
            v = arena[0:P, self.off:self.off + nb].bitcast(dt)
            self.off += nb_al
            if len(shape) == 3:
                v = v.rearrange("p (a b) -> p a b", a=shape[1])
            elif len(shape) == 4:
                v = v.rearrange("p (a b c) -> p a b c", a=shape[1], b=shape[2])
            return T(v)

    AR = Arena()
    stg_bf = Pool([c.sb([128, 1024], BF16, "stgbf%d" % i) for i in range(3)])
    stg_f = Pool([c.sb([128, 1024], F32, "stgf%d" % i) for i in range(2)])
    sq_p = Pool([c.sb([128, 512], BF16, "sqp%d" % i) for i in range(3)])
    rstd_p = Pool([c.sb([128, 1024], F32, "rstd%d" % i) for i in range(2)])

    cond_f = c.sb([128, 16, 2], F32, "cond_f")
    cond_b = c.sb([128, 16, 2], BF16, "cond_b")
    c.dma("sp", cond_f.t[:, :, :], condT, writes=[cond_f])
    act(cond_b, cond_b.t[:, :, :], cond_f.t[:, :, :], AF.Silu, [cond_f])
    mods = [c.sb([128, 96, 2], F32, "mods%d" % l) for l in range(NL)]
    adab = [c.sb([128, 96], F32, "adab%d" % l) for l in range(NL)]
    ngs = [c.sb([128, 4, 16], F32, "ng%d" % l) for l in range(NL)]
    PR = [[c.sb([128, 6, 16], F32, "pr%d_%d" % (l, j)) for j in range(2)] for l in range(NL)]
    AR.reset()
    wA = Pool([AR.tile([128, 16, 512], BF16) for i in range(2)])
    for l in range(NL):
        c.dma("sp", adab[l].t[:, :], ada_bT[l], writes=[adab[l]])
        c.dma("sp", ngs[l].t[:, :, :], norm_gT[l], writes=[ngs[l]])
        for blk in range(24):
            w = wA.get()
            c.dma("pool", w.t[:, :, :], ada_w[l, :, blk * 512:(blk + 1) * 512].rearrange("(k p) n -> p k n", p=128), writes=[w])
            ps = ps_mm.get()
            for mi in range(4):
                for kc in range(16):
                    mm(ps, ps.t[:, mi * 2:mi * 2 + 2], w.t[:, kc, mi * 128:(mi + 1) * 128], cond_b.t[:, kc, :], kc == 0, kc == 15, [w, cond_b])
            for mi in range(4):
                m = blk * 4 + mi
                act(mods[l], mods[l].t[:, m, :], ps.t[:, mi * 2:mi * 2 + 2], AF.Identity, [ps, adab[l]], bias=adab[l].t[:, m:m + 1])
        for j in range(2):
            P = PR[l][j]
            M = mods[l]
            stt(P, P.t[:, 0, :], M.t[:, 16:32, j], 1.0, ngs[l].t[:, 0, :], ALU.add, ALU.mult, [M, ngs[l]])
            c.op("dve", lambda h, P=P, M=M, j=j: h.tensor_copy(P.t[:, 1, :], M.t[:, 0:16, j]), reads=[M], writes=[P])
            tt(P, P.t[:, 2, :], M.t[:, 32:48, j], ngs[l].t[:, 1, :], ALU.mult, [M, ngs[l]])
            stt(P, P.t[:, 3, :], M.t[:, 64:80, j], 1.0, ngs[l].t[:, 2, :], ALU.add, ALU.mult, [M, ngs[l]])
            c.op("dve", lambda h, P=P, M=M, j=j: h.tensor_copy(P.t[:, 4, :], M.t[:, 48:64, j]), reads=[M], writes=[P])
            tt(P, P.t[:, 5, :], M.t[:, 80:96, j], ngs[l].t[:, 3, :], ALU.mult, [M, ngs[l]])

    s5_mag = [[c.sb([128, 32], F32, "s5mag%d_%d" % (l, d)) for d in range(2)] for l in range(NL)]
    s5_dc = [[c.sb([128, 13, 32], F32, "s5dc%d_%d" % (l, d)) for d in range(2)] for l in range(NL)]
    s5_ds = [[c.sb([128, 13, 32], F32, "s5ds%d_%d" % (l, d)) for d in range(2)] for l in range(NL)]
    AR.reset()
    wb_d = dscr("wb_d", [NL, 2, 2, 32, 32, 128], BF16)
    wc_d = dscr("wc_d", [NL, 2, 128, 32, 32], BF16)
    wb_stage = Pool([AR.tile([32, 32, 128], BF16) for _ in range(2)])
    wc_stage = Pool([AR.tile([128, 32, 32], BF16) for _ in range(2)])
    s5_dsk = [c.sb([32, 32], F32, "s5dsk%d" % l) for l in range(NL)]
    s5_init = [[c.sb([128, 32, 2], F32, "s5init%d_%d" % (l, d)) for d in range(2)] for l in range(NL)]
    tmpP = Pool([AR.tile([128, 32], F32) for i in range(64)])
    nbig = Pool([AR.tile([128, 32, 32], F32) for i in range(12)])

    def range_reduce_sin(out_t, x_t, shift):
        a = tmpP.get()
        ts(a, a.t[:, :], x_t.t[:, :], shift, None, ALU.add, None, [x_t])
        n = tmpP.get()
        ts(n, n.t[:, :], a.t[:, :], 1.0 / TWO_PI, MAGIC, ALU.mult, ALU.add, [a])
        ts(n, n.t[:, :], n.t[:, :], MAGIC, None, ALU.subtract, None, [n])
        stt(a, a.t[:, :], n.t[:, :], -TWO_PI, a.t[:, :], ALU.mult, ALU.add, [n, a])
        ts(a, a.t[:, :], a.t[:, :], math.pi, -math.pi, ALU.min, ALU.max, [a])
        act(out_t, out_t.t[:, :], a.t[:, :], AF.Sin, [a])

    for l in range(NL):
        c.dma("sp", s5_dsk[l].t[:, :], s5_dT[l], writes=[s5_dsk[l]])
        nbr, nbi = nbig.get(), nbig.get()
        c.dma("sp", nbr.t[:, :, :], s5_nb[l, 0], writes=[nbr])
        c.dma("sp", nbi.t[:, :, :], s5_nb[l, 1], writes=[nbi])
        cr_, ci_ = nbig.get(), nbig.get()
        c.dma("sp", cr_.t[:, :, :], s5_cb[l, 0], writes=[cr_])
        c.dma("sp", ci_.t[:, :, :], s5_cb[l, 1], writes=[ci_])
        for r, (srcc, scl) in enumerate(((cr_, 1.0), (ci_, -1.0))):
            wcs = wc_stage.get()
            act(wcs, wcs.t[:, :, :], srcc.t[:, :, :], AF.Copy, [srcc], scale=scl)
            c.dma("sp", wc_d.t[l, r], wcs.t[:, :, :], reads=[wcs], writes=[wc_d])
        for d in range(2):
            c.dma("sp", s5_init[l][d].t[:, :, :], s5_s0[l, d], writes=[s5_init[l][d]])
            lr, li, ldt = tmpP.get(), tmpP.get(), tmpP.get()
            c.dma("sp", lr.t[:, :], s5_lam[l, d, 0], writes=[lr])
            c.dma("sp", li.t[:, :], s5_lam[l, d, 1], writes=[li])
            c.dma("sp", ldt.t[:, :], s5_lam[l, d, 2], writes=[ldt])
            dt_ = tmpP.get()
            act(dt_, dt_.t[:, :], ldt.t[:, :], AF.Exp, [ldt])
            ar, ai = tmpP.get(), tmpP.get()
            tt(ar, ar.t[:, :], lr.t[:, :], dt_.t[:, :], ALU.mult, [lr, dt_])
            tt(ai, ai.t[:, :], li.t[:, :], dt_.t[:, :], ALU.mult, [li, dt_])
            mag = s5_mag[l][d]
            act(mag, mag.t[:, :], ar.t[:, :], AF.Exp, [ar])
            dc, ds = s5_dc[l][d], s5_ds[l][d]
            cs, sn = tmpP.get(), tmpP.get()
            range_reduce_sin(cs, ai, math.pi / 2)
            range_reduce_sin(sn, ai, 0.0)
            c.op("dve", lambda h, dc=dc, cs=cs: h.tensor_copy(dc.t[:, 0, :], cs.t[:, :]), reads=[cs], writes=[dc])
            c.op("dve", lambda h, ds=ds, sn=sn: h.tensor_copy(ds.t[:, 0, :], sn.t[:, :]), reads=[sn], writes=[ds])
            for k in range(12):
                t1, t2 = tmpP.get(), tmpP.get()
                tt(t1, t1.t[:, :], dc.t[:, k, :], dc.t[:, k, :], ALU.mult, [dc])
                tt(t2, t2.t[:, :], ds.t[:, k, :], ds.t[:, k, :], ALU.mult, [ds])
                tt(dc, dc.t[:, k + 1, :], t1.t[:, :], t2.t[:, :], ALU.subtract, [t1, t2])
                stt(ds, ds.t[:, k + 1, :], dc.t[:, k, :], 2.0, ds.t[:, k, :], ALU.mult, ALU.mult, [dc, ds])
            abr, abi = tmpP.get(), tmpP.get()
            tt(abr, abr.t[:, :], mag.t[:, :], cs.t[:, :], ALU.mult, [mag, cs])
            tt(abi, abi.t[:, :], mag.t[:, :], sn.t[:, :], ALU.mult, [mag, sn])
            ts(abr, abr.t[:, :], abr.t[:, :], -1.0, None, ALU.add, None, [abr])
            den, t1, t2 = tmpP.get(), tmpP.get(), tmpP.get()
            tt(den, den.t[:, :], lr.t[:, :], lr.t[:, :], ALU.mult, [lr])
            tt(t1, t1.t[:, :], li.t[:, :], li.t[:, :], ALU.mult, [li])
            tt(den, den.t[:, :], den.t[:, :], t1.t[:, :], ALU.add, [den, t1])
            recip(den, den.t[:, :], den.t[:, :], [den])
            cre, cim = tmpP.get(), tmpP.get()
            tt(t1, t1.t[:, :], abr.t[:, :], lr.t[:, :], ALU.mult, [abr, lr])
            tt(t2, t2.t[:, :], abi.t[:, :], li.t[:, :], ALU.mult, [abi, li])
            tt(t1, t1.t[:, :], t1.t[:, :], t2.t[:, :], ALU.add, [t1, t2])
            tt(cre, cre.t[:, :], t1.t[:, :], den.t[:, :], ALU.mult, [t1, den])
            tt(t1, t1.t[:, :], abi.t[:, :], lr.t[:, :], ALU.mult, [abi, lr])
            tt(t2, t2.t[:, :], abr.t[:, :], li.t[:, :], ALU.mult, [abr, li])
            tt(t1, t1.t[:, :], t1.t[:, :], t2.t[:, :], ALU.subtract, [t1, t2])
            tt(cim, cim.t[:, :], t1.t[:, :], den.t[:, :], ALU.mult, [t1, den])
            bpr, bpi = nbig.get(), nbig.get()
            cre_b, cim_b = bc_last(cre.t[:, :], 32), bc_last(cim.t[:, :], 32)
            tt(bpr, bpr.t[:, :, :], nbr.t[:, :, :], cre_b, ALU.mult, [nbr, cre])
            tt(bpi, bpi.t[:, :, :], nbi.t[:, :, :], cim_b, ALU.mult, [nbi, cim])
            tt(bpr, bpr.t[:, :, :], bpr.t[:, :, :], bpi.t[:, :, :], ALU.subtract, [bpr, bpi])
            tt(bpi, bpi.t[:, :, :], nbi.t[:, :, :], cre_b, ALU.mult, [nbi, cre])
            t3 = nbig.get()
            tt(t3, t3.t[:, :, :], nbr.t[:, :, :], cim_b, ALU.mult, [nbr, cim])
            tt(bpi, bpi.t[:, :, :], bpi.t[:, :, :], t3.t[:, :, :], ALU.add, [bpi, t3])
            for r, src in ((0, bpr), (1, bpi)):
                wb = wb_stage.get()
                for j4 in range(8):
                    ps = ps_mm.get()
                    for jj in range(4):
                        j = j4 * 4 + jj
                        c.op("pe", lambda h, ps=ps, src=src, j=j, jj=jj: h.transpose(ps.t[0:32, jj * 128:(jj + 1) * 128], src.t[:, j, :], ident.t[:, :]),
                             reads=[src, ident], writes=[ps])
                    act(wb, wb.t[:, j4 * 4:(j4 + 1) * 4, :], ps.t[0:32, :].rearrange("p (a b) -> p a b", a=4), AF.Copy, [ps])
                c.dma("sp", wb_d.t[l, d, r], wb.t[:, :, :], reads=[wb], writes=[wb_d])

    rtab = c.sb([128, 6, 128], F32, "rtab")
    rcol = c.sb([128, 3], F32, "rcol")
    c.dma("sp", rtab.t[:, :, :], ret_tab, writes=[rtab])
    c.dma("sp", rcol.t[:, :], ret_col, writes=[rcol])
    r_lg = [c.sb([128, 16], F32, "rlg%d" % l) for l in range(NL)]
    r_ng = [c.sb([128, 8], F32, "rng%d" % l) for l in range(NL)]
    for l in range(NL):
        c.dma("sp", r_lg[l].t[:, :], ret_dec[l], writes=[r_lg[l]])
        c.dma("sp", r_ng[l].t[:, :], ret_ngT[l], writes=[r_ng[l]])
        act(r_lg[l], r_lg[l].t[:, :], r_lg[l].t[:, :], AF.Exp, [r_lg[l]])
        ts(r_lg[l], r_lg[l].t[:, :], r_lg[l].t[:, :], -1.0, None, ALU.mult, None, [r_lg[l]])

    def ret_tables(l):
        DT = AR.tile([128, 8, 128], F32)
        qp = AR.tile([128, 2, 8, 128], F32)
        kp = AR.tile([128, 2, 8], F32)
        dsc = AR.tile([128, 2, 8], F32)
        e1 = AR.tile([128, 128], F32)
        e2 = AR.tile([128, 128], F32)
        for hh in range(8):
            lgf = r_lg[l].t[:, hh:hh + 1]
            lgb = r_lg[l].t[:, 8 + hh:9 + hh]
            act(e1, e1.t[:, :], rtab.t[:, 0, :], AF.Exp, [rtab, r_lg[l]], scale=lgf)
            tt(e1, e1.t[:, :], e1.t[:, :], rtab.t[:, 1, :], ALU.mult, [e1, rtab])
            act(e2, e2.t[:, :], rtab.t[:, 2, :], AF.Exp, [rtab, r_lg[l]], scale=lgb)
            tt(e2, e2.t[:, :], e2.t[:, :], rtab.t[:, 3, :], ALU.mult, [e2, rtab])
            tt(e1, e1.t[:, :], e1.t[:, :], e2.t[:, :], ALU.add, [e1, e2])
            ts(DT, DT.t[:, hh, :], e1.t[:, :], RET_SCALE, None, ALU.mult, None, [e1])
            act(qp, qp.t[:, 0, hh, :], rtab.t[:, 4, :], AF.Exp, [rtab, r_lg[l]], scale=lgf)
            act(qp, qp.t[:, 1, hh, :], rtab.t[:, 5, :], AF.Exp, [rtab, r_lg[l]], scale=lgb)
            act(kp, kp.t[:, 0, hh:hh + 1], rcol.t[:, 0:1], AF.Exp, [rcol, r_lg[l]], scale=lgf)
            act(kp, kp.t[:, 1, hh:hh + 1], rcol.t[:, 1:2], AF.Exp, [rcol, r_lg[l]], scale=lgb)
            act(dsc, dsc.t[:, 0, hh:hh + 1], rcol.t[:, 2:3], AF.Exp, [rcol, r_lg[l]], scale=lgf)
            act(dsc, dsc.t[:, 1, hh:hh + 1], rcol.t[:, 2:3], AF.Exp, [rcol, r_lg[l]], scale=lgb)
        ts(kp, kp.t[:, :, :], kp.t[:, :, :], RET_SCALE, None, ALU.mult, None, [kp])
        return DT, qp, kp, dsc

    glub = [c.sb([128, 8], F32, "glub%d" % l) for l in range(NL)]
    qn_t = [c.sb([128, 6], F32, "qn%d" % l) for l in range(NL)]
    kvn_t = [c.sb([128, 4], F32, "kvn%d" % l) for l in range(NL)]
    kvn_rep = [c.sb([128, 512], F32, "kvnr%d" % l) for l in range(NL)]
    cvw = [c.sb([128, 88, 3], F32, "cvw%d" % l) for l in range(NL)]
    cvb = [c.sb([128, 88], F32, "cvb%d" % l) for l in range(NL)]
    for l in range(NL):
        c.dma("sp", glub[l].t[:, :], glu_bT[l], writes=[glub[l]])
        c.dma("sp", qn_t[l].t[:, :], q_normT[l], writes=[qn_t[l]])
        c.dma("sp", kvn_t[l].t[:, :], kv_normT[l], writes=[kvn_t[l]])
        c.dma("sp", kvn_rep[l].t[:, :], kv_norm_rep[l], writes=[kvn_rep[l]])
        c.dma("sp", cvw[l].t[:, :, :], conv_wT[l], writes=[cvw[l]])
        c.dma("sp", cvb[l].t[:, :], conv_bT[l], writes=[cvb[l]])

    def norm_mod(xsrc, xdep, src0, n, hbuf, dst0, Aap, Bap, xblk_pool):
        xb = xblk_pool.get()
        kw = {"allow_slow_non_contiguous": True} if n == 1 else {}
        c.dma("sp", xb.t[:, :, 0:n], xsrc[:, src0:src0 + n].rearrange("(k p) n -> p k n", p=128), reads=[xdep], writes=[xb], **kw)
        sq = xblk_sq.get()
        act(sq, sq.t[:, :, 0:n], xb.t[:, :, 0:n], AF.Square, [xb])
        ps = ps_aux.get()
        for kc in range(16):
            mm(ps, ps.t[:, 0:n], ones_bf.t[:, :], sq.t[:, kc, 0:n], kc == 0, kc == 15, [ones_bf, sq])
        rs = rstd_p.get()
        rstd_from_ssq(ps, ps.t[:, 0:n], rs, rs.t[:, 0:n], 1.0 / D)
        tt(xb, xb.t[:, :, 0:n], xb.t[:, :, 0:n], bc_last(Aap, n), ALU.mult, [xb] + A_reads)
        tt(xb, xb.t[:, :, 0:n], xb.t[:, :, 0:n], bc_mid(rs.t[:, 0:n], 16), ALU.mult, [xb, rs])
        tt(hbuf, hbuf.t[:, :, dst0:dst0 + n], xb.t[:, :, 0:n], bc_last(Bap, n), ALU.add, [xb] + A_reads)

    A_reads = [PR[l][j] for l in range(NL) for j in range(2)]

    groups = [
        dict(name="p", cond=0, xin=xTp, xin_dep=d_xTp, ntok=1024, L=LP, nseq=NPS, yout=yTp),
        dict(name="s", cond=1, xin=xTs, xin_dep=d_xTs, ntok=LS, L=LS, nseq=1, yout=yTs),
    ]
    x1T, x2T = x1b, x2b

    for G in groups:
        is_p = G["name"] == "p"
        L, nseq, ntok, cj = G["L"], G["nseq"], G["ntok"], G["cond"]
        nseg = ntok // 1024
        for l in range(NL):
            P = PR[l][cj]
            if l == 0:
                xs_ap, xs_dep = G["xin"], G["xin_dep"]
            else:
                xs_ap, xs_dep = x2T.t, x2T.d
            AR.reset()
            hT = AR.tile([128, 16, 1024], BF16)
            xblk_pool = Pool([AR.tile([128, 16, 256], F32) for _ in range(2)])
            xblk_sq = Pool([AR.tile([128, 16, 256], BF16) for _ in range(2)])
            stok = Pool([AR.tile([128, 512], BF16) for _ in range(3)])
            stokf = Pool([AR.tile([128, 512], F32) for _ in range(3)])
            small = Pool([AR.tile([128, 2], F32) for _ in range(4)])
            wA = Pool([AR.tile([128, 16, 512], BF16) for _ in range(2)])
            for sg in range(nseg):
                c0 = sg * 1024
                for b4 in range(4):
                    norm_mod(xs_ap, xs_dep, c0 + b4 * 256, 256, hT, b4 * 256, P.t[:, 0, :], P.t[:, 1, :], xblk_pool)
                fm = []
                for cb in range(6):
                    fm.append((w_in[l], cb * 512, 512, cb * 512, "copy", True))
                for cb in range(4):
                    fm.append((w_in[l], 4096 + cb * 512, 512, R_G + cb * 512, "copy", True))
                fm.append((w_in[l], 4096 + 2048, 256, R_G + 2048, "copy", True))
                fm.append((w_in_kr[l], 0, 64, R_KR, "copy", False))
                for cb in range(12):
                    fm.append((w_in[l], 6464 + cb * 512, 512, R_GATE + cb * 512, "sig", True))
                tokm = [(2048, 512, ktok, 0), (2560, 512, ktok, 512), (3072, 512, vtok, 0), (3584, 512, vtok, 512)]
                if is_p:
                    tokm += [(5888, 512, "ckv", 0), (6400, 64, "kr", 0)]
                tok_by_col = {t[0]: t for t in tokm}
                for bi, (wsrc, col0, ncols, zr0, epi, is_main) in enumerate(fm):
                    w = wA.get()
                    load_w(w, 16, ncols, wsrc[:, col0:col0 + ncols], wcA[l], bi, is_p)
                    nm = (ncols + 127) // 128
                    for mi in range(nm):
                        mw = min(128, ncols - mi * 128)
                        st = stg_bf.get()
                        for tb in range(2):
                            ps = ps_mm.get()
                            for kc in range(16):
                                mm(ps, ps.t[0:mw, :], w.t[:, kc, mi * 128:mi * 128 + mw], hT.t[:, kc, tb * 512:(tb + 1) * 512], kc == 0, kc == 15, [w, hT])
                            if epi == "sig":
                                act(st, st.t[0:mw, tb * 512:(tb + 1) * 512], ps.t[0:mw, :], AF.Sigmoid, [ps])
                            elif (mi + tb) % 2 == 0:
                                act(st, st.t[0:mw, tb * 512:(tb + 1) * 512], ps.t[0:mw, :], AF.Copy, [ps])
                            else:
                                c.op("dve", lambda h, st=st, ps=ps, mw=mw, tb=tb: h.tensor_copy(st.t[0:mw, tb * 512:(tb + 1) * 512], ps.t[0:mw, :]), reads=[ps], writes=[st])
                        c.dma("sp", zT.t[zr0 + mi * 128:zr0 + mi * 128 + mw, c0:c0 + 1024], st.t[0:mw, :], reads=[st], writes=[zT])
                    if is_main and col0 in tok_by_col:
                        _, _, dst, dcol = tok_by_col.pop(col0)
                        for tti in range(8):
                            ps = ps_mm.get()
                            for kc in range(16):
                                mm(ps, ps.t[:, 0:ncols], hT.t[:, kc, tti * 128:(tti + 1) * 128], w.t[:, kc, 0:ncols], kc == 0, kc == 15, [w, hT])
                            so = stok.get()
                            act(so, so.t[:, 0:ncols], ps.t[:, 0:ncols], AF.Copy, [ps])
                            c.dma("sp", dst.t[c0 + tti * 128:c0 + (tti + 1) * 128, dcol:dcol + ncols], so.t[:, 0:ncols], reads=[so], writes=[dst])
                for li_, (col0, ncols, dst, dcol) in enumerate(list(tok_by_col.values())):
                    w = wA.get()
                    if isinstance(dst, str):
                        c.dma("pool", w.t[:, :, 0:ncols], w_in[l, :, col0:col0 + ncols].rearrange("(k p) n -> p k n", p=128), writes=[w])
                    else:
                        load_w(w, 16, ncols, w_in[l, :, col0:col0 + ncols], wcA[l], 29 + li_, is_p)
                    for tti in range(8):
                        ps = ps_mm.get()
                        for kc in range(16):
                            mm(ps, ps.t[:, 0:ncols], hT.t[:, kc, tti * 128:(tti + 1) * 128], w.t[:, kc, 0:ncols], kc == 0, kc == 15, [w, hT])
                        if dst == "ckv":
                            sqf = stokf.get()
                            ss = small.get()
                            c.op("act", lambda h, sqf=sqf, ps=ps, ss=ss: h.activation(sqf.t[:, :], ps.t[:, :], AF.Square, accum_out=ss.t[:, 0:1]),
                                 reads=[ps], writes=[sqf, ss])
                            act(ss, ss.t[:, 1:2], ss.t[:, 0:1], AF.Sqrt, [ss, eps_t], bias=eps_t.t[:, 0:1], scale=1.0 / 512)
                            recip(ss, ss.t[:, 1:2], ss.t[:, 1:2], [ss])
                            so = stokf.get()
                            stt(so, so.t[:, :], ps.t[:, :], ss.t[:, 1:2], kvn_rep[l].t[:, :], ALU.mult, ALU.mult, [ps, ss, kvn_rep[l]])
                            sq_i, t0 = (tti * 128) // LP, (tti * 128) % LP
                            c.dma("sp", ckv_o[sq_i, l, t0:t0 + 128, :], so.t[:, :], reads=[so], writes=[d_out])
                        elif dst == "kr":
                            so = stokf.get()
                            act(so, so.t[:, 0:64], ps.t[:, 0:64], AF.Copy, [ps])
                            sq_i, t0 = (tti * 128) // LP, (tti * 128) % LP
                            c.dma("sp", kr_o[sq_i, l, t0:t0 + 128, :], so.t[:, 0:64], reads=[so], writes=[d_out])
                        else:
                            so = stok.get()
                            act(so, so.t[:, 0:ncols], ps.t[:, 0:ncols], AF.Copy, [ps])
                            c.dma("sp", dst.t[c0 + tti * 128:c0 + (tti + 1) * 128, dcol:dcol + ncols], so.t[:, 0:ncols], reads=[so], writes=[dst])
            if debug_stop == "A":
                break
            AR.reset()
            HL = L // 2
            Ec = AR.tile([128, L], F32)
            Es = AR.tile([128, L], F32)
            tq = AR.tile([128, HL], F32)
            btil = [AR.tile([128, L], F32) for _ in range(2)]
            sbf = [[AR.tile([128, ntok], BF16) for _ in range(2)] for _ in range(2)]
            u_pool = Pool([AR.tile([32, ntok], BF16) for _ in range(1)])
            tmp32 = Pool([AR.tile([128, 512], F32) for _ in range(4)])
            tmp32p = Pool([AR.tile([128, 512], F32) for _ in range(4)])
            yj_p = Pool([AR.tile([32, 512], F32) for _ in range(2)])
            g_p = Pool([AR.tile([32, 512], F32) for _ in range(3)])
            wbj = Pool([AR.tile([32, 128], BF16) for _ in range(8)])
            wcj = Pool([AR.tile([128, 32], BF16) for _ in range(4)])
            fin = [[AR.tile([128, 2, 32], F32) for _ in range(2)] for _ in range(nseq)] if is_p else None
            fin_tmp = Pool([AR.tile([128, 2], F32) for _ in range(4)])
            BLK = min(512, L)
            br_t, bi_t = btil
            for j in range(32):
                uj = u_pool.get()
                c.dma("sp", uj.t[:, :], zT.t[R_U + j * 32:R_U + (j + 1) * 32, 0:ntok], reads=[zT], writes=[uj])
                for d in range(2):
                    wbr, wbi = wbj.get(), wbj.get()
                    c.dma("sp", wbr.t[:, :], wb_d.t[l, d, 0, :, j, :], reads=[wb_d], writes=[wbr])
                    c.dma("sp", wbi.t[:, :], wb_d.t[l, d, 1, :, j, :], reads=[wb_d], writes=[wbi])
                    dc, ds = s5_dc[l][d], s5_ds[l][d]
                    c.op("dve", lambda h, dc=dc, j=j, Ec=Ec: h.tensor_copy(Ec.t[:, 0:1], dc.t[:, 0, j:j + 1]), reads=[dc], writes=[Ec])
                    c.op("dve", lambda h, ds=ds, j=j, Es=Es: h.tensor_copy(Es.t[:, 0:1], ds.t[:, 0, j:j + 1]), reads=[ds], writes=[Es])
                    n = 1
                    k = 0
                    while n < L:
                        Ck, Sk = dc.t[:, k, j:j + 1], ds.t[:, k, j:j + 1]
                        ts(tq, tq.t[:, 0:n], Es.t[:, 0:n], Sk, None, ALU.mult, None, [Es, ds])
                        stt(Ec, Ec.t[:, n:2 * n], Ec.t[:, 0:n], Ck, tq.t[:, 0:n], ALU.mult, ALU.subtract, [Ec, dc, tq])
                        ts(tq, tq.t[:, 0:n], Ec.t[:, 0:n], Sk, None, ALU.mult, None, [Ec, ds])
                        stt(Es, Es.t[:, n:2 * n], Es.t[:, 0:n], Ck, tq.t[:, 0:n], ALU.mult, ALU.add, [Es, dc, tq])
                        n *= 2
                        k += 1
                    mg = s5_mag[l][d].t[:, j:j + 1]
                    magb = bass.AP(mg.tensor, mg.offset, [list(mg.ap[0]), [0, L]])
                    sre, sim = sbf[d]
                    for s in range(nseq):
                        s0 = s * L
                        for blk in range(L // BLK):
                            cols = slice(s0 + blk * BLK, s0 + (blk + 1) * BLK)
                            if d == 0:
                                tcols = slice(blk * BLK, (blk + 1) * BLK)
                            else:
                                tcols = slice(L - (blk + 1) * BLK, L - blk * BLK)
                            pr, pi = ps_mm.get(), ps_mm.get()
                            mm(pr, pr.t[:, 0:BLK], wbr.t[:, :], uj.t[:, cols], True, True, [wbr, uj])
                            mm(pi, pi.t[:, 0:BLK], wbi.t[:, :], uj.t[:, cols], True, True, [wbi, uj])
                            prv = pr.t[:, 0:BLK] if d == 0 else rev(pr.t[:, 0:BLK])
                            piv = pi.t[:, 0:BLK] if d == 0 else rev(pi.t[:, 0:BLK])
                            t1, t2 = tmp32.get(), tmp32.get()
                            tt(t1, t1.t[:, 0:BLK], prv, Ec.t[:, tcols], ALU.mult, [pr, Ec])
                            tt(t2, t2.t[:, 0:BLK], piv, Es.t[:, tcols], ALU.mult, [pi, Es])
                            tt(br_t, br_t.t[:, tcols], t1.t[:, 0:BLK], t2.t[:, 0:BLK], ALU.add, [t1, t2])
                            t3, t4 = tmp32.get(), tmp32.get()
                            tt(t3, t3.t[:, 0:BLK], piv, Ec.t[:, tcols], ALU.mult, [pi, Ec])
                            tt(t4, t4.t[:, 0:BLK], prv, Es.t[:, tcols], ALU.mult, [pr, Es])
                            tt(bi_t, bi_t.t[:, tcols], t3.t[:, 0:BLK], t4.t[:, 0:BLK], ALU.subtract, [t3, t4])
                        if is_p and l == 0 and j == 0 and s == 0:
                            dump("bt_r%d" % d, br_t.t[:, :], [128, L], F32, [br_t])
                            dump("bt_i%d" % d, bi_t.t[:, :], [128, L], F32, [bi_t])
                        for ri, bt in enumerate(btil):
                            if is_p:
                                init = 0.0
                                rd = [bt, s5_mag[l][d]]
                            else:
                                init = s5_init[l][d].t[:, j, ri:ri + 1]
                                rd = [bt, s5_mag[l][d], s5_init[l][d]]
                            c.op("dve", lambda h, bt=bt, magb=magb, init=init: h.tensor_tensor_scan(bt.t[:, :], magb, bt.t[:, :], init, ALU.mult, ALU.add),
                                 reads=rd, writes=[bt])
                        if is_p and l == 0 and j == 0 and s == 0:
                            dump("Ec%d" % d, Ec.t[:, :], [128, L], F32, [Ec])
                            dump("Es%d" % d, Es.t[:, :], [128, L], F32, [Es])
                            dump("sr%d" % d, br_t.t[:, :], [128, L], F32, [br_t])
                            dump("si%d" % d, bi_t.t[:, :], [128, L], F32, [bi_t])
                        for q0 in range(0, L, 512):
                            qn_ = min(512, L - q0)
                            tc = slice(q0, q0 + qn_)
                            if d == 0:
                                ore, oim = sre.t[:, s0 + q0:s0 + q0 + qn_], sim.t[:, s0 + q0:s0 + q0 + qn_]
                            else:
                                ore = rev(sre.t[:, s0 + L - q0 - qn_:s0 + L - q0])
                                oim = rev(sim.t[:, s0 + L - q0 - qn_:s0 + L - q0])
                            t1, t2 = tmp32.get(), tmp32.get()
                            tt(t1, t1.t[:, 0:qn_], br_t.t[:, tc], Ec.t[:, tc], ALU.mult, [br_t, Ec])
                            tt(t2, t2.t[:, 0:qn_], bi_t.t[:, tc], Es.t[:, tc], ALU.mult, [bi_t, Es])
                            tt(sre, ore, t1.t[:, 0:qn_], t2.t[:, 0:qn_], ALU.subtract, [t1, t2])
                            t3, t4 = tmp32p.get(), tmp32p.get()
                            tt(t3, t3.t[:, 0:qn_], bi_t.t[:, tc], Ec.t[:, tc], ALU.mult, [bi_t, Ec], eng="pool")
                            tt(t4, t4.t[:, 0:qn_], br_t.t[:, tc], Es.t[:, tc], ALU.mult, [br_t, Es], eng="pool")
                            tt(sim, oim, t3.t[:, 0:qn_], t4.t[:, 0:qn_], ALU.add, [t3, t4], eng="pool")
                        if is_p:
                            ft = fin_tmp.get()
                            f_ = fin[s][d]
                            e_c, e_s = Ec.t[:, L - 1:L], Es.t[:, L - 1:L]
                            tt(ft, ft.t[:, 0:1], br_t.t[:, L - 1:L], e_c, ALU.mult, [br_t, Ec])
                            tt(ft, ft.t[:, 1:2], bi_t.t[:, L - 1:L], e_s, ALU.mult, [bi_t, Es])
                            tt(f_, f_.t[:, 0, j:j + 1], ft.t[:, 0:1], ft.t[:, 1:2], ALU.subtract, [ft])
                            ft = fin_tmp.get()
                            tt(ft, ft.t[:, 0:1], bi_t.t[:, L - 1:L], e_c, ALU.mult, [bi_t, Ec])
                            tt(ft, ft.t[:, 1:2], br_t.t[:, L - 1:L], e_s, ALU.mult, [br_t, Es])
                            tt(f_, f_.t[:, 1, j:j + 1], ft.t[:, 0:1], ft.t[:, 1:2], ALU.add, [ft])
                wcr, wci = wcj.get(), wcj.get()
                c.dma("sp", wcr.t[:, :], wc_d.t[l, 0, :, j, :], reads=[wc_d], writes=[wcr])
                c.dma("sp", wci.t[:, :], wc_d.t[l, 1, :, j, :], reads=[wc_d], writes=[wci])
                for blk in range(ntok // 512):
                    cols = slice(blk * 512, (blk + 1) * 512)
                    ps = ps_mm.get()
                    mm(ps, ps.t[0:32, :], wcr.t[:, :], sbf[0][0].t[:, cols], True, False, [wcr, sbf[0][0]])
                    mm(ps, ps.t[0:32, :], wci.t[:, :], sbf[0][1].t[:, cols], False, False, [wci, sbf[0][1]])
                    mm(ps, ps.t[0:32, :], wcr.t[:, :], sbf[1][0].t[:, cols], False, False, [wcr, sbf[1][0]])
                    mm(ps, ps.t[0:32, :], wci.t[:, :], sbf[1][1].t[:, cols], False, True, [wci, sbf[1][1]])
                    yj = yj_p.get()
                    stt(yj, yj.t[:, :], uj.t[:, cols], s5_dsk[l].t[:, j:j + 1], ps.t[0:32, :], ALU.mult, ALU.add, [uj, s5_dsk[l], ps])
                    g1_ = g_p.get()
                    tt(g1_, g1_.t[:, :], yj.t[:, :], yj.t[:, :], ALU.mult, [yj])
                    ts(g1_, g1_.t[:, :], g1_.t[:, :], 0.044715, 1.0, ALU.mult, ALU.add, [g1_])
                    tt(g1_, g1_.t[:, :], g1_.t[:, :], yj.t[:, :], ALU.mult, [g1_, yj])
                    act(g1_, g1_.t[:, :], g1_.t[:, :], AF.Sigmoid, [g1_], scale=2.0 * math.sqrt(2.0 / math.pi))
                    so = stg_bf.get()
                    tt(so, so.t[0:32, 0:512], g1_.t[:, :], yj.t[:, :], ALU.mult, [g1_, yj])
                    c.dma("sp", ygT.t[j * 32:(j + 1) * 32, cols], so.t[0:32, 0:512], reads=[so], writes=[ygT])
            if is_p:
                for s in range(nseq):
                    for d in range(2):
                        c.dma("sp", s5_o[d, s, l].rearrange("r q j -> q r j"), fin[s][d].t[:, :, :], reads=[fin[s][d]], writes=[d_out])
            AR.reset()
            wg = AR.tile([128, 8, 1024], BF16)
            c.dma("pool", wg.t[:, :, :], glu_w[l].rearrange("(k p) n -> p k n", p=128), writes=[wg])
            yg_p = Pool([AR.tile([128, 8, 512], BF16) for _ in range(2)])
            sg_p = Pool([AR.tile([128, 512], F32) for _ in range(3)])
            for tb in range(ntok // 512):
                cols = slice(tb * 512, (tb + 1) * 512)
                yg = yg_p.get()
                c.dma("sp", yg.t[:, :, :], ygT.t[:, cols].rearrange("(k p) n -> p k n", p=128), reads=[ygT], writes=[yg])
                for m in range(8):
                    ps = ps_mm.get()
                    for kc in range(8):
                        mm(ps, ps.t[:, :], wg.t[:, kc, m * 128:(m + 1) * 128], yg.t[:, kc, :], kc == 0, kc == 7, [wg, yg])
                    sg = sg_p.get()
                    act(sg, sg.t[:, :], ps.t[:, :], AF.Sigmoid, [ps, glub[l]], bias=glub[l].t[:, m:m + 1])
                    so = stg_bf.get()
                    tt(so, so.t[:, 0:512], sg.t[:, :], yg.t[:, m, :], ALU.mult, [sg, yg])
                    c.dma("sp", yT.t[m * 128:(m + 1) * 128, cols], so.t[:, 0:512], reads=[so], writes=[yT])
            AR.reset()
            NCH = L // 128
            rDT, rqp, rkp, rds = ret_tables(l)
            qT_ = AR.tile([128, ntok], BF16)
            kT_ = AR.tile([128, ntok], BF16)
            qf_ = AR.tile([128, ntok], BF16)
            qb_ = AR.tile([128, ntok], BF16)
            kt_ = AR.tile([128, ntok // 128, 128], BF16)
            vt_ = AR.tile([128, ntok // 128, 128], BF16)
            kf_ = AR.tile([128, ntok // 128, 128], BF16)
            kb_ = AR.tile([128, ntok // 128, 128], BF16)
            acc = AR.tile([128, ntok], F32)
            Sst = [AR.tile([128, 128], F32) for _ in range(2)]
            Sbf = Pool([AR.tile([128, 128], BF16) for _ in range(3)])
            pt_p = Pool([AR.tile([128, 128], BF16) for _ in range(3)])
            gt_p = Pool([AR.tile([128, 512], BF16) for _ in range(2)])
            w32 = Pool([AR.tile([128, 512], F32) for _ in range(6)])
            obf_p = Pool([AR.tile([128, 512], BF16) for _ in range(2)])
            for hh in range(8):
                c.dma("sp", qT_.t[:, :], zT.t[R_Q + hh * 128:R_Q + (hh + 1) * 128, 0:ntok], reads=[zT], writes=[qT_])
                c.dma("sp", kT_.t[:, :], zT.t[R_K + hh * 128:R_K + (hh + 1) * 128, 0:ntok], reads=[zT], writes=[kT_])
                c.dma("sp", kt_.t[:, :, :], ktok.t[0:ntok, hh * 128:(hh + 1) * 128].rearrange("(c j) d -> j c d", j=128), reads=[ktok], writes=[kt_])
                c.dma("sp", vt_.t[:, :, :], vtok.t[0:ntok, hh * 128:(hh + 1) * 128].rearrange("(c j) d -> j c d", j=128), reads=[vtok], writes=[vt_])
                q3 = qT_.t[:, :].rearrange("p (a b) -> p a b", b=128)
                tt(qf_, qf_.t[:, :].rearrange("p (a b) -> p a b", b=128), q3, bc_mid(rqp.t[:, 0, hh, :], ntok // 128), ALU.mult, [qT_, rqp])
                tt(qb_, qb_.t[:, :].rearrange("p (a b) -> p a b", b=128), q3, bc_mid(rqp.t[:, 1, hh, :], ntok // 128), ALU.mult, [qT_, rqp])
                ts(kf_, kf_.t[:, :, :], kt_.t[:, :, :], rkp.t[:, 0, hh:hh + 1], None, ALU.mult, None, [kt_, rkp])
                ts(kb_, kb_.t[:, :, :], kt_.t[:, :, :], rkp.t[:, 1, hh:hh + 1], None, ALU.mult, None, [kt_, rkp])
                for s in range(nseq):
                    for d in range(2):
                        S = Sst[d]
                        if is_p:
                            c.op("pool", lambda h, S=S: h.memset(S.t[:, :], 0.0), writes=[S])
                        else:
                            c.dma("sp", S.t[:, :], ret_s0[l, d, hh], writes=[S])
                        order = range(NCH) if d == 0 else range(NCH - 1, -1, -1)
                        qd = qf_ if d == 0 else qb_
                        kd = kf_ if d == 0 else kb_
                        for ch in order:
                            gc = s * NCH + ch
                            cs_ = slice(gc * 128, (gc + 1) * 128)
                            sb_ = Sbf.get()
                            act(sb_, sb_.t[:, :], S.t[:, :], AF.Copy, [S])
                            po = ps_mm.get()
                            if d == 0:
                                pa = ps_mm.get()
                                mm(pa, pa.t[:, 0:128], kT_.t[:, cs_], qT_.t[:, cs_], True, True, [kT_, qT_])
                                pt = pt_p.get()
                                tt(pt, pt.t[:, :], pa.t[:, 0:128], rDT.t[:, hh, :], ALU.mult, [pa, rDT])
                                mm(po, po.t[:, 0:128], vt_.t[:, gc, :], pt.t[:, :], True, False, [vt_, pt])
                                mm(po, po.t[:, 0:128], sb_.t[:, :], qd.t[:, cs_], False, True, [sb_, qd])
                                act(acc, acc.t[:, cs_], po.t[:, 0:128], AF.Copy, [po])
                            else:
                                mm(po, po.t[:, 0:128], sb_.t[:, :], qd.t[:, cs_], True, True, [sb_, qd])
                                tt(acc, acc.t[:, cs_], acc.t[:, cs_], po.t[:, 0:128], ALU.add, [acc, po])
                            pS = ps_mm.get()
                            mm(pS, pS.t[:, 0:128], kd.t[:, gc, :], vt_.t[:, gc, :], True, True, [kd, vt_])
                            stt(S, S.t[:, :], S.t[:, :], rds.t[:, d, hh:hh + 1], pS.t[:, 0:128], ALU.mult, ALU.add, [S, rds, pS])
                        if is_p:
                            c.dma("sp", ret_o[d, s, l, hh], S.t[:, :], reads=[S], writes=[d_out])
                for tb in range(ntok // 512):
                    cols = slice(tb * 512, (tb + 1) * 512)
                    ob = obf_p.get()
                    act(ob, ob.t[:, :], acc.t[:, cols], AF.Copy, [acc])
                    sq = sq_p.get()
                    act(sq, sq.t[:, :], acc.t[:, cols], AF.Square, [acc])
                    pm, pv = ps_aux.get(), ps_aux.get()
                    mm(pm, pm.t[:, :], ones128.t[:, :], ob.t[:, :], True, True, [ones128, ob])
                    mm(pv, pv.t[:, :], ones128.t[:, :], sq.t[:, :], True, True, [ones128, sq])
                    mean = w32.get()
                    act(mean, mean.t[:, :], pm.t[:, :], AF.Copy, [pm])
                    var = w32.get()
                    tt(var, var.t[:, :], mean.t[:, :], mean.t[:, :], ALU.mult, [mean])
                    tt(var, var.t[:, :], pv.t[:, :], var.t[:, :], ALU.subtract, [pv, var])
                    ts(var, var.t[:, :], var.t[:, :], 0.0, None, ALU.max, None, [var])
                    act(var, var.t[:, :], var.t[:, :], AF.Sqrt, [var, eps_t], bias=eps_t.t[:, 0:1])
                    recip(var, var.t[:, :], var.t[:, :], [var])
                    cen = w32.get()
                    tt(cen, cen.t[:, :], acc.t[:, cols], mean.t[:, :], ALU.subtract, [acc, mean])
                    tt(cen, cen.t[:, :], cen.t[:, :], var.t[:, :], ALU.mult, [cen, var])
                    gt = gt_p.get()
                    c.dma("sp", gt.t[:, :], zT.t[R_G + hh * 128:R_G + (hh + 1) * 128, cols], reads=[zT], writes=[gt])
                    sgt = w32.get()
                    act(sgt, sgt.t[:, :], gt.t[:, :], AF.Silu, [gt])
                    so = stg_bf.get()
                    stt(so, so.t[:, 0:512], cen.t[:, :], r_ng[l].t[:, hh:hh + 1], sgt.t[:, :], ALU.mult, ALU.mult, [cen, r_ng[l], sgt])
                    c.dma("sp", yT.t[DM + hh * 128:DM + (hh + 1) * 128, cols], so.t[:, 0:512], reads=[so], writes=[yT])
            AR.reset()
            cqn = AR.tile([128, 6, 512], BF16)
            wq_all = AR.tile([128, 6, 1536], BF16)
            c.dma("pool", wq_all.t[:, :, :], w_uq[l].rearrange("(k p) n -> p k n", p=128), writes=[wq_all])
            ld_p = Pool([AR.tile([128, 6, 512], BF16) for _ in range(2)])
            ldsq = Pool([AR.tile([128, 6, 512], BF16) for _ in range(2)])
            k32 = Pool([AR.tile([32, 512], F32) for _ in range(6)])
            rope_p = Pool([AR.tile([32, 512], F32) for _ in range(4)])
            qst = Pool([AR.tile([128, 512], BF16) for _ in range(3)])
            qst32 = Pool([AR.tile([32, 512], BF16) for _ in range(4)])
            for tb in range(ntok // 512):
                cols = slice(tb * 512, (tb + 1) * 512)
                ld = ld_p.get()
                c.dma("sp", ld.t[:, :, :], zT.t[R_CQ:R_CQ + 768, cols].rearrange("(k p) n -> p k n", p=128), reads=[zT], writes=[ld])
                sq = ldsq.get()
                act(sq, sq.t[:, :, :], ld.t[:, :, :], AF.Square, [ld])
                ps = ps_aux.get()
                for kc in range(6):
                    mm(ps, ps.t[:, :], ones_bf.t[:, :], sq.t[:, kc, :], kc == 0, kc == 5, [ones_bf, sq])
                rs = rstd_p.get()
                rstd_from_ssq(ps, ps.t[:, :], rs, rs.t[:, 0:512], 1.0 / 768)
                for kc in range(6):
                    stt(cqn, cqn.t[:, kc, :], ld.t[:, kc, :], qn_t[l].t[:, kc:kc + 1], rs.t[:, 0:512], ALU.mult, ALU.mult, [ld, qn_t[l], rs])
                if not is_p:
                    rc, rsn = rope_p.get(), rope_p.get()
                    c.dma("sp", rc.t[:, :], rope_cs[0, :, cols], writes=[rc])
                    c.dma("sp", rsn.t[:, :], rope_cs[1, :, cols], writes=[rsn])
                for hh in range(8):
                    ps = ps_mm.get()
                    for kc in range(6):
                        mm(ps, ps.t[:, :], wq_all.t[:, kc, hh * 192:hh * 192 + 128], cqn.t[:, kc, :], kc == 0, kc == 5, [wq_all, cqn])
                    so = qst.get()
                    act(so, so.t[:, :], ps.t[:, :], AF.Copy, [ps])
                    c.dma("sp", qscr.t[hh, 0:128, cols], so.t[:, :], reads=[so], writes=[qscr])
                    pr = []
                    for hf in range(2):
                        p_ = ps_mm.get()
                        for kc in range(6):
                            mm(p_, p_.t[0:32, :], wq_all.t[:, kc, hh * 192 + 128 + hf * 32:hh * 192 + 160 + hf * 32], cqn.t[:, kc, :], kc == 0, kc == 5, [wq_all, cqn])
                        pr.append(p_)
                    o0, o1 = qst32.get(), qst32.get()
                    if is_p:
                        act(o0, o0.t[:, :], pr[0].t[0:32, :], AF.Copy, [pr[0]])
                        act(o1, o1.t[:, :], pr[1].t[0:32, :], AF.Copy, [pr[1]])
                    else:
                        a1_, a2_ = k32.get(), k32.get()
                        tt(a1_, a1_.t[:, :], pr[0].t[0:32, :], rc.t[:, :], ALU.mult, [pr[0], rc])
                        tt(a2_, a2_.t[:, :], pr[1].t[0:32, :], rsn.t[:, :], ALU.mult, [pr[1], rsn])
                        tt(o0, o0.t[:, :], a1_.t[:, :], a2_.t[:, :], ALU.subtract, [a1_, a2_])
                        a3_, a4_ = k32.get(), k32.get()
                        tt(a3_, a3_.t[:, :], pr[0].t[0:32, :], rsn.t[:, :], ALU.mult, [pr[0], rsn])
                        tt(a4_, a4_.t[:, :], pr[1].t[0:32, :], rc.t[:, :], ALU.mult, [pr[1], rc])
                        tt(o1, o1.t[:, :], a3_.t[:, :], a4_.t[:, :], ALU.add, [a3_, a4_])
                    c.dma("sp", qscr.t[hh, 128:160, cols], o0.t[:, :], reads=[o0], writes=[qscr])
                    c.dma("sp", qscr.t[hh, 160:192, cols], o1.t[:, :], reads=[o1], writes=[qscr])
            AR.reset()
            SK = L if is_p else L + PAST
            NKT = SK // 128
            ckv = AR.tile([128, 4, nseq * SK], BF16)
            kr = [AR.tile([32, nseq * SK], BF16) for _ in range(2)]
            qn_h = AR.tile([128, ntok], BF16)
            qr_h = [AR.tile([32, ntok], BF16) for _ in range(2)]
            kn_h = AR.tile([128, nseq * SK], BF16)
            v_h = AR.tile([128, nseq * NKT, 128], BF16)
            wk_h = AR.tile([128, 4, 128], BF16)
            wv_h = AR.tile([128, 4, 128], BF16)
            ld_p = Pool([AR.tile([128, 4, 512], BF16) for _ in range(2)])
            ldsq = Pool([AR.tile([128, 4, 512], BF16) for _ in range(1)])
            pt_p = Pool([AR.tile([128, 512], BF16) for _ in range(4)])
            w32 = Pool([AR.tile([128, 512], F32) for _ in range(3)])
            k32 = Pool([AR.tile([32, 512], F32) for _ in range(4)])
            rope_p = Pool([AR.tile([32, 512], F32) for _ in range(2)])
            kraw = Pool([AR.tile([32, 512], BF16) for _ in range(2)])
            for tb in range(ntok // 512):
                cols = slice(tb * 512, (tb + 1) * 512)
                ld = ld_p.get()
                c.dma("sp", ld.t[:, :, :], zT.t[R_CKV:R_CKV + 512, cols].rearrange("(k p) n -> p k n", p=128), reads=[zT], writes=[ld])
                sq = ldsq.get()
                act(sq, sq.t[:, :, :], ld.t[:, :, :], AF.Square, [ld])
                ps = ps_aux.get()
                for kc in range(4):
                    mm(ps, ps.t[:, :], ones_bf.t[:, :], sq.t[:, kc, :], kc == 0, kc == 3, [ones_bf, sq])
                rs = rstd_p.get()
                rstd_from_ssq(ps, ps.t[:, :], rs, rs.t[:, 0:512], 1.0 / 512)
                for kc in range(4):
                    stt(ckv, ckv.t[:, kc, cols], ld.t[:, kc, :], kvn_t[l].t[:, kc:kc + 1], rs.t[:, 0:512], ALU.mult, ALU.mult, [ld, kvn_t[l], rs])
                kw0, kw1 = kraw.get(), kraw.get()
                c.dma("sp", kw0.t[:, :], zT.t[R_KR:R_KR + 32, cols], reads=[zT], writes=[kw0])
                c.dma("sp", kw1.t[:, :], zT.t[R_KR + 32:R_KR + 64, cols], reads=[zT], writes=[kw1])
                if is_p:
                    c.op("dve", lambda h, kw0=kw0, cols=cols, k0_=kr[0]: h.tensor_copy(k0_.t[:, cols], kw0.t[:, :]), reads=[kw0], writes=[kr[0]])
                    c.op("dve", lambda h, kw1=kw1, cols=cols, k1_=kr[1]: h.tensor_copy(k1_.t[:, cols], kw1.t[:, :]), reads=[kw1], writes=[kr[1]])
                else:
                    rc, rsn = rope_p.get(), rope_p.get()
                    c.dma("sp", rc.t[:, :], rope_cs[0, :, cols], writes=[rc])
                    c.dma("sp", rsn.t[:, :], rope_cs[1, :, cols], writes=[rsn])
                    a1_, a2_ = k32.get(), k32.get()
                    tt(a1_, a1_.t[:, :], kw0.t[:, :], rc.t[:, :], ALU.mult, [kw0, rc])
                    tt(a2_, a2_.t[:, :], kw1.t[:, :], rsn.t[:, :], ALU.mult, [kw1, rsn])
                    tt(kr[0], kr[0].t[:, cols], a1_.t[:, :], a2_.t[:, :], ALU.subtract, [a1_, a2_])
                    a3_, a4_ = k32.get(), k32.get()
                    tt(a3_, a3_.t[:, :], kw0.t[:, :], rsn.t[:, :], ALU.mult, [kw0, rsn])
                    tt(a4_, a4_.t[:, :], kw1.t[:, :], rc.t[:, :], ALU.mult, [kw1, rc])
                    tt(kr[1], kr[1].t[:, cols], a3_.t[:, :], a4_.t[:, :], ALU.add, [a3_, a4_])
            if not is_p:
                c.dma("pool", ckv.t[:, :, L:L + PAST], cache_ckvT[l].rearrange("(k p) n -> p k n", p=128), writes=[ckv])
                for hf in range(2):
                    c.dma("pool", kr[hf].t[:, L:L + PAST], cache_kr[l, hf], writes=[kr[hf]])
            for hh in range(8):
                c.dma("pool", wk_h.t[:, :, :], w_uk[l, :, hh * 128:(hh + 1) * 128].rearrange("(k p) n -> p k n", p=128), writes=[wk_h])
                c.dma("pool", wv_h.t[:, :, :], w_uv[l, :, hh * 128:(hh + 1) * 128].rearrange("(k p) n -> p k n", p=128), writes=[wv_h])
                c.dma("sp", qn_h.t[:, :], qscr.t[hh, 0:128, 0:ntok], reads=[qscr], writes=[qn_h])
                c.dma("sp", qr_h[0].t[:, :], qscr.t[hh, 128:160, 0:ntok], reads=[qscr], writes=[qr_h[0]])
                c.dma("sp", qr_h[1].t[:, :], qscr.t[hh, 160:192, 0:ntok], reads=[qscr], writes=[qr_h[1]])
                nk_tot = nseq * SK
                for k0 in range(0, nk_tot, 512):
                    kn_ = min(512, nk_tot - k0)
                    ps = ps_mm.get()
                    for kc in range(4):
                        mm(ps, ps.t[:, 0:kn_], wk_h.t[:, kc, :], ckv.t[:, kc, k0:k0 + kn_], kc == 0, kc == 3, [wk_h, ckv])
                    act(kn_h, kn_h.t[:, k0:k0 + kn_], ps.t[:, 0:kn_], AF.Copy, [ps])
                for kt4 in range(0, nseq * NKT, 4):
                    nn = min(4, nseq * NKT - kt4)
                    ps = ps_mm.get()
                    for i in range(nn):
                        kt = kt4 + i
                        for kc in range(4):
                            mm(ps, ps.t[:, i * 128:(i + 1) * 128], ckv.t[:, kc, kt * 128:(kt + 1) * 128], wv_h.t[:, kc, :], kc == 0, kc == 3, [ckv, wv_h])
                    c.op("dve", lambda h, ps=ps, kt4=kt4, nn=nn, v_h=v_h: h.tensor_copy(v_h.t[:, kt4:kt4 + nn, :], ps.t[:, 0:nn * 128].rearrange("p (a b) -> p a b", b=128)),
                         reads=[ps], writes=[v_h])
                if is_p and l == 0 and hh == 0:
                    dump("qn", qn_h.t[:, :], [128, ntok], BF16, [qn_h])
                    dump("qr0", qr_h[0].t[:, :], [32, ntok], BF16, [qr_h[0]])
                    dump("kn", kn_h.t[:, :], [128, nseq * SK], BF16, [kn_h])
                    dump("kr0", kr[0].t[:, :], [32, nseq * SK], BF16, [kr[0]])
                    dump("vh", v_h.t[:, :, :], [128, nseq * NKT, 128], BF16, [v_h])
                    dump("ckv", ckv.t[:, :, :], [128, 4, nseq * SK], BF16, [ckv])
                QB = min(512, L)
                for s in range(nseq):
                    for qb in range(L // QB):
                        qc = slice(s * L + qb * QB, s * L + (qb + 1) * QB)
                        pend = []
                        for kt in range(NKT + 2):
                            if kt < NKT:
                                kc_ = slice(s * SK + kt * 128, s * SK + (kt + 1) * 128)
                                ps = ps_mm.get()
                                mm(ps, ps.t[:, 0:QB], kn_h.t[:, kc_], qn_h.t[:, qc], True, False, [kn_h, qn_h])
                                mm(ps, ps.t[:, 0:QB], kr[0].t[:, kc_], qr_h[0].t[:, qc], False, False, [kr[0], qr_h[0]])
                                mm(ps, ps.t[:, 0:QB], kr[1].t[:, kc_], qr_h[1].t[:, qc], False, True, [kr[1], qr_h[1]])
                                pt = pt_p.get()
                                act(pt, pt.t[:, 0:QB], ps.t[:, 0:QB], AF.Exp, [ps], scale=ATT_SCALE)
                                pend.append((kt, pt))
                            if kt >= 2:
                                k0, pt0 = pend.pop(0)
                                mm(ACC0, ACC0.t[:, 0:QB], v_h.t[:, s * NKT + k0, :], pt0.t[:, 0:QB], k0 == 0, k0 == NKT - 1, [v_h, pt0])
                                mm(ACC1, ACC1.t[:, 0:QB], ones_bf.t[:, :], pt0.t[:, 0:QB], k0 == 0, k0 == NKT - 1, [ones_bf, pt0])
                        rd = w32.get()
                        recip(rd, rd.t[:, 0:QB], ACC1.t[:, 0:QB], [ACC1])
                        if is_p and l == 0 and hh == 0 and s == 0:
                            dump("rden", rd.t[:, 0:QB], [128, QB], F32, [rd])
                            on_ = w32.get()
                            act(on_, on_.t[:, 0:QB], ACC0.t[:, 0:QB], AF.Copy, [ACC0])
                            dump("onum", on_.t[:, 0:QB], [128, QB], F32, [on_])
                        so = stg_bf.get()
                        tt(so, so.t[:, 0:QB], ACC0.t[:, 0:QB], rd.t[:, 0:QB], ALU.mult, [ACC0, rd])
                        c.dma("sp", yT.t[2 * DM + hh * 128:2 * DM + (hh + 1) * 128, qc], so.t[:, 0:QB], reads=[so], writes=[yT])
            if DEBUG and is_p and l == 0:
                c.dma("sp", dbg_y, yT.t[:, 0:1024], reads=[yT], writes=[d_out])
            AR.reset()
            merged = AR.tile([128, 16, 1024], BF16)
            yb_p = Pool([AR.tile([128, 8, 1024], BF16) for _ in range(2)])
            gt_p = Pool([AR.tile([128, 1024], BF16) for _ in range(2)])
            t32 = Pool([AR.tile([128, 512], F32) for _ in range(3)])
            xl_p = Pool([AR.tile([128, 1024], F32) for _ in range(2)])
            fl_p = Pool([AR.tile([128, 1024], F32) for _ in range(2)])
            wA = Pool([AR.tile([128, 16, 512], BF16) for _ in range(2)])
            for sg in range(nseg):
                c0 = sg * 1024
                for b in range(3):
                    yb = yb_p.get()
                    c.dma("sp", yb.t[:, :, :], yT.t[b * DM:(b + 1) * DM, c0:c0 + 1024].rearrange("(k p) n -> p k n", p=128), reads=[yT], writes=[yb])
                    for blk in range(4):
                        w = wA.get()
                        load_w(w, 8, 512, w_branch[l, b, :, blk * 512:(blk + 1) * 512], wcC[l], b * 4 + blk, is_p)
                        for mi in range(4):
                            m = blk * 4 + mi
                            gt = gt_p.get()
                            r0 = R_GATE + b * D + m * 128
                            c.dma("sp", gt.t[:, :], zT.t[r0:r0 + 128, c0:c0 + 1024], reads=[zT], writes=[gt])
                            for tb in range(2):
                                tc = slice(tb * 512, (tb + 1) * 512)
                                ps = ps_mm.get()
                                for kc in range(8):
                                    mm(ps, ps.t[:, :], w.t[:, kc, mi * 128:(mi + 1) * 128], yb.t[:, kc, tc], kc == 0, kc == 7, [w, yb])
                                if b == 0:
                                    tt(merged, merged.t[:, m, tc], ps.t[:, :], gt.t[:, tc], ALU.mult, [ps, gt])
                                else:
                                    tq_ = t32.get()
                                    tt(tq_, tq_.t[:, :], ps.t[:, :], gt.t[:, tc], ALU.mult, [ps, gt])
                                    tt(merged, merged.t[:, m, tc], merged.t[:, m, tc], tq_.t[:, :], ALU.add, [merged, tq_])
                for blk in range(4):
                    w = wA.get()
                    load_w(w, 16, 512, w_out[l, :, blk * 512:(blk + 1) * 512], wcC[l], 12 + blk, is_p)
                    for mi in range(4):
                        m = blk * 4 + mi
                        so = stg_f.get()
                        for tb in range(2):
                            tc = slice(tb * 512, (tb + 1) * 512)
                            ps = ps_mm.get()
                            for kc in range(16):
                                mm(ps, ps.t[:, :], w.t[:, kc, mi * 128:(mi + 1) * 128], merged.t[:, kc, tc], kc == 0, kc == 15, [w, merged])
                            act(so, so.t[:, tc], ps.t[:, :], AF.Copy, [ps])
                            sq = sq_p.get()
                            act(sq, sq.t[:, :], ps.t[:, :], AF.Square, [ps])
                            A_ = ACC0 if tb == 0 else ACC1
                            mm(A_, A_.t[:, :], ones_bf.t[:, :], sq.t[:, :], m == 0, m == 15, [ones_bf, sq])
                        c.dma("sp", fbuf.t[m * 128:(m + 1) * 128, :], so.t[:, :], reads=[so], writes=[fbuf])
                rs = rstd_p.get()
                rstd_from_ssq(ACC0, ACC0.t[:, :], rs, rs.t[:, 0:512], 1.0 / D)
                rstd_from_ssq(ACC1, ACC1.t[:, :], rs, rs.t[:, 512:1024], 1.0 / D)
                for m in range(16):
                    xl = xl_p.get()
                    c.dma("sp", xl.t[:, :], xs_ap[m * 128:(m + 1) * 128, c0:c0 + 1024], reads=[xs_dep], writes=[xl])
                    fl = fl_p.get()
                    c.dma("sp", fl.t[:, :], fbuf.t[m * 128:(m + 1) * 128, :], reads=[fbuf], writes=[fl])
                    so = stg_f.get()
                    stt(so, so.t[:, :], fl.t[:, :], P.t[:, 2, m:m + 1], rs.t[:, :], ALU.mult, ALU.mult, [fl, rs] + A_reads)
                    tt(so, so.t[:, :], so.t[:, :], xl.t[:, :], ALU.add, [so, xl])
                    c.dma("sp", x1T.t[m * 128:(m + 1) * 128, c0:c0 + 1024], so.t[:, :], reads=[so], writes=[x1T])
            if DEBUG and is_p and l == 0:
                c.dma("sp", dbg_x1, x1T.t[:, 0:1024], reads=[x1T], writes=[d_out])
            AR.reset()
            NCOL = 520
            h2 = AR.tile([128, 16, NCOL], BF16)
            aT = AR.tile([128, 44, 512], BF16)
            xblk_pool = Pool([AR.tile([128, 16, 128], F32) for _ in range(1)])
            xblk_sq = Pool([AR.tile([128, 16, 128], BF16) for _ in range(1)])
            uv_p = Pool([AR.tile([128, NCOL], F32) for _ in range(2)])
            ug_p = Pool([AR.tile([128, NCOL], F32) for _ in range(2)])
            cv_p = Pool([AR.tile([128, 512], F32) for _ in range(4)])
            wD = Pool([AR.tile([128, 44, 128], BF16) for _ in range(2)])
            wU = Pool([AR.tile([128, 16, 256], BF16) for _ in range(4)])
            yo_ap, yo_dep = (x2T.t, x2T.d) if l == 0 else (G["yout"], d_out)
            nsegD = ntok // 512
            for sg in range(nsegD):
                c0 = sg * 512
                if is_p:
                    c.op("pool", lambda h, h2=h2: h.memset(h2.t[:, :, :], 0.0), writes=[h2])
                    for sq_i in range(2):
                        for q0 in range(0, 256, 128):
                            norm_mod(x1T.t, x1T.d, c0 + sq_i * 256 + q0, 128, h2, sq_i * 258 + 1 + q0, P.t[:, 3, :], P.t[:, 4, :], xblk_pool)
                    mmblocks = [(0, 258), (258, 258)]
                else:
                    lo = c0 - 1 if sg > 0 else c0
                    hi = c0 + 513 if sg < nsegD - 1 else c0 + 512
                    if sg == 0 or sg == nsegD - 1:
                        c.op("pool", lambda h, h2=h2: h.memset(h2.t[:, :, :], 0.0), writes=[h2])
                    q = lo
                    while q < hi:
                        n = min(128, hi - q)
                        norm_mod(x1T.t, x1T.d, q, n, h2, q - (c0 - 1), P.t[:, 3, :], P.t[:, 4, :], xblk_pool)
                        q += n
                    mmblocks = [(0, 512), (512, 2)]
                for hb in range(22):
                    wv_ = wU.get()
                    firstD = is_p and sg == 0
                    load_w(wv_, 16, 256, w_up[l, :, hb * 256:(hb + 1) * 256], wcU[l], hb, firstD)
                    wg_ = wU.get()
                    load_w(wg_, 16, 256, w_up[l, :, DFF + hb * 256:DFF + (hb + 1) * 256], wcU[l], 22 + hb, firstD)
                    for mi in range(2):
                        hm = hb * 2 + mi
                        uv, ug = uv_p.get(), ug_p.get()
                        for (wt, ut, eng) in ((wv_, uv, "act"), (wg_, ug, "dve")):
                            for (b0, bn) in mmblocks:
                                ps = ps_mm.get()
                                for kc in range(16):
                                    mm(ps, ps.t[:, 0:bn], wt.t[:, kc, mi * 128:(mi + 1) * 128], h2.t[:, kc, b0:b0 + bn], kc == 0, kc == 15, [wt, h2])
                                if eng == "act":
                                    act(ut, ut.t[:, b0:b0 + bn], ps.t[:, 0:bn], AF.Copy, [ps])
                                else:
                                    c.op("dve", lambda h, ut=ut, ps=ps, b0=b0, bn=bn: h.tensor_copy(ut.t[:, b0:b0 + bn], ps.t[:, 0:bn]), reads=[ps], writes=[ut])
                        cvs = []
                        for (ut, fm_) in ((uv, hm), (ug, 44 + hm)):
                            cv = cv_p.get()
                            w0, w1, w2 = (cvw[l].t[:, fm_, i:i + 1] for i in range(3))
                            bb = cvb[l].t[:, fm_:fm_ + 1]
                            if is_p:
                                u3 = ut.t[:, 0:516].rearrange("p (a b) -> p a b", b=258)
                                o3 = cv.t[:, :].rearrange("p (a b) -> p a b", b=256)
                                i0, i1, i2 = u3[:, :, 0:256], u3[:, :, 1:257], u3[:, :, 2:258]
                            else:
                                o3 = cv.t[:, :]
                                i0, i1, i2 = ut.t[:, 0:512], ut.t[:, 1:513], ut.t[:, 2:514]
                            ts(cv, o3, i1, w1, bb, ALU.mult, ALU.add, [ut, cvw[l], cvb[l]])
                            stt(cv, o3, i0, w0, o3, ALU.mult, ALU.add, [ut, cvw[l], cv])
                            stt(cv, o3, i2, w2, o3, ALU.mult, ALU.add, [ut, cvw[l], cv])
                            cvs.append(cv)
                        act(cvs[1], cvs[1].t[:, :], cvs[1].t[:, :], AF.Silu, [cvs[1]])
                        tt(aT, aT.t[:, hm, :], cvs[1].t[:, :], cvs[0].t[:, :], ALU.mult, [cvs[0], cvs[1]])
                for m in range(16):
                    w = wD.get()
                    load_w(w, 44, 128, w_down[l, :, m * 128:(m + 1) * 128], wcD[l], m, is_p and sg == 0)
                    so = stg_f.get()
                    ps = ps_mm.get()
                    for kc in range(44):
                        mm(ps, ps.t[:, :], w.t[:, kc, :], aT.t[:, kc, :], kc == 0, kc == 43, [w, aT])
                    act(so, so.t[:, 0:512], ps.t[:, :], AF.Copy, [ps])
                    sq = sq_p.get()
                    act(sq, sq.t[:, :], ps.t[:, :], AF.Square, [ps])
                    mm(ACC0, ACC0.t[:, :], ones_bf.t[:, :], sq.t[:, :], m == 0, m == 15, [ones_bf, sq])
                    c.dma("sp", fbuf.t[m * 128:(m + 1) * 128, 0:512], so.t[:, 0:512], reads=[so], writes=[fbuf])
                rs = rstd_p.get()
                rstd_from_ssq(ACC0, ACC0.t[:, :], rs, rs.t[:, 0:512], 1.0 / D)
                for m in range(16):
                    xl = cv_p.get()
                    c.dma("sp", xl.t[:, :], x1T.t[m * 128:(m + 1) * 128, c0:c0 + 512], reads=[x1T], writes=[xl])
                    fl = cv_p.get()
                    c.dma("sp", fl.t[:, :], fbuf.t[m * 128:(m + 1) * 128, 0:512], reads=[fbuf], writes=[fl])
                    so = stg_f.get()
                    stt(so, so.t[:, 0:512], fl.t[:, :], P.t[:, 5, m:m + 1], rs.t[:, 0:512], ALU.mult, ALU.mult, [fl, rs] + A_reads)
                    tt(so, so.t[:, 0:512], so.t[:, 0:512], xl.t[:, :], ALU.add, [so, xl])
                    c.dma("sp", yo_ap[m * 128:(m + 1) * 128, c0:c0 + 512], so.t[:, 0:512], reads=[so], writes=[yo_dep])
            if DEBUG and is_p and l == 0:
                c.dma("sp", dbg_x2, x2T.t[:, 0:1024], reads=[x2T], writes=[d_out])
    c.finish()
    return nc, c


def host_inputs(I, core):
    f32 = np.float32
    b = core % 2
    A = lambda x: np.ascontiguousarray(x, dtype=f32)
    m = {}
    m["xTp"] = A(I["x_prompt"][4 * core:4 * core + 4].reshape(1024, D).T)
    m["xTs"] = A(I["x_sample"][b].T)
    conds = np.stack([I["c_ctx"], I["c"][b]], axis=-1)
    m["condT"] = A(conds.reshape(16, 128, 2).transpose(1, 0, 2))
    m["ada_w"] = A(I["ada_w"])
    m["ada_bT"] = A(I["ada_b"].reshape(NL, 96, 128).transpose(0, 2, 1))
    m["norm_gT"] = A(I["norm_g"].reshape(NL, 4, 16, 128).transpose(0, 3, 1, 2))
    m["w_in"] = A(I["w_in"])
    kcols = 6400 + np.concatenate([np.arange(0, 64, 2), np.arange(1, 64, 2)])
    m["w_in_kr"] = A(I["w_in"][:, :, kcols])

    def qj(a):
        sh = a.shape[:-2]
        return a.reshape(sh + (32, 2, 64)).reshape(sh + (32, 128)).swapaxes(-1, -2)
    ldt = np.broadcast_to(I["s5_log_dt"][..., None], I["s5_lam_re"].shape)
    m["s5_lam"] = A(np.stack([qj(I["s5_lam_re"]), qj(I["s5_lam_im"]), qj(ldt)], axis=2))
    nb = np.zeros((NL, 2, 128, 32, 32), f32)
    cbk = np.zeros((NL, 2, 128, 32, 32), f32)
    for r, (bsrc, csrc) in enumerate(((I["s5_b_re"], I["s5_c_re"]), (I["s5_b_im"], I["s5_c_im"]))):
        bb = bsrc.reshape(NL, 32, 2, 64, 16)
        cc = csrc.reshape(NL, 32, 2, 16, 64)
        for g2 in range(2):
            nb[:, r, g2 * 64:(g2 + 1) * 64, :, g2 * 16:(g2 + 1) * 16] = bb[:, :, g2].transpose(0, 2, 1, 3)
            cbk[:, r, g2 * 64:(g2 + 1) * 64, :, g2 * 16:(g2 + 1) * 16] = cc[:, :, g2].transpose(0, 3, 1, 2)
    m["s5_nb"] = nb
    m["s5_cb"] = cbk
    m["s5_dT"] = A(I["s5_d"].reshape(NL, 32, 32).transpose(0, 2, 1))
    s0 = np.stack([I["state_s5_fwd"][b], I["state_s5_bwd"][b]], axis=1)
    m["s5_s0"] = A(s0.reshape(NL, 2, 32, 128, 2).transpose(0, 1, 3, 2, 4))
    m["glu_w"] = A(I["s5_glu_w"])
    m["glu_bT"] = A(I["s5_glu_b"].reshape(NL, 8, 128).transpose(0, 2, 1))
    m["ret_dec"] = A(np.broadcast_to(I["ret_decay"].reshape(NL, 1, 16), (NL, 128, 16)))
    m["ret_ngT"] = A(I["ret_norm_g"].reshape(NL, 8, 128).transpose(0, 2, 1))
    m["ret_s0"] = A(np.stack([I["state_ret_fwd"][b], I["state_ret_bwd"][b]], axis=1))
    jj = np.arange(128)[:, None].astype(f32)
    ii = np.arange(128)[None, :].astype(f32)
    tab = np.stack([np.maximum(ii - jj, 0), (ii >= jj).astype(f32), np.maximum(jj - ii, 0), (jj >= ii).astype(f32),
                    np.broadcast_to(ii + 1, (128, 128)), np.broadcast_to(128 - ii, (128, 128))], axis=1)
    m["ret_tab"] = A(tab)
    pp = np.arange(128).astype(f32)
    m["ret_col"] = A(np.stack([127 - pp, pp, np.full(128, 128.0, f32)], axis=1))
    m["q_normT"] = A(I["mla_q_norm"].reshape(NL, 6, 128).transpose(0, 2, 1))
    m["kv_normT"] = A(I["mla_kv_norm"].reshape(NL, 4, 128).transpose(0, 2, 1))
    m["kv_norm_rep"] = A(np.broadcast_to(I["mla_kv_norm"][:, None, :], (NL, 128, 512)))
    hcols = np.concatenate([np.arange(128), 128 + np.arange(0, 64, 2), 128 + np.arange(1, 64, 2)])
    allc = np.concatenate([h * 192 + hcols for h in range(8)])
    m["w_uq"] = A(I["mla_w_uq"][:, :, allc])
    m["w_uk"] = A(I["mla_w_uk"])
    m["w_uv"] = A(I["mla_w_uv"])
    t = np.arange(LS)
    row = (t // 64).astype(f32)
    col = (t % 64).astype(f32)
    inv = (np.float32(10000.0) ** (-np.arange(16, dtype=f32) / np.float32(16))).astype(f32)
    ang = np.concatenate([row[:, None] * inv, col[:, None] * inv], axis=-1).astype(f32)
    m["rope_cs"] = A(np.stack([np.cos(ang).T, np.sin(ang).T]))
    m["cache_ckvT"] = A(I["cache_mla_ckv"][b].transpose(0, 2, 1))
    ck = I["cache_mla_krope"][b]
    m["cache_kr"] = A(np.stack([ck[:, :, 0::2], ck[:, :, 1::2]], axis=1).transpose(0, 1, 3, 2))
    m["w_branch"] = A(I["w_branch"])
    m["w_out"] = A(I["w_out"])
    m["w_up"] = A(I["ffn_w_up"])
    m["conv_wT"] = A(I["ffn_conv_w"].reshape(NL, 3, 88, 128).transpose(0, 3, 2, 1))
    m["conv_bT"] = A(I["ffn_conv_b"].reshape(NL, 88, 128).transpose(0, 2, 1))
    m["w_down"] = A(I["ffn_w_down"])
    return m


_CACHE = {}


def kernel(**inputs):
    I = {k: np.asarray(v) for k, v in inputs.items()}
    if "nc" not in _CACHE:
        _CACHE["nc"] = build()[0]
    nc = _CACHE["nc"]
    in_maps = [host_inputs(I, core) for core in range(8)]
    res = run_bass_kernel_spmd(nc, in_maps, core_ids=list(range(8)))
    R = res.results
    _CACHE["last"] = R
    y_prompt = np.concatenate([R[cidx]["yTp"].T.reshape(4, LP, D) for cidx in range(8)], axis=0)
    y_sample = np.stack([R[0]["yTs"].T, R[1]["yTs"].T], axis=0)
    ckv = np.concatenate([R[cidx]["ckv_o"] for cidx in range(8)], axis=0)
    krp = np.concatenate([R[cidx]["kr_o"] for cidx in range(8)], axis=0)
    retf = np.concatenate([R[cidx]["ret_o"][0] for cidx in range(8)], axis=0)
    retb = np.concatenate([R[cidx]["ret_o"][1] for cidx in range(8)], axis=0)

    def s5fix(a):
        a = a.transpose(0, 1, 4, 3, 2)
        return np.ascontiguousarray(a.reshape(a.shape[0], NL, 64, 64, 2))
    s5f = np.concatenate([s5fix(R[cidx]["s5_o"][0]) for cidx in range(8)], axis=0)
    s5b = np.concatenate([s5fix(R[cidx]["s5_o"][1]) for cidx in range(8)], axis=0)
    f = lambda a: np.ascontiguousarray(a, dtype=np.float32)
    return (f(y_prompt), f(y_sample), f(ckv), f(krp), f(retf), f(retb), f(s5f), f(s5b))
```

```python
import math
import numpy as np
import concourse.bass as bass
import concourse.mybir as mybir
from concourse.bass_utils import run_bass_kernel_spmd

F32 = mybir.dt.float32
BF16 = mybir.dt.bfloat16
AF = mybir.ActivationFunctionType
ALU = mybir.AluOpType

RING = 8
DEBUG = False
EPS = 1e-6
D = 2048
DM = 1024
NL = 2
LP = 256
NPS = 4
LS = 4096
PAST = 256
DFF = 5632
ZROWS = 11584
R_U, R_Q, R_K, R_G, R_CQ, R_CKV, R_KR, R_GATE = 0, 1024, 2048, 3072, 4096, 4864, 5376, 5440
ATT_SCALE = (128 + 64) ** -0.5
RET_SCALE = 128 ** -0.5
MAGIC = 12582912.0
TWO_PI = 2.0 * math.pi


class Dep:
    __slots__ = ("w", "r")

    def __init__(self):
        self.w = {}
        self.r = {}


class T:
    __slots__ = ("t", "d")

    def __init__(self, t, d=None):
        self.t = t
        self.d = d if d is not None else Dep()


class Ctx:
    def __init__(self, nc):
        self.nc = nc
        self.enames = ["pe", "act", "dve", "pool", "sp"]
        self.ops = {e: [] for e in self.enames}
        self.cnt = {e: 0 for e in self.enames}
        self.sem = {e: nc.alloc_semaphore("s_" + e) for e in self.enames}
        self.waited = {e: {} for e in self.enames}
        self.ring = {q: [nc.alloc_semaphore("r_%s%d" % (q, i)) for i in range(RING)] for q in ("sp", "pool", "act")}
        self.dman = {q: 0 for q in self.ring}
        self.nalloc = 0
        self.ninstr = 0

    def sb(self, shape, dt, name=None):
        self.nalloc += 1
        return T(self.nc.alloc_sbuf_tensor(name or ("sb%d" % self.nalloc), list(shape), dt))

    def semof(self, key):
        if key[0] == "E":
            return self.sem[key[1]]
        return self.ring[key[1]][key[2]]

    def _collect(self, e, reads, writes, extra=None):
        need = {}
        toks = []
        for d in reads:
            toks += list(d.w.items())
        for d in writes:
            toks += list(d.w.items())
            toks += list(d.r.items())
        if extra:
            toks += extra
        for key, (val, src) in toks:
            if src == "pe" and e == "pe" and key[0] == "E":
                continue
            if self.waited[e].get(key, 0) >= val:
                continue
            if need.get(key, 0) < val:
                need[key] = val
        for key, val in need.items():
            self.waited[e][key] = val
        return [(self.semof(k), v) for k, v in need.items()]

    def op(self, e, fn, reads=(), writes=()):
        reads = [x.d if isinstance(x, T) else x for x in reads]
        writes = [x.d if isinstance(x, T) else x for x in writes]
        wl = self._collect(e, reads, writes)
        self.cnt[e] += 1
        n = self.cnt[e]
        key = ("E", e)
        sem = self.sem[e]

        def emit(h):
            for sm, v in wl:
                h.wait_ge(sm, v)
            fn(h).then_inc(sem, 1)
        self.ops[e].append(emit)
        self.ninstr += 1 + len(wl)
        for d in reads:
            d.r[key] = (n, e)
        for d in writes:
            d.w[key] = (n, e)

    def dma(self, q, out, in_, reads=(), writes=(), **kw):
        reads = [x.d if isinstance(x, T) else x for x in reads]
        writes = [x.d if isinstance(x, T) else x for x in writes]
        i = self.dman[q]
        self.dman[q] += 1
        ri = i % RING
        val = 16 * (i // RING + 1)
        prev = 16 * (i // RING)
        key = ("D", q, ri)
        extra = [(key, (prev, None))] if prev > 0 else None
        wl = self._collect(q, reads, writes, extra)
        sem = self.ring[q][ri]

        def emit(h):
            for sm, v in wl:
                h.wait_ge(sm, v)
            h.dma_start(out=out, in_=in_, **kw).then_inc(sem, 16)
        self.ops[q].append(emit)
        self.ninstr += 1 + len(wl)
        for d in reads:
            d.r[key] = (val, None)
        for d in writes:
            d.w[key] = (val, None)

    def finish(self):
        finals = []
        for q in self.ring:
            n = self.dman[q]
            for ri in range(RING):
                cntr = (n - ri + RING - 1) // RING if n > ri else 0
                if cntr > 0:
                    finals.append((self.ring[q][ri], 16 * cntr))
        efinal = [(self.sem[e], self.cnt[e]) for e in self.enames if self.cnt[e] > 0]
        ops = self.ops
        with self.nc.Block() as block:
            @block.tensor
            def _(h):
                for f in ops["pe"]:
                    f(h)

            @block.scalar
            def _(h):
                for f in ops["act"]:
                    f(h)

            @block.vector
            def _(h):
                for f in ops["dve"]:
                    f(h)

            @block.gpsimd
            def _(h):
                for f in ops["pool"]:
                    f(h)

            @block.sync
            def _(h):
                for f in ops["sp"]:
                    f(h)
                for sm, v in efinal:
                    h.wait_ge(sm, v)
                for sm, v in finals:
                    h.wait_ge(sm, v)


class Pool:
    def __init__(self, tiles):
        self.tiles = tiles
        self.i = 0

    def get(self):
        t = self.tiles[self.i % len(self.tiles)]
        self.i += 1
        return t


def pat(a):
    return a.tensor, a.offset, [list(x) for x in a.ap]


def bc_mid(a, n):
    t, o, p = pat(a)
    return bass.AP(t, o, [p[0], [0, n]] + p[1:])


def bc_last(a, n):
    t, o, p = pat(a)
    return bass.AP(t, o, p + [[0, n]])


def rev(a):
    t, o, p = pat(a)
    st, n = p[1]
    return bass.AP(t, o + st * (n - 1), [p[0], [-st, n]])


def build(debug_stop=None):
    nc = bass.Bass("TRN2", target_bir_lowering=False)
    c = Ctx(nc)

    def din(name, shape, dt=F32):
        return nc.dram_tensor(name, list(shape), dt, kind="ExternalInput").ap()

    def dout(name, shape, dt=F32):
        return nc.dram_tensor(name, list(shape), dt, kind="ExternalOutput").ap()

    def dscr(name, shape, dt):
        return T(nc.dram_tensor(name, list(shape), dt, kind="Internal").ap())

    xTp = din("xTp", [D, 1024])
    xTs = din("xTs", [D, LS])
    condT = din("condT", [128, 16, 2])
    ada_w = din("ada_w", [NL, D, 6 * D])
    ada_bT = din("ada_bT", [NL, 128, 96])
    norm_gT = din("norm_gT", [NL, 128, 4, 16])
    w_in = din("w_in", [NL, D, 12608])
    w_in_kr = din("w_in_kr", [NL, D, 64])
    s5_lam = din("s5_lam", [NL, 2, 3, 128, 32])
    s5_nb = din("s5_nb", [NL, 2, 128, 32, 32])
    s5_cb = din("s5_cb", [NL, 2, 128, 32, 32])
    s5_dT = din("s5_dT", [NL, 32, 32])
    s5_s0 = din("s5_s0", [NL, 2, 128, 32, 2])
    glu_w = din("glu_w", [NL, DM, DM])
    glu_bT = din("glu_bT", [NL, 128, 8])
    ret_dec = din("ret_dec", [NL, 128, 16])
    ret_ngT = din("ret_ngT", [NL, 128, 8])
    ret_s0 = din("ret_s0", [NL, 2, 8, 128, 128])
    ret_tab = din("ret_tab", [128, 6, 128])
    ret_col = din("ret_col", [128, 3])
    q_normT = din("q_normT", [NL, 128, 6])
    kv_normT = din("kv_normT", [NL, 128, 4])
    kv_norm_rep = din("kv_norm_rep", [NL, 128, 512])
    w_uq = din("w_uq", [NL, 768, 1536])
    w_uk = din("w_uk", [NL, 512, 1024])
    w_uv = din("w_uv", [NL, 512, 1024])
    rope_cs = din("rope_cs", [2, 32, LS])
    cache_ckvT = din("cache_ckvT", [NL, 512, PAST])
    cache_kr = din("cache_kr", [NL, 2, 32, PAST])
    w_branch = din("w_branch", [NL, 3, DM, D])
    w_out = din("w_out", [NL, D, D])
    w_up = din("w_up", [NL, D, 2 * DFF])
    conv_wT = din("conv_wT", [NL, 128, 88, 3])
    conv_bT = din("conv_bT", [NL, 128, 88])
    w_down = din("w_down", [NL, DFF, D])
    yTp = dout("yTp", [D, 1024])
    yTs = dout("yTs", [D, LS])
    ckv_o = dout("ckv_o", [NPS, NL, LP, 512])
    kr_o = dout("kr_o", [NPS, NL, LP, 64])
    ret_o = dout("ret_o", [2, NPS, NL, 8, 128, 128])
    s5_o = dout("s5_o", [2, NPS, NL, 2, 128, 32])
    if DEBUG:
        dbg_y = dout("dbg_y", [3 * DM, 1024], BF16)
        dbg_x1 = dout("dbg_x1", [D, 1024])
        dbg_x2 = dout("dbg_x2", [D, 1024])
    zT = dscr("zT", [ZROWS, LS], BF16)
    ktok = dscr("ktok", [LS, DM], BF16)
    vtok = dscr("vtok", [LS, DM], BF16)
    ygT = dscr("ygT", [DM, LS], BF16)
    yT = dscr("yT", [3 * DM, LS], BF16)
    x1b = dscr("x1b", [D, LS], F32)
    x2b = dscr("x2b", [D, LS], F32)
    fbuf = dscr("fbuf", [D, 1024], F32)
    qscr = dscr("qscr", [8, 192, LS], BF16)
    d_xTp, d_xTs = Dep(), Dep()
    d_out = Dep()

    wcA = [dscr("wcA%d" % l, [32, 128, 16 * 512], BF16) for l in range(NL)]
    wcC = [dscr("wcC%d" % l, [16, 128, 16 * 512], BF16) for l in range(NL)]
    wcU = [dscr("wcU%d" % l, [44, 128, 16 * 256], BF16) for l in range(NL)]
    wcD = [dscr("wcD%d" % l, [16, 128, 44 * 128], BF16) for l in range(NL)]

    def load_w(w, nk, ncols, src_ap, cache, bi, first):
        cview = cache.t[bi, :, 0:nk * ncols].rearrange("p (k n) -> p k n", k=nk)
        if first:
            c.dma("pool", w.t[:, 0:nk, 0:ncols], src_ap.rearrange("(k p) n -> p k n", p=128), writes=[w])
            c.dma("sp", cview, w.t[:, 0:nk, 0:ncols], reads=[w], writes=[cache])
        else:
            c.dma("pool", w.t[:, 0:nk, 0:ncols], cview, reads=[cache], writes=[w])

    def dump(name, ap, shape, dt, reads):
        if DEBUG:
            o_ = dout("dbg_" + name, shape, dt)
            c.dma("sp", o_, ap, reads=reads, writes=[d_out])

    PS = [T(nc.alloc_psum_tensor("psb%d" % i, [128, 512], F32)) for i in range(8)]
    ps_mm = Pool(PS[0:4])
    ps_aux = Pool(PS[6:8])
    ACC0, ACC1 = PS[4], PS[5]

    ident = c.sb([128, 128], F32, "ident")
    c.op("pool", lambda h: h.memset(ident.t[:, :], 0.0), writes=[ident])
    c.op("pool", lambda h: h.affine_select(ident.t[:, :], ident.t[:, :], pattern=[[-1, 128]], compare_op=ALU.not_equal,
                                           fill=1.0, base=0, channel_multiplier=1), reads=[ident], writes=[ident])
    ones_bf = c.sb([128, 128], BF16, "ones_bf")
    c.op("pool", lambda h: h.memset(ones_bf.t[:, :], 1.0), writes=[ones_bf])
    ones128 = c.sb([128, 128], BF16, "ones128")
    c.op("pool", lambda h: h.memset(ones128.t[:, :], 1.0 / 128.0), writes=[ones128])
    eps_t = c.sb([128, 1], F32, "eps_t")
    c.op("pool", lambda h: h.memset(eps_t.t[:, :], EPS), writes=[eps_t])

    def barrier():
        toks = [(("E", e), (c.cnt[e], e)) for e in c.enames if c.cnt[e] > 0]
        for q in c.ring:
            n = c.dman[q]
            for ri in range(RING):
                cntr = (n - ri + RING - 1) // RING if n > ri else 0
                if cntr > 0:
                    toks.append((("D", q, ri), (16 * cntr, None)))
        d = Dep()
        for k, v in toks:
            d.w[k] = v
        for e in c.enames:
            if e in ("sp",):
                wl = c._collect(e, [d], [])

                def emit(h, wl=wl):
                    for sm, v in wl:
                        h.wait_ge(sm, v)
                c.ops[e].append(emit)
            elif e == "pe":
                wl = c._collect(e, [d], [])

                def emit(h, wl=wl):
                    for sm, v in wl:
                        h.wait_ge(sm, v)
                c.ops[e].append(emit)
            else:
                wl = c._collect(e, [d], [])

                def emit(h, wl=wl):
                    for sm, v in wl:
                        h.wait_ge(sm, v)
                c.ops[e].append(emit)

    def mm(ps, ps_ap, lhsT, rhs, start, stop, reads):
        c.op("pe", lambda h: h.matmul(ps_ap, lhsT, rhs, start=start, stop=stop), reads=reads, writes=[ps])

    def act(out_t, out_ap, in_ap, func, reads, bias=None, scale=None):
        kw = {}
        if bias is not None:
            kw["bias"] = bias
        if scale is not None:
            kw["scale"] = scale
        c.op("act", lambda h: h.activation(out_ap, in_ap, func, **kw), reads=reads, writes=[out_t])

    def tt(out_t, out_ap, a, b, op, reads, eng="dve"):
        c.op(eng, lambda h: h.tensor_tensor(out_ap, a, b, op), reads=reads, writes=[out_t])

    def ts(out_t, out_ap, a, s1, s2, op0, op1, reads, eng="dve"):
        if op1 is None:
            c.op(eng, lambda h: h.tensor_scalar(out_ap, a, s1, None, op0), reads=reads, writes=[out_t])
        else:
            c.op(eng, lambda h: h.tensor_scalar(out_ap, a, s1, s2, op0, op1), reads=reads, writes=[out_t])

    def stt(out_t, out_ap, a, s, b, op0, op1, reads):
        c.op("dve", lambda h: h.scalar_tensor_tensor(out_ap, a, s, b, op0, op1), reads=reads, writes=[out_t])

    def recip(out_t, out_ap, a, reads):
        c.op("dve", lambda h: h.reciprocal(out_ap, a), reads=reads, writes=[out_t])

    def rstd_from_ssq(ps, ps_ap, out_t, out_ap, inv_n):
        act(out_t, out_ap, ps_ap, AF.Sqrt, [ps, eps_t], bias=eps_t.t[:, 0:1], scale=inv_n)
        recip(out_t, out_ap, out_ap, [out_t])

    ARENA_BYTES = 146 * 1024
    arena = nc.alloc_sbuf_tensor("arena", [128, ARENA_BYTES], mybir.dt.uint8)

    class Arena:
        def __init__(self):
            self.off = 0

        def reset(self):
            barrier()
            self.off = 0

        def tile(self, shape, dt):
            nb = int(np.prod(shape[1:])) * mybir.dt.size(dt)
            nb_al = (nb + 31) // 32 * 32
            assert self.off + nb_al <= ARENA_BYTES, ("arena overflow", self.off, nb_al)
            P = shape[0]
            v = arena[0:P, self.off:self.off + nb].bitcast(dt)
            self.off += nb_al
            if len(shape) == 3:
                v = v.rearrange("p (a b) -> p a b", a=shape[1])
            elif len(shape) == 4:
                v = v.rearrange("p (a b c) -> p a b c", a=shape[1], b=shape[2])
            return T(v)

    AR = Arena()
    stg_bf = Pool([c.sb([128, 1024], BF16, "stgbf%d" % i) for i in range(3)])
    stg_f = Pool([c.sb([128, 1024], F32, "stgf%d" % i) for i in range(2)])
    sq_p = Pool([c.sb([128, 512], BF16, "sqp%d" % i) for i in range(3)])
    rstd_p = Pool([c.sb([128, 1024], F32, "rstd%d" % i) for i in range(2)])

    cond_f = c.sb([128, 16, 2], F32, "cond_f")
    cond_b = c.sb([128, 16, 2], BF16, "cond_b")
    c.dma("sp", cond_f.t[:, :, :], condT, writes=[cond_f])
    act(cond_b, cond_b.t[:, :, :], cond_f.t[:, :, :], AF.Silu, [cond_f])
    mods = [c.sb([128, 96, 2], F32, "mods%d" % l) for l in range(NL)]
    adab = [c.sb([128, 96], F32, "adab%d" % l) for l in range(NL)]
    ngs = [c.sb([128, 4, 16], F32, "ng%d" % l) for l in range(NL)]
    PR = [[c.sb([128, 6, 16], F32, "pr%d_%d" % (l, j)) for j in range(2)] for l in range(NL)]
    AR.reset()
    wA = Pool([AR.tile([128, 16, 512], BF16) for i in range(2)])
    for l in range(NL):
        c.dma("sp", adab[l].t[:, :], ada_bT[l], writes=[adab[l]])
        c.dma("sp", ngs[l].t[:, :, :], norm_gT[l], writes=[ngs[l]])
        for blk in range(24):
            w = wA.get()
            c.dma("pool", w.t[:, :, :], ada_w[l, :, blk * 512:(blk + 1) * 512].rearrange("(k p) n -> p k n", p=128), writes=[w])
            ps = ps_mm.get()
            for mi in range(4):
                for kc in range(16):
                    mm(ps, ps.t[:, mi * 2:mi * 2 + 2], w.t[:, kc, mi * 128:(mi + 1) * 128], cond_b.t[:, kc, :], kc == 0, kc == 15, [w, cond_b])
            for mi in range(4):
                m = blk * 4 + mi
                act(mods[l], mods[l].t[:, m, :], ps.t[:, mi * 2:mi * 2 + 2], AF.Identity, [ps, adab[l]], bias=adab[l].t[:, m:m + 1])
        for j in range(2):
            P = PR[l][j]
            M = mods[l]
            stt(P, P.t[:, 0, :], M.t[:, 16:32, j], 1.0, ngs[l].t[:, 0, :], ALU.add, ALU.mult, [M, ngs[l]])
            c.op("dve", lambda h, P=P, M=M, j=j: h.tensor_copy(P.t[:, 1, :], M.t[:, 0:16, j]), reads=[M], writes=[P])
            tt(P, P.t[:, 2, :], M.t[:, 32:48, j], ngs[l].t[:, 1, :], ALU.mult, [M, ngs[l]])
            stt(P, P.t[:, 3, :], M.t[:, 64:80, j], 1.0, ngs[l].t[:, 2, :], ALU.add, ALU.mult, [M, ngs[l]])
            c.op("dve", lambda h, P=P, M=M, j=j: h.tensor_copy(P.t[:, 4, :], M.t[:, 48:64, j]), reads=[M], writes=[P])
            tt(P, P.t[:, 5, :], M.t[:, 80:96, j], ngs[l].t[:, 3, :], ALU.mult, [M, ngs[l]])

    s5_mag = [[c.sb([128, 32], F32, "s5mag%d_%d" % (l, d)) for d in range(2)] for l in range(NL)]
    s5_dc = [[c.sb([128, 13, 32], F32, "s5dc%d_%d" % (l, d)) for d in range(2)] for l in range(NL)]
    s5_ds = [[c.sb([128, 13, 32], F32, "s5ds%d_%d" % (l, d)) for d in range(2)] for l in range(NL)]
    AR.reset()
    wb_d = dscr("wb_d", [NL, 2, 2, 32, 32, 128], BF16)
    wc_d = dscr("wc_d", [NL, 2, 128, 32, 32], BF16)
    wb_stage = Pool([AR.tile([32, 32, 128], BF16) for _ in range(2)])
    wc_stage = Pool([AR.tile([128, 32, 32], BF16) for _ in range(2)])
    s5_dsk = [c.sb([32, 32], F32, "s5dsk%d" % l) for l in range(NL)]
    s5_init = [[c.sb([128, 32, 2], F32, "s5init%d_%d" % (l, d)) for d in range(2)] for l in range(NL)]
    tmpP = Pool([AR.tile([128, 32], F32) for i in range(64)])
    nbig = Pool([AR.tile([128, 32, 32], F32) for i in range(12)])

    def range_reduce_sin(out_t, x_t, shift):
        a = tmpP.get()
        ts(a, a.t[:, :], x_t.t[:, :], shift, None, ALU.add, None, [x_t])
        n = tmpP.get()
        ts(n, n.t[:, :], a.t[:, :], 1.0 / TWO_PI, MAGIC, ALU.mult, ALU.add, [a])
        ts(n, n.t[:, :], n.t[:, :], MAGIC, None, ALU.subtract, None, [n])
        stt(a, a.t[:, :], n.t[:, :], -TWO_PI, a.t[:, :], ALU.mult, ALU.add, [n, a])
        ts(a, a.t[:, :], a.t[:, :], math.pi, -math.pi, ALU.min, ALU.max, [a])
        act(out_t, out_t.t[:, :], a.t[:, :], AF.Sin, [a])

    for l in range(NL):
        c.dma("sp", s5_dsk[l].t[:, :], s5_dT[l], writes=[s5_dsk[l]])
        nbr, nbi = nbig.get(), nbig.get()
        c.dma("sp", nbr.t[:, :, :], s5_nb[l, 0], writes=[nbr])
        c.dma("sp", nbi.t[:, :, :], s5_nb[l, 1], writes=[nbi])
        cr_, ci_ = nbig.get(), nbig.get()
        c.dma("sp", cr_.t[:, :, :], s5_cb[l, 0], writes=[cr_])
        c.dma("sp", ci_.t[:, :, :], s5_cb[l, 1], writes=[ci_])
        for r, (srcc, scl) in enumerate(((cr_, 1.0), (ci_, -1.0))):
            wcs = wc_stage.get()
            act(wcs, wcs.t[:, :, :], srcc.t[:, :, :], AF.Copy, [srcc], scale=scl)
            c.dma("sp", wc_d.t[l, r], wcs.t[:, :, :], reads=[wcs], writes=[wc_d])
        for d in range(2):
            c.dma("sp", s5_init[l][d].t[:, :, :], s5_s0[l, d], writes=[s5_init[l][d]])
            lr, li, ldt = tmpP.get(), tmpP.get(), tmpP.get()
            c.dma("sp", lr.t[:, :], s5_lam[l, d, 0], writes=[lr])
            c.dma("sp", li.t[:, :], s5_lam[l, d, 1], writes=[li])
            c.dma("sp", ldt.t[:, :], s5_lam[l, d, 2], writes=[ldt])
            dt_ = tmpP.get()
            act(dt_, dt_.t[:, :], ldt.t[:, :], AF.Exp, [ldt])
            ar, ai = tmpP.get(), tmpP.get()
            tt(ar, ar.t[:, :], lr.t[:, :], dt_.t[:, :], ALU.mult, [lr, dt_])
            tt(ai, ai.t[:, :], li.t[:, :], dt_.t[:, :], ALU.mult, [li, dt_])
            mag = s5_mag[l][d]
            act(mag, mag.t[:, :], ar.t[:, :], AF.Exp, [ar])
            dc, ds = s5_dc[l][d], s5_ds[l][d]
            cs, sn = tmpP.get(), tmpP.get()
            range_reduce_sin(cs, ai, math.pi / 2)
            range_reduce_sin(sn, ai, 0.0)
            c.op("dve", lambda h, dc=dc, cs=cs: h.tensor_copy(dc.t[:, 0, :], cs.t[:, :]), reads=[cs], writes=[dc])
            c.op("dve", lambda h, ds=ds, sn=sn: h.tensor_copy(ds.t[:, 0, :], sn.t[:, :]), reads=[sn], writes=[ds])
            for k in range(12):
                t1, t2 = tmpP.get(), tmpP.get()
                tt(t1, t1.t[:, :], dc.t[:, k, :], dc.t[:, k, :], ALU.mult, [dc])
                tt(t2, t2.t[:, :], ds.t[:, k, :], ds.t[:, k, :], ALU.mult, [ds])
                tt(dc, dc.t[:, k + 1, :], t1.t[:, :], t2.t[:, :], ALU.subtract, [t1, t2])
                stt(ds, ds.t[:, k + 1, :], dc.t[:, k, :], 2.0, ds.t[:, k, :], ALU.mult, ALU.mult, [dc, ds])
            abr, abi = tmpP.get(), tmpP.get()
            tt(abr, abr.t[:, :], mag.t[:, :], cs.t[:, :], ALU.mult, [mag, cs])
            tt(abi, abi.t[:, :], mag.t[:, :], sn.t[:, :], ALU.mult, [mag, sn])
            ts(abr, abr.t[:, :], abr.t[:, :], -1.0, None, ALU.add, None, [abr])
            den, t1, t2 = tmpP.get(), tmpP.get(), tmpP.get()
            tt(den, den.t[:, :], lr.t[:, :], lr.t[:, :], ALU.mult, [lr])
            tt(t1, t1.t[:, :], li.t[:, :], li.t[:, :], ALU.mult, [li])
            tt(den, den.t[:, :], den.t[:, :], t1.t[:, :], ALU.add, [den, t1])
            recip(den, den.t[:, :], den.t[:, :], [den])
            cre, cim = tmpP.get(), tmpP.get()
            tt(t1, t1.t[:, :], abr.t[:, :], lr.t[:, :], ALU.mult, [abr, lr])
            tt(t2, t2.t[:, :], abi.t[:, :], li.t[:, :], ALU.mult, [abi, li])
            tt(t1, t1.t[:, :], t1.t[:, :], t2.t[:, :], ALU.add, [t1, t2])
            tt(cre, cre.t[:, :], t1.t[:, :], den.t[:, :], ALU.mult, [t1, den])
            tt(t1, t1.t[:, :], abi.t[:, :], lr.t[:, :], ALU.mult, [abi, lr])
            tt(t2, t2.t[:, :], abr.t[:, :], li.t[:, :], ALU.mult, [abr, li])
            tt(t1, t1.t[:, :], t1.t[:, :], t2.t[:, :], ALU.subtract, [t1, t2])
            tt(cim, cim.t[:, :], t1.t[:, :], den.t[:, :], ALU.mult, [t1, den])
            bpr, bpi = nbig.get(), nbig.get()
            cre_b, cim_b = bc_last(cre.t[:, :], 32), bc_last(cim.t[:, :], 32)
            tt(bpr, bpr.t[:, :, :], nbr.t[:, :, :], cre_b, ALU.mult, [nbr, cre])
            tt(bpi, bpi.t[:, :, :], nbi.t[:, :, :], cim_b, ALU.mult, [nbi, cim])
            tt(bpr, bpr.t[:, :, :], bpr.t[:, :, :], bpi.t[:, :, :], ALU.subtract, [bpr, bpi])
            tt(bpi, bpi.t[:, :, :], nbi.t[:, :, :], cre_b, ALU.mult, [nbi, cre])
            t3 = nbig.get()
            tt(t3, t3.t[:, :, :], nbr.t[:, :, :], cim_b, ALU.mult, [nbr, cim])
            tt(bpi, bpi.t[:, :, :], bpi.t[:, :, :], t3.t[:, :, :], ALU.add, [bpi, t3])
            for r, src in ((0, bpr), (1, bpi)):
                wb = wb_stage.get()
                for j4 in range(8):
                    ps = ps_mm.get()
                    for jj in range(4):
                        j = j4 * 4 + jj
                        c.op("pe", lambda h, ps=ps, src=src, j=j, jj=jj: h.transpose(ps.t[0:32, jj * 128:(jj + 1) * 128], src.t[:, j, :], ident.t[:, :]),
                             reads=[src, ident], writes=[ps])
                    act(wb, wb.t[:, j4 * 4:(j4 + 1) * 4, :], ps.t[0:32, :].rearrange("p (a b) -> p a b", a=4), AF.Copy, [ps])
                c.dma("sp", wb_d.t[l, d, r], wb.t[:, :, :], reads=[wb], writes=[wb_d])

    rtab = c.sb([128, 6, 128], F32, "rtab")
    rcol = c.sb([128, 3], F32, "rcol")
    c.dma("sp", rtab.t[:, :, :], ret_tab, writes=[rtab])
    c.dma("sp", rcol.t[:, :], ret_col, writes=[rcol])
    r_lg = [c.sb([128, 16], F32, "rlg%d" % l) for l in range(NL)]
    r_ng = [c.sb([128, 8], F32, "rng%d" % l) for l in range(NL)]
    for l in range(NL):
        c.dma("sp", r_lg[l].t[:, :], ret_dec[l], writes=[r_lg[l]])
        c.dma("sp", r_ng[l].t[:, :], ret_ngT[l], writes=[r_ng[l]])
        act(r_lg[l], r_lg[l].t[:, :], r_lg[l].t[:, :], AF.Exp, [r_lg[l]])
        ts(r_lg[l], r_lg[l].t[:, :], r_lg[l].t[:, :], -1.0, None, ALU.mult, None, [r_lg[l]])

    def ret_tables(l):
        DT = AR.tile([128, 8, 128], F32)
        qp = AR.tile([128, 2, 8, 128], F32)
        kp = AR.tile([128, 2, 8], F32)
        dsc = AR.tile([128, 2, 8], F32)
        e1 = AR.tile([128, 128], F32)
        e2 = AR.tile([128, 128], F32)
        for hh in range(8):
            lgf = r_lg[l].t[:, hh:hh + 1]
            lgb = r_lg[l].t[:, 8 + hh:9 + hh]
            act(e1, e1.t[:, :], rtab.t[:, 0, :], AF.Exp, [rtab, r_lg[l]], scale=lgf)
            tt(e1, e1.t[:, :], e1.t[:, :], rtab.t[:, 1, :], ALU.mult, [e1, rtab])
            act(e2, e2.t[:, :], rtab.t[:, 2, :], AF.Exp, [rtab, r_lg[l]], scale=lgb)
            tt(e2, e2.t[:, :], e2.t[:, :], rtab.t[:, 3, :], ALU.mult, [e2, rtab])
            tt(e1, e1.t[:, :], e1.t[:, :], e2.t[:, :], ALU.add, [e1, e2])
            ts(DT, DT.t[:, hh, :], e1.t[:, :], RET_SCALE, None, ALU.mult, None, [e1])
            act(qp, qp.t[:, 0, hh, :], rtab.t[:, 4, :], AF.Exp, [rtab, r_lg[l]], scale=lgf)
            act(qp, qp.t[:, 1, hh, :], rtab.t[:, 5, :], AF.Exp, [rtab, r_lg[l]], scale=lgb)
            act(kp, kp.t[:, 0, hh:hh + 1], rcol.t[:, 0:1], AF.Exp, [rcol, r_lg[l]], scale=lgf)
            act(kp, kp.t[:, 1, hh:hh + 1], rcol.t[:, 1:2], AF.Exp, [rcol, r_lg[l]], scale=lgb)
            act(dsc, dsc.t[:, 0, hh:hh + 1], rcol.t[:, 2:3], AF.Exp, [rcol, r_lg[l]], scale=lgf)
            act(dsc, dsc.t[:, 1, hh:hh + 1], rcol.t[:, 2:3], AF.Exp, [rcol, r_lg[l]], scale=lgb)
        ts(kp, kp.t[:, :, :], kp.t[:, :, :], RET_SCALE, None, ALU.mult, None, [kp])
        return DT, qp, kp, dsc

    glub = [c.sb([128, 8], F32, "glub%d" % l) for l in range(NL)]
    qn_t = [c.sb([128, 6], F32, "qn%d" % l) for l in range(NL)]
    kvn_t = [c.sb([128, 4], F32, "kvn%d" % l) for l in range(NL)]
    kvn_rep = [c.sb([128, 512], F32, "kvnr%d" % l) for l in range(NL)]
    cvw = [c.sb([128, 88, 3], F32, "cvw%d" % l) for l in range(NL)]
    cvb = [c.sb([128, 88], F32, "cvb%d" % l) for l in range(NL)]
    for l in range(NL):
        c.dma("sp", glub[l].t[:, :], glu_bT[l], writes=[glub[l]])
        c.dma("sp", qn_t[l].t[:, :], q_normT[l], writes=[qn_t[l]])
        c.dma("sp", kvn_t[l].t[:, :], kv_normT[l], writes=[kvn_t[l]])
        c.dma("sp", kvn_rep[l].t[:, :], kv_norm_rep[l], writes=[kvn_rep[l]])
        c.dma("sp", cvw[l].t[:, :, :], conv_wT[l], writes=[cvw[l]])
        c.dma("sp", cvb[l].t[:, :], conv_bT[l], writes=[cvb[l]])

    def norm_mod(xsrc, xdep, src0, n, hbuf, dst0, Aap, Bap, xblk_pool):
        xb = xblk_pool.get()
        kw = {"allow_slow_non_contiguous": True} if n == 1 else {}
        c.dma("sp", xb.t[:, :, 0:n], xsrc[:, src0:src0 + n].rearrange("(k p) n -> p k n", p=128), reads=[xdep], writes=[xb], **kw)
        sq = xblk_sq.get()
        act(sq, sq.t[:, :, 0:n], xb.t[:, :, 0:n], AF.Square, [xb])
        ps = ps_aux.get()
        for kc in range(16):
            mm(ps, ps.t[:, 0:n], ones_bf.t[:, :], sq.t[:, kc, 0:n], kc == 0, kc == 15, [ones_bf, sq])
        rs = rstd_p.get()
        rstd_from_ssq(ps, ps.t[:, 0:n], rs, rs.t[:, 0:n], 1.0 / D)
        tt(xb, xb.t[:, :, 0:n], xb.t[:, :, 0:n], bc_last(Aap, n), ALU.mult, [xb] + A_reads)
        tt(xb, xb.t[:, :, 0:n], xb.t[:, :, 0:n], bc_mid(rs.t[:, 0:n], 16), ALU.mult, [xb, rs])
        tt(hbuf, hbuf.t[:, :, dst0:dst0 + n], xb.t[:, :, 0:n], bc_last(Bap, n), ALU.add, [xb] + A_reads)

    A_reads = [PR[l][j] for l in range(NL) for j in range(2)]

    groups = [
        dict(name="p", cond=0, xin=xTp, xin_dep=d_xTp, ntok=1024, L=LP, nseq=NPS, yout=yTp),
        dict(name="s", cond=1, xin=xTs, xin_dep=d_xTs, ntok=LS, L=LS, nseq=1, yout=yTs),
    ]
    x1T, x2T = x1b, x2b

    for G in groups:
        is_p = G["name"] == "p"
        L, nseq, ntok, cj = G["L"], G["nseq"], G["ntok"], G["cond"]
        nseg = ntok // 1024
        for l in range(NL):
            P = PR[l][cj]
            if l == 0:
                xs_ap, xs_dep = G["xin"], G["xin_dep"]
            else:
                xs_ap, xs_dep = x2T.t, x2T.d
            AR.reset()
            hT = AR.tile([128, 16, 1024], BF16)
            xblk_pool = Pool([AR.tile([128, 16, 256], F32) for _ in range(2)])
            xblk_sq = Pool([AR.tile([128, 16, 256], BF16) for _ in range(2)])
            stok = Pool([AR.tile([128, 512], BF16) for _ in range(3)])
            stokf = Pool([AR.tile([128, 512], F32) for _ in range(3)])
            small = Pool([AR.tile([128, 2], F32) for _ in range(4)])
            wA = Pool([AR.tile([128, 16, 512], BF16) for _ in range(2)])
            for sg in range(nseg):
                c0 = sg * 1024
                for b4 in range(4):
                    norm_mod(xs_ap, xs_dep, c0 + b4 * 256, 256, hT, b4 * 256, P.t[:, 0, :], P.t[:, 1, :], xblk_pool)
                fm = []
                for cb in range(6):
                    fm.append((w_in[l], cb * 512, 512, cb * 512, "copy", True))
                for cb in range(4):
                    fm.append((w_in[l], 4096 + cb * 512, 512, R_G + cb * 512, "copy", True))
                fm.append((w_in[l], 4096 + 2048, 256, R_G + 2048, "copy", True))
                fm.append((w_in_kr[l], 0, 64, R_KR, "copy", False))
                for cb in range(12):
                    fm.append((w_in[l], 6464 + cb * 512, 512, R_GATE + cb * 512, "sig", True))
                tokm = [(2048, 512, ktok, 0), (2560, 512, ktok, 512), (3072, 512, vtok, 0), (3584, 512, vtok, 512)]
                if is_p:
                    tokm += [(5888, 512, "ckv", 0), (6400, 64, "kr", 0)]
                tok_by_col = {t[0]: t for t in tokm}
                for bi, (wsrc, col0, ncols, zr0, epi, is_main) in enumerate(fm):
                    w = wA.get()
                    load_w(w, 16, ncols, wsrc[:, col0:col0 + ncols], wcA[l], bi, is_p)
                    nm = (ncols + 127) // 128
                    for mi in range(nm):
                        mw = min(128, ncols - mi * 128)
                        st = stg_bf.get()
                        for tb in range(2):
                            ps = ps_mm.get()
                            for kc in range(16):
                                mm(ps, ps.t[0:mw, :], w.t[:, kc, mi * 128:mi * 128 + mw], hT.t[:, kc, tb * 512:(tb + 1) * 512], kc == 0, kc == 15, [w, hT])
                            if epi == "sig":
                                act(st, st.t[0:mw, tb * 512:(tb + 1) * 512], ps.t[0:mw, :], AF.Sigmoid, [ps])
                            elif (mi + tb) % 2 == 0:
                                act(st, st.t[0:mw, tb * 512:(tb + 1) * 512], ps.t[0:mw, :], AF.Copy, [ps])
                            else:
                                c.op("dve", lambda h, st=st, ps=ps, mw=mw, tb=tb: h.tensor_copy(st.t[0:mw, tb * 512:(tb + 1) * 512], ps.t[0:mw, :]), reads=[ps], writes=[st])
                        c.dma("sp", zT.t[zr0 + mi * 128:zr0 + mi * 128 + mw, c0:c0 + 1024], st.t[0:mw, :], reads=[st], writes=[zT])
                    if is_main and col0 in tok_by_col:
                        _, _, dst, dcol = tok_by_col.pop(col0)
                        for tti in range(8):
                            ps = ps_mm.get()
                            for kc in range(16):
                                mm(ps, ps.t[:, 0:ncols], hT.t[:, kc, tti * 128:(tti + 1) * 128], w.t[:, kc, 0:ncols], kc == 0, kc == 15, [w, hT])
                            so = stok.get()
                            act(so, so.t[:, 0:ncols], ps.t[:, 0:ncols], AF.Copy, [ps])
                            c.dma("sp", dst.t[c0 + tti * 128:c0 + (tti + 1) * 128, dcol:dcol + ncols], so.t[:, 0:ncols], reads=[so], writes=[dst])
                for li_, (col0, ncols, dst, dcol) in enumerate(list(tok_by_col.values())):
                    w = wA.get()
                    if isinstance(dst, str):
                        c.dma("pool", w.t[:, :, 0:ncols], w_in[l, :, col0:col0 + ncols].rearrange("(k p) n -> p k n", p=128), writes=[w])
                    else:
                        load_w(w, 16, ncols, w_in[l, :, col0:col0 + ncols], wcA[l], 29 + li_, is_p)
                    for tti in range(8):
                        ps = ps_mm.get()
                        for kc in range(16):
                            mm(ps, ps.t[:, 0:ncols], hT.t[:, kc, tti * 128:(tti + 1) * 128], w.t[:, kc, 0:ncols], kc == 0, kc == 15, [w, hT])
                        if dst == "ckv":
                            sqf = stokf.get()
                            ss = small.get()
                            c.op("act", lambda h, sqf=sqf, ps=ps, ss=ss: h.activation(sqf.t[:, :], ps.t[:, :], AF.Square, accum_out=ss.t[:, 0:1]),
                                 reads=[ps], writes=[sqf, ss])
                            act(ss, ss.t[:, 1:2], ss.t[:, 0:1], AF.Sqrt, [ss, eps_t], bias=eps_t.t[:, 0:1], scale=1.0 / 512)
                            recip(ss, ss.t[:, 1:2], ss.t[:, 1:2], [ss])
                            so = stokf.get()
                            stt(so, so.t[:, :], ps.t[:, :], ss.t[:, 1:2], kvn_rep[l].t[:, :], ALU.mult, ALU.mult, [ps, ss, kvn_rep[l]])
                            sq_i, t0 = (tti * 128) // LP, (tti * 128) % LP
                            c.dma("sp", ckv_o[sq_i, l, t0:t0 + 128, :], so.t[:, :], reads=[so], writes=[d_out])
                        elif dst == "kr":
                            so = stokf.get()
                            act(so, so.t[:, 0:64], ps.t[:, 0:64], AF.Copy, [ps])
                            sq_i, t0 = (tti * 128) // LP, (tti * 128) % LP
                            c.dma("sp", kr_o[sq_i, l, t0:t0 + 128, :], so.t[:, 0:64], reads=[so], writes=[d_out])
                        else:
                            so = stok.get()
                            act(so, so.t[:, 0:ncols], ps.t[:, 0:ncols], AF.Copy, [ps])
                            c.dma("sp", dst.t[c0 + tti * 128:c0 + (tti + 1) * 128, dcol:dcol + ncols], so.t[:, 0:ncols], reads=[so], writes=[dst])
            if debug_stop == "A":
                break
            AR.reset()
            HL = L // 2
            Ec = AR.tile([128, L], F32)
            Es = AR.tile([128, L], F32)
            tq = AR.tile([128, HL], F32)
            btil = [AR.tile([128, L], F32) for _ in range(2)]
            sbf = [[AR.tile([128, ntok], BF16) for _ in range(2)] for _ in range(2)]
            u_pool = Pool([AR.tile([32, ntok], BF16) for _ in range(1)])
            tmp32 = Pool([AR.tile([128, 512], F32) for _ in range(4)])
            yj_p = Pool([AR.tile([32, 512], F32) for _ in range(2)])
            g_p = Pool([AR.tile([32, 512], F32) for _ in range(3)])
            wbj = Pool([AR.tile([32, 128], BF16) for _ in range(8)])
            wcj = Pool([AR.tile([128, 32], BF16) for _ in range(4)])
            fin = [[AR.tile([128, 2, 32], F32) for _ in range(2)] for _ in range(nseq)] if is_p else None
            fin_tmp = Pool([AR.tile([128, 2], F32) for _ in range(4)])
            BLK = min(512, L)
            br_t, bi_t = btil
            for j in range(32):
                uj = u_pool.get()
                c.dma("sp", uj.t[:, :], zT.t[R_U + j * 32:R_U + (j + 1) * 32, 0:ntok], reads=[zT], writes=[uj])
                for d in range(2):
                    wbr, wbi = wbj.get(), wbj.get()
                    c.dma("sp", wbr.t[:, :], wb_d.t[l, d, 0, :, j, :], reads=[wb_d], writes=[wbr])
                    c.dma("sp", wbi.t[:, :], wb_d.t[l, d, 1, :, j, :], reads=[wb_d], writes=[wbi])
                    dc, ds = s5_dc[l][d], s5_ds[l][d]
                    c.op("dve", lambda h, dc=dc, j=j, Ec=Ec: h.tensor_copy(Ec.t[:, 0:1], dc.t[:, 0, j:j + 1]), reads=[dc], writes=[Ec])
                    c.op("dve", lambda h, ds=ds, j=j, Es=Es: h.tensor_copy(Es.t[:, 0:1], ds.t[:, 0, j:j + 1]), reads=[ds], writes=[Es])
                    n = 1
                    k = 0
                    while n < L:
                        Ck, Sk = dc.t[:, k, j:j + 1], ds.t[:, k, j:j + 1]
                        ts(tq, tq.t[:, 0:n], Es.t[:, 0:n], Sk, None, ALU.mult, None, [Es, ds])
                        stt(Ec, Ec.t[:, n:2 * n], Ec.t[:, 0:n], Ck, tq.t[:, 0:n], ALU.mult, ALU.subtract, [Ec, dc, tq])
                        ts(tq, tq.t[:, 0:n], Ec.t[:, 0:n], Sk, None, ALU.mult, None, [Ec, ds])
                        stt(Es, Es.t[:, n:2 * n], Es.t[:, 0:n], Ck, tq.t[:, 0:n], ALU.mult, ALU.add, [Es, dc, tq])
                        n *= 2
                        k += 1
                    mg = s5_mag[l][d].t[:, j:j + 1]
                    magb = bass.AP(mg.tensor, mg.offset, [list(mg.ap[0]), [0, L]])
                    sre, sim = sbf[d]
                    for s in range(nseq):
                        s0 = s * L
                        for blk in range(L // BLK):
                            cols = slice(s0 + blk * BLK, s0 + (blk + 1) * BLK)
                            if d == 0:
                                tcols = slice(blk * BLK, (blk + 1) * BLK)
                            else:
                                tcols = slice(L - (blk + 1) * BLK, L - blk * BLK)
                            pr, pi = ps_mm.get(), ps_mm.get()
                            mm(pr, pr.t[:, 0:BLK], wbr.t[:, :], uj.t[:, cols], True, True, [wbr, uj])
                            mm(pi, pi.t[:, 0:BLK], wbi.t[:, :], uj.t[:, cols], True, True, [wbi, uj])
                            prv = pr.t[:, 0:BLK] if d == 0 else rev(pr.t[:, 0:BLK])
                            piv = pi.t[:, 0:BLK] if d == 0 else rev(pi.t[:, 0:BLK])
                            t1, t2 = tmp32.get(), tmp32.get()
                            tt(t1, t1.t[:, 0:BLK], prv, Ec.t[:, tcols], ALU.mult, [pr, Ec])
                            tt(t2, t2.t[:, 0:BLK], piv, Es.t[:, tcols], ALU.mult, [pi, Es])
                            tt(br_t, br_t.t[:, tcols], t1.t[:, 0:BLK], t2.t[:, 0:BLK], ALU.add, [t1, t2])
                            t3, t4 = tmp32.get(), tmp32.get()
                            tt(t3, t3.t[:, 0:BLK], piv, Ec.t[:, tcols], ALU.mult, [pi, Ec])
                            tt(t4, t4.t[:, 0:BLK], prv, Es.t[:, tcols], ALU.mult, [pr, Es])
                            tt(bi_t, bi_t.t[:, tcols], t3.t[:, 0:BLK], t4.t[:, 0:BLK], ALU.subtract, [t3, t4])
                        if is_p and l == 0 and j == 0 and s == 0:
                            dump("bt_r%d" % d, br_t.t[:, :], [128, L], F32, [br_t])
                            dump("bt_i%d" % d, bi_t.t[:, :], [128, L], F32, [bi_t])
                        for ri, bt in enumerate(btil):
                            if is_p:
                                init = 0.0
                                rd = [bt, s5_mag[l][d]]
                            else:
                                init = s5_init[l][d].t[:, j, ri:ri + 1]
                                rd = [bt, s5_mag[l][d], s5_init[l][d]]
                            c.op("dve", lambda h, bt=bt, magb=magb, init=init: h.tensor_tensor_scan(bt.t[:, :], magb, bt.t[:, :], init, ALU.mult, ALU.add),
                                 reads=rd, writes=[bt])
                        if is_p and l == 0 and j == 0 and s == 0:
                            dump("Ec%d" % d, Ec.t[:, :], [128, L], F32, [Ec])
                            dump("Es%d" % d, Es.t[:, :], [128, L], F32, [Es])
                            dump("sr%d" % d, br_t.t[:, :], [128, L], F32, [br_t])
                            dump("si%d" % d, bi_t.t[:, :], [128, L], F32, [bi_t])
                        for q0 in range(0, L, 512):
                            qn_ = min(512, L - q0)
                            tc = slice(q0, q0 + qn_)
                            if d == 0:
                                ore, oim = sre.t[:, s0 + q0:s0 + q0 + qn_], sim.t[:, s0 + q0:s0 + q0 + qn_]
                            else:
                                ore = rev(sre.t[:, s0 + L - q0 - qn_:s0 + L - q0])
                                oim = rev(sim.t[:, s0 + L - q0 - qn_:s0 + L - q0])
                            t1, t2 = tmp32.get(), tmp32.get()
                            tt(t1, t1.t[:, 0:qn_], br_t.t[:, tc], Ec.t[:, tc], ALU.mult, [br_t, Ec])
                            tt(t2, t2.t[:, 0:qn_], bi_t.t[:, tc], Es.t[:, tc], ALU.mult, [bi_t, Es])
                            tt(sre, ore, t1.t[:, 0:qn_], t2.t[:, 0:qn_], ALU.subtract, [t1, t2])
                            t3, t4 = tmp32.get(), tmp32.get()
                            tt(t3, t3.t[:, 0:qn_], bi_t.t[:, tc], Ec.t[:, tc], ALU.mult, [bi_t, Ec])
                            tt(t4, t4.t[:, 0:qn_], br_t.t[:, tc], Es.t[:, tc], ALU.mult, [br_t, Es])
                            tt(sim, oim, t3.t[:, 0:qn_], t4.t[:, 0:qn_], ALU.add, [t3, t4])
                        if is_p:
                            ft = fin_tmp.get()
                            f_ = fin[s][d]
                            e_c, e_s = Ec.t[:, L - 1:L], Es.t[:, L - 1:L]
                            tt(ft, ft.t[:, 0:1], br_t.t[:, L - 1:L], e_c, ALU.mult, [br_t, Ec])
                            tt(ft, ft.t[:, 1:2], bi_t.t[:, L - 1:L], e_s, ALU.mult, [bi_t, Es])
                            tt(f_, f_.t[:, 0, j:j + 1], ft.t[:, 0:1], ft.t[:, 1:2], ALU.subtract, [ft])
                            ft = fin_tmp.get()
                            tt(ft, ft.t[:, 0:1], bi_t.t[:, L - 1:L], e_c, ALU.mult, [bi_t, Ec])
                            tt(ft, ft.t[:, 1:2], br_t.t[:, L - 1:L], e_s, ALU.mult, [br_t, Es])
                            tt(f_, f_.t[:, 1, j:j + 1], ft.t[:, 0:1], ft.t[:, 1:2], ALU.add, [ft])
                wcr, wci = wcj.get(), wcj.get()
                c.dma("sp", wcr.t[:, :], wc_d.t[l, 0, :, j, :], reads=[wc_d], writes=[wcr])
                c.dma("sp", wci.t[:, :], wc_d.t[l, 1, :, j, :], reads=[wc_d], writes=[wci])
                for blk in range(ntok // 512):
                    cols = slice(blk * 512, (blk + 1) * 512)
                    ps = ps_mm.get()
                    mm(ps, ps.t[0:32, :], wcr.t[:, :], sbf[0][0].t[:, cols], True, False, [wcr, sbf[0][0]])
                    mm(ps, ps.t[0:32, :], wci.t[:, :], sbf[0][1].t[:, cols], False, False, [wci, sbf[0][1]])
                    mm(ps, ps.t[0:32, :], wcr.t[:, :], sbf[1][0].t[:, cols], False, False, [wcr, sbf[1][0]])
                    mm(ps, ps.t[0:32, :], wci.t[:, :], sbf[1][1].t[:, cols], False, True, [wci, sbf[1][1]])
                    yj = yj_p.get()
                    stt(yj, yj.t[:, :], uj.t[:, cols], s5_dsk[l].t[:, j:j + 1], ps.t[0:32, :], ALU.mult, ALU.add, [uj, s5_dsk[l], ps])
                    g1_ = g_p.get()
                    tt(g1_, g1_.t[:, :], yj.t[:, :], yj.t[:, :], ALU.mult, [yj])
                    ts(g1_, g1_.t[:, :], g1_.t[:, :], 0.044715, 1.0, ALU.mult, ALU.add, [g1_])
                    tt(g1_, g1_.t[:, :], g1_.t[:, :], yj.t[:, :], ALU.mult, [g1_, yj])
                    act(g1_, g1_.t[:, :], g1_.t[:, :], AF.Sigmoid, [g1_], scale=2.0 * math.sqrt(2.0 / math.pi))
                    so = stg_bf.get()
                    tt(so, so.t[0:32, 0:512], g1_.t[:, :], yj.t[:, :], ALU.mult, [g1_, yj])
                    c.dma("sp", ygT.t[j * 32:(j + 1) * 32, cols], so.t[0:32, 0:512], reads=[so], writes=[ygT])
            if is_p:
                for s in range(nseq):
                    for d in range(2):
                        c.dma("sp", s5_o[d, s, l].rearrange("r q j -> q r j"), fin[s][d].t[:, :, :], reads=[fin[s][d]], writes=[d_out])
            AR.reset()
            wg = AR.tile([128, 8, 1024], BF16)
            c.dma("pool", wg.t[:, :, :], glu_w[l].rearrange("(k p) n -> p k n", p=128), writes=[wg])
            yg_p = Pool([AR.tile([128, 8, 512], BF16) for _ in range(2)])
            sg_p = Pool([AR.tile([128, 512], F32) for _ in range(3)])
            for tb in range(ntok // 512):
                cols = slice(tb * 512, (tb + 1) * 512)
                yg = yg_p.get()
                c.dma("sp", yg.t[:, :, :], ygT.t[:, cols].rearrange("(k p) n -> p k n", p=128), reads=[ygT], writes=[yg])
                for m in range(8):
                    ps = ps_mm.get()
                    for kc in range(8):
                        mm(ps, ps.t[:, :], wg.t[:, kc, m * 128:(m + 1) * 128], yg.t[:, kc, :], kc == 0, kc == 7, [wg, yg])
                    sg = sg_p.get()
                    act(sg, sg.t[:, :], ps.t[:, :], AF.Sigmoid, [ps, glub[l]], bias=glub[l].t[:, m:m + 1])
                    so = stg_bf.get()
                    tt(so, so.t[:, 0:512], sg.t[:, :], yg.t[:, m, :], ALU.mult, [sg, yg])
                    c.dma("sp", yT.t[m * 128:(m + 1) * 128, cols], so.t[:, 0:512], reads=[so], writes=[yT])
            AR.reset()
            NCH = L // 128
            rDT, rqp, rkp, rds = ret_tables(l)
            qT_ = AR.tile([128, ntok], BF16)
            kT_ = AR.tile([128, ntok], BF16)
            qf_ = AR.tile([128, ntok], BF16)
            qb_ = AR.tile([128, ntok], BF16)
            kt_ = AR.tile([128, ntok // 128, 128], BF16)
            vt_ = AR.tile([128, ntok // 128, 128], BF16)
            kf_ = AR.tile([128, ntok // 128, 128], BF16)
            kb_ = AR.tile([128, ntok // 128, 128], BF16)
            acc = AR.tile([128, ntok], F32)
            Sst = [AR.tile([128, 128], F32) for _ in range(2)]
            Sbf = Pool([AR.tile([128, 128], BF16) for _ in range(3)])
            pt_p = Pool([AR.tile([128, 128], BF16) for _ in range(3)])
            gt_p = Pool([AR.tile([128, 512], BF16) for _ in range(2)])
            w32 = Pool([AR.tile([128, 512], F32) for _ in range(6)])
            obf_p = Pool([AR.tile([128, 512], BF16) for _ in range(2)])
            for hh in range(8):
                c.dma("sp", qT_.t[:, :], zT.t[R_Q + hh * 128:R_Q + (hh + 1) * 128, 0:ntok], reads=[zT], writes=[qT_])
                c.dma("sp", kT_.t[:, :], zT.t[R_K + hh * 128:R_K + (hh + 1) * 128, 0:ntok], reads=[zT], writes=[kT_])
                c.dma("sp", kt_.t[:, :, :], ktok.t[0:ntok, hh * 128:(hh + 1) * 128].rearrange("(c j) d -> j c d", j=128), reads=[ktok], writes=[kt_])
                c.dma("sp", vt_.t[:, :, :], vtok.t[0:ntok, hh * 128:(hh + 1) * 128].rearrange("(c j) d -> j c d", j=128), reads=[vtok], writes=[vt_])
                q3 = qT_.t[:, :].rearrange("p (a b) -> p a b", b=128)
                tt(qf_, qf_.t[:, :].rearrange("p (a b) -> p a b", b=128), q3, bc_mid(rqp.t[:, 0, hh, :], ntok // 128), ALU.mult, [qT_, rqp])
                tt(qb_, qb_.t[:, :].rearrange("p (a b) -> p a b", b=128), q3, bc_mid(rqp.t[:, 1, hh, :], ntok // 128), ALU.mult, [qT_, rqp])
                ts(kf_, kf_.t[:, :, :], kt_.t[:, :, :], rkp.t[:, 0, hh:hh + 1], None, ALU.mult, None, [kt_, rkp])
                ts(kb_, kb_.t[:, :, :], kt_.t[:, :, :], rkp.t[:, 1, hh:hh + 1], None, ALU.mult, None, [kt_, rkp])
                for s in range(nseq):
                    for d in range(2):
                        S = Sst[d]
                        if is_p:
                            c.op("pool", lambda h, S=S: h.memset(S.t[:, :], 0.0), writes=[S])
                        else:
                            c.dma("sp", S.t[:, :], ret_s0[l, d, hh], writes=[S])
                        order = range(NCH) if d == 0 else range(NCH - 1, -1, -1)
                        qd = qf_ if d == 0 else qb_
                        kd = kf_ if d == 0 else kb_
                        for ch in order:
                            gc = s * NCH + ch
                            cs_ = slice(gc * 128, (gc + 1) * 128)
                            sb_ = Sbf.get()
                            act(sb_, sb_.t[:, :], S.t[:, :], AF.Copy, [S])
                            po = ps_mm.get()
                            if d == 0:
                                pa = ps_mm.get()
                                mm(pa, pa.t[:, 0:128], kT_.t[:, cs_], qT_.t[:, cs_], True, True, [kT_, qT_])
                                pt = pt_p.get()
                                tt(pt, pt.t[:, :], pa.t[:, 0:128], rDT.t[:, hh, :], ALU.mult, [pa, rDT])
                                mm(po, po.t[:, 0:128], vt_.t[:, gc, :], pt.t[:, :], True, False, [vt_, pt])
                                mm(po, po.t[:, 0:128], sb_.t[:, :], qd.t[:, cs_], False, True, [sb_, qd])
                                act(acc, acc.t[:, cs_], po.t[:, 0:128], AF.Copy, [po])
                            else:
                                mm(po, po.t[:, 0:128], sb_.t[:, :], qd.t[:, cs_], True, True, [sb_, qd])
                                tt(acc, acc.t[:, cs_], acc.t[:, cs_], po.t[:, 0:128], ALU.add, [acc, po])
                            pS = ps_mm.get()
                            mm(pS, pS.t[:, 0:128], kd.t[:, gc, :], vt_.t[:, gc, :], True, True, [kd, vt_])
                            stt(S, S.t[:, :], S.t[:, :], rds.t[:, d, hh:hh + 1], pS.t[:, 0:128], ALU.mult, ALU.add, [S, rds, pS])
                        if is_p:
                            c.dma("sp", ret_o[d, s, l, hh], S.t[:, :], reads=[S], writes=[d_out])
                for tb in range(ntok // 512):
                    cols = slice(tb * 512, (tb + 1) * 512)
                    ob = obf_p.get()
                    act(ob, ob.t[:, :], acc.t[:, cols], AF.Copy, [acc])
                    sq = sq_p.get()
                    act(sq, sq.t[:, :], acc.t[:, cols], AF.Square, [acc])
                    pm, pv = ps_aux.get(), ps_aux.get()
                    mm(pm, pm.t[:, :], ones128.t[:, :], ob.t[:, :], True, True, [ones128, ob])
                    mm(pv, pv.t[:, :], ones128.t[:, :], sq.t[:, :], True, True, [ones128, sq])
                    mean = w32.get()
                    act(mean, mean.t[:, :], pm.t[:, :], AF.Copy, [pm])
                    var = w32.get()
                    tt(var, var.t[:, :], mean.t[:, :], mean.t[:, :], ALU.mult, [mean])
                    tt(var, var.t[:, :], pv.t[:, :], var.t[:, :], ALU.subtract, [pv, var])
                    ts(var, var.t[:, :], var.t[:, :], 0.0, None, ALU.max, None, [var])
                    act(var, var.t[:, :], var.t[:, :], AF.Sqrt, [var, eps_t], bias=eps_t.t[:, 0:1])
                    recip(var, var.t[:, :], var.t[:, :], [var])
                    cen = w32.get()
                    tt(cen, cen.t[:, :], acc.t[:, cols], mean.t[:, :], ALU.subtract, [acc, mean])
                    tt(cen, cen.t[:, :], cen.t[:, :], var.t[:, :], ALU.mult, [cen, var])
                    gt = gt_p.get()
                    c.dma("sp", gt.t[:, :], zT.t[R_G + hh * 128:R_G + (hh + 1) * 128, cols], reads=[zT], writes=[gt])
                    sgt = w32.get()
                    act(sgt, sgt.t[:, :], gt.t[:, :], AF.Silu, [gt])
                    so = stg_bf.get()
                    stt(so, so.t[:, 0:512], cen.t[:, :], r_ng[l].t[:, hh:hh + 1], sgt.t[:, :], ALU.mult, ALU.mult, [cen, r_ng[l], sgt])
                    c.dma("sp", yT.t[DM + hh * 128:DM + (hh + 1) * 128, cols], so.t[:, 0:512], reads=[so], writes=[yT])
            AR.reset()
            cqn = AR.tile([128, 6, 512], BF16)
            wq_all = AR.tile([128, 6, 1536], BF16)
            c.dma("pool", wq_all.t[:, :, :], w_uq[l].rearrange("(k p) n -> p k n", p=128), writes=[wq_all])
            ld_p = Pool([AR.tile([128, 6, 512], BF16) for _ in range(2)])
            ldsq = Pool([AR.tile([128, 6, 512], BF16) for _ in range(2)])
            k32 = Pool([AR.tile([32, 512], F32) for _ in range(6)])
            rope_p = Pool([AR.tile([32, 512], F32) for _ in range(4)])
            qst = Pool([AR.tile([128, 512], BF16) for _ in range(3)])
            qst32 = Pool([AR.tile([32, 512], BF16) for _ in range(4)])
            for tb in range(ntok // 512):
                cols = slice(tb * 512, (tb + 1) * 512)
                ld = ld_p.get()
                c.dma("sp", ld.t[:, :, :], zT.t[R_CQ:R_CQ + 768, cols].rearrange("(k p) n -> p k n", p=128), reads=[zT], writes=[ld])
                sq = ldsq.get()
                act(sq, sq.t[:, :, :], ld.t[:, :, :], AF.Square, [ld])
                ps = ps_aux.get()
                for kc in range(6):
                    mm(ps, ps.t[:, :], ones_bf.t[:, :], sq.t[:, kc, :], kc == 0, kc == 5, [ones_bf, sq])
                rs = rstd_p.get()
                rstd_from_ssq(ps, ps.t[:, :], rs, rs.t[:, 0:512], 1.0 / 768)
                for kc in range(6):
                    stt(cqn, cqn.t[:, kc, :], ld.t[:, kc, :], qn_t[l].t[:, kc:kc + 1], rs.t[:, 0:512], ALU.mult, ALU.mult, [ld, qn_t[l], rs])
                if not is_p:
                    rc, rsn = rope_p.get(), rope_p.get()
                    c.dma("sp", rc.t[:, :], rope_cs[0, :, cols], writes=[rc])
                    c.dma("sp", rsn.t[:, :], rope_cs[1, :, cols], writes=[rsn])
                for hh in range(8):
                    ps = ps_mm.get()
                    for kc in range(6):
                        mm(ps, ps.t[:, :], wq_all.t[:, kc, hh * 192:hh * 192 + 128], cqn.t[:, kc, :], kc == 0, kc == 5, [wq_all, cqn])
                    so = qst.get()
                    act(so, so.t[:, :], ps.t[:, :], AF.Copy, [ps])
                    c.dma("sp", qscr.t[hh, 0:128, cols], so.t[:, :], reads=[so], writes=[qscr])
                    pr = []
                    for hf in range(2):
                        p_ = ps_mm.get()
                        for kc in range(6):
                            mm(p_, p_.t[0:32, :], wq_all.t[:, kc, hh * 192 + 128 + hf * 32:hh * 192 + 160 + hf * 32], cqn.t[:, kc, :], kc == 0, kc == 5, [wq_all, cqn])
                        pr.append(p_)
                    o0, o1 = qst32.get(), qst32.get()
                    if is_p:
                        act(o0, o0.t[:, :], pr[0].t[0:32, :], AF.Copy, [pr[0]])
                        act(o1, o1.t[:, :], pr[1].t[0:32, :], AF.Copy, [pr[1]])
                    else:
                        a1_, a2_ = k32.get(), k32.get()
                        tt(a1_, a1_.t[:, :], pr[0].t[0:32, :], rc.t[:, :], ALU.mult, [pr[0], rc])
                        tt(a2_, a2_.t[:, :], pr[1].t[0:32, :], rsn.t[:, :], ALU.mult, [pr[1], rsn])
                        tt(o0, o0.t[:, :], a1_.t[:, :], a2_.t[:, :], ALU.subtract, [a1_, a2_])
                        a3_, a4_ = k32.get(), k32.get()
                        tt(a3_, a3_.t[:, :], pr[0].t[0:32, :], rsn.t[:, :], ALU.mult, [pr[0], rsn])
                        tt(a4_, a4_.t[:, :], pr[1].t[0:32, :], rc.t[:, :], ALU.mult, [pr[1], rc])
                        tt(o1, o1.t[:, :], a3_.t[:, :], a4_.t[:, :], ALU.add, [a3_, a4_])
                    c.dma("sp", qscr.t[hh, 128:160, cols], o0.t[:, :], reads=[o0], writes=[qscr])
                    c.dma("sp", qscr.t[hh, 160:192, cols], o1.t[:, :], reads=[o1], writes=[qscr])
            AR.reset()
            SK = L if is_p else L + PAST
            NKT = SK // 128
            ckv = AR.tile([128, 4, nseq * SK], BF16)
            kr = [AR.tile([32, nseq * SK], BF16) for _ in range(2)]
            qn_h = AR.tile([128, ntok], BF16)
            qr_h = [AR.tile([32, ntok], BF16) for _ in range(2)]
            kn_h = AR.tile([128, nseq * SK], BF16)
            v_h = AR.tile([128, nseq * NKT, 128], BF16)
            wk_h = AR.tile([128, 4, 128], BF16)
            wv_h = AR.tile([128, 4, 128], BF16)
            ld_p = Pool([AR.tile([128, 4, 512], BF16) for _ in range(2)])
            ldsq = Pool([AR.tile([128, 4, 512], BF16) for _ in range(1)])
            pt_p = Pool([AR.tile([128, 512], BF16) for _ in range(4)])
            w32 = Pool([AR.tile([128, 512], F32) for _ in range(3)])
            k32 = Pool([AR.tile([32, 512], F32) for _ in range(4)])
            rope_p = Pool([AR.tile([32, 512], F32) for _ in range(2)])
            kraw = Pool([AR.tile([32, 512], BF16) for _ in range(2)])
            for tb in range(ntok // 512):
                cols = slice(tb * 512, (tb + 1) * 512)
                ld = ld_p.get()
                c.dma("sp", ld.t[:, :, :], zT.t[R_CKV:R_CKV + 512, cols].rearrange("(k p) n -> p k n", p=128), reads=[zT], writes=[ld])
                sq = ldsq.get()
                act(sq, sq.t[:, :, :], ld.t[:, :, :], AF.Square, [ld])
                ps = ps_aux.get()
                for kc in range(4):
                    mm(ps, ps.t[:, :], ones_bf.t[:, :], sq.t[:, kc, :], kc == 0, kc == 3, [ones_bf, sq])
                rs = rstd_p.get()
                rstd_from_ssq(ps, ps.t[:, :], rs, rs.t[:, 0:512], 1.0 / 512)
                for kc in range(4):
                    stt(ckv, ckv.t[:, kc, cols], ld.t[:, kc, :], kvn_t[l].t[:, kc:kc + 1], rs.t[:, 0:512], ALU.mult, ALU.mult, [ld, kvn_t[l], rs])
                kw0, kw1 = kraw.get(), kraw.get()
                c.dma("sp", kw0.t[:, :], zT.t[R_KR:R_KR + 32, cols], reads=[zT], writes=[kw0])
                c.dma("sp", kw1.t[:, :], zT.t[R_KR + 32:R_KR + 64, cols], reads=[zT], writes=[kw1])
                if is_p:
                    c.op("dve", lambda h, kw0=kw0, cols=cols, k0_=kr[0]: h.tensor_copy(k0_.t[:, cols], kw0.t[:, :]), reads=[kw0], writes=[kr[0]])
                    c.op("dve", lambda h, kw1=kw1, cols=cols, k1_=kr[1]: h.tensor_copy(k1_.t[:, cols], kw1.t[:, :]), reads=[kw1], writes=[kr[1]])
                else:
                    rc, rsn = rope_p.get(), rope_p.get()
                    c.dma("sp", rc.t[:, :], rope_cs[0, :, cols], writes=[rc])
                    c.dma("sp", rsn.t[:, :], rope_cs[1, :, cols], writes=[rsn])
                    a1_, a2_ = k32.get(), k32.get()
                    tt(a1_, a1_.t[:, :], kw0.t[:, :], rc.t[:, :], ALU.mult, [kw0, rc])
                    tt(a2_, a2_.t[:, :], kw1.t[:, :], rsn.t[:, :], ALU.mult, [kw1, rsn])
                    tt(kr[0], kr[0].t[:, cols], a1_.t[:, :], a2_.t[:, :], ALU.subtract, [a1_, a2_])
                    a3_, a4_ = k32.get(), k32.get()
                    tt(a3_, a3_.t[:, :], kw0.t[:, :], rsn.t[:, :], ALU.mult, [kw0, rsn])
                    tt(a4_, a4_.t[:, :], kw1.t[:, :], rc.t[:, :], ALU.mult, [kw1, rc])
                    tt(kr[1], kr[1].t[:, cols], a3_.t[:, :], a4_.t[:, :], ALU.add, [a3_, a4_])
            if not is_p:
                c.dma("pool", ckv.t[:, :, L:L + PAST], cache_ckvT[l].rearrange("(k p) n -> p k n", p=128), writes=[ckv])
                for hf in range(2):
                    c.dma("pool", kr[hf].t[:, L:L + PAST], cache_kr[l, hf], writes=[kr[hf]])
            for hh in range(8):
                c.dma("pool", wk_h.t[:, :, :], w_uk[l, :, hh * 128:(hh + 1) * 128].rearrange("(k p) n -> p k n", p=128), writes=[wk_h])
                c.dma("pool", wv_h.t[:, :, :], w_uv[l, :, hh * 128:(hh + 1) * 128].rearrange("(k p) n -> p k n", p=128), writes=[wv_h])
                c.dma("sp", qn_h.t[:, :], qscr.t[hh, 0:128, 0:ntok], reads=[qscr], writes=[qn_h])
                c.dma("sp", qr_h[0].t[:, :], qscr.t[hh, 128:160, 0:ntok], reads=[qscr], writes=[qr_h[0]])
                c.dma("sp", qr_h[1].t[:, :], qscr.t[hh, 160:192, 0:ntok], reads=[qscr], writes=[qr_h[1]])
                nk_tot = nseq * SK
                for k0 in range(0, nk_tot, 512):
                    kn_ = min(512, nk_tot - k0)
                    ps = ps_mm.get()
                    for kc in range(4):
                        mm(ps, ps.t[:, 0:kn_], wk_h.t[:, kc, :], ckv.t[:, kc, k0:k0 + kn_], kc == 0, kc == 3, [wk_h, ckv])
                    act(kn_h, kn_h.t[:, k0:k0 + kn_], ps.t[:, 0:kn_], AF.Copy, [ps])
                for kt4 in range(0, nseq * NKT, 4):
                    nn = min(4, nseq * NKT - kt4)
                    ps = ps_mm.get()
                    for i in range(nn):
                        kt = kt4 + i
                        for kc in range(4):
                            mm(ps, ps.t[:, i * 128:(i + 1) * 128], ckv.t[:, kc, kt * 128:(kt + 1) * 128], wv_h.t[:, kc, :], kc == 0, kc == 3, [ckv, wv_h])
                    c.op("dve", lambda h, ps=ps, kt4=kt4, nn=nn, v_h=v_h: h.tensor_copy(v_h.t[:, kt4:kt4 + nn, :], ps.t[:, 0:nn * 128].rearrange("p (a b) -> p a b", b=128)),
                         reads=[ps], writes=[v_h])
                if is_p and l == 0 and hh == 0:
                    dump("qn", qn_h.t[:, :], [128, ntok], BF16, [qn_h])
                    dump("qr0", qr_h[0].t[:, :], [32, ntok], BF16, [qr_h[0]])
                    dump("kn", kn_h.t[:, :], [128, nseq * SK], BF16, [kn_h])
                    dump("kr0", kr[0].t[:, :], [32, nseq * SK], BF16, [kr[0]])
                    dump("vh", v_h.t[:, :, :], [128, nseq * NKT, 128], BF16, [v_h])
                    dump("ckv", ckv.t[:, :, :], [128, 4, nseq * SK], BF16, [ckv])
                QB = min(512, L)
                for s in range(nseq):
                    for qb in range(L // QB):
                        qc = slice(s * L + qb * QB, s * L + (qb + 1) * QB)
                        pend = []
                        for kt in range(NKT + 2):
                            if kt < NKT:
                                kc_ = slice(s * SK + kt * 128, s * SK + (kt + 1) * 128)
                                ps = ps_mm.get()
                                mm(ps, ps.t[:, 0:QB], kn_h.t[:, kc_], qn_h.t[:, qc], True, False, [kn_h, qn_h])
                                mm(ps, ps.t[:, 0:QB], kr[0].t[:, kc_], qr_h[0].t[:, qc], False, False, [kr[0], qr_h[0]])
                                mm(ps, ps.t[:, 0:QB], kr[1].t[:, kc_], qr_h[1].t[:, qc], False, True, [kr[1], qr_h[1]])
                                pt = pt_p.get()
                                act(pt, pt.t[:, 0:QB], ps.t[:, 0:QB], AF.Exp, [ps], scale=ATT_SCALE)
                                pend.append((kt, pt))
                            if kt >= 2:
                                k0, pt0 = pend.pop(0)
                                mm(ACC0, ACC0.t[:, 0:QB], v_h.t[:, s * NKT + k0, :], pt0.t[:, 0:QB], k0 == 0, k0 == NKT - 1, [v_h, pt0])
                                mm(ACC1, ACC1.t[:, 0:QB], ones_bf.t[:, :], pt0.t[:, 0:QB], k0 == 0, k0 == NKT - 1, [ones_bf, pt0])
                        rd = w32.get()
                        recip(rd, rd.t[:, 0:QB], ACC1.t[:, 0:QB], [ACC1])
                        if is_p and l == 0 and hh == 0 and s == 0:
                            dump("rden", rd.t[:, 0:QB], [128, QB], F32, [rd])
                            on_ = w32.get()
                            act(on_, on_.t[:, 0:QB], ACC0.t[:, 0:QB], AF.Copy, [ACC0])
                            dump("onum", on_.t[:, 0:QB], [128, QB], F32, [on_])
                        so = stg_bf.get()
                        tt(so, so.t[:, 0:QB], ACC0.t[:, 0:QB], rd.t[:, 0:QB], ALU.mult, [ACC0, rd])
                        c.dma("sp", yT.t[2 * DM + hh * 128:2 * DM + (hh + 1) * 128, qc], so.t[:, 0:QB], reads=[so], writes=[yT])
            if DEBUG and is_p and l == 0:
                c.dma("sp", dbg_y, yT.t[:, 0:1024], reads=[yT], writes=[d_out])
            AR.reset()
            merged = AR.tile([128, 16, 1024], BF16)
            yb_p = Pool([AR.tile([128, 8, 1024], BF16) for _ in range(2)])
            gt_p = Pool([AR.tile([128, 1024], BF16) for _ in range(2)])
            t32 = Pool([AR.tile([128, 512], F32) for _ in range(3)])
            xl_p = Pool([AR.tile([128, 1024], F32) for _ in range(2)])
            fl_p = Pool([AR.tile([128, 1024], F32) for _ in range(2)])
            wA = Pool([AR.tile([128, 16, 512], BF16) for _ in range(2)])
            for sg in range(nseg):
                c0 = sg * 1024
                for b in range(3):
                    yb = yb_p.get()
                    c.dma("sp", yb.t[:, :, :], yT.t[b * DM:(b + 1) * DM, c0:c0 + 1024].rearrange("(k p) n -> p k n", p=128), reads=[yT], writes=[yb])
                    for blk in range(4):
                        w = wA.get()
                        load_w(w, 8, 512, w_branch[l, b, :, blk * 512:(blk + 1) * 512], wcC[l], b * 4 + blk, is_p)
                        for mi in range(4):
                            m = blk * 4 + mi
                            gt = gt_p.get()
                            r0 = R_GATE + b * D + m * 128
                            c.dma("sp", gt.t[:, :], zT.t[r0:r0 + 128, c0:c0 + 1024], reads=[zT], writes=[gt])
                            for tb in range(2):
                                tc = slice(tb * 512, (tb + 1) * 512)
                                ps = ps_mm.get()
                                for kc in range(8):
                                    mm(ps, ps.t[:, :], w.t[:, kc, mi * 128:(mi + 1) * 128], yb.t[:, kc, tc], kc == 0, kc == 7, [w, yb])
                                if b == 0:
                                    tt(merged, merged.t[:, m, tc], ps.t[:, :], gt.t[:, tc], ALU.mult, [ps, gt])
                                else:
                                    tq_ = t32.get()
                                    tt(tq_, tq_.t[:, :], ps.t[:, :], gt.t[:, tc], ALU.mult, [ps, gt])
                                    tt(merged, merged.t[:, m, tc], merged.t[:, m, tc], tq_.t[:, :], ALU.add, [merged, tq_])
                for blk in range(4):
                    w = wA.get()
                    load_w(w, 16, 512, w_out[l, :, blk * 512:(blk + 1) * 512], wcC[l], 12 + blk, is_p)
                    for mi in range(4):
                        m = blk * 4 + mi
                        so = stg_f.get()
                        for tb in range(2):
                            tc = slice(tb * 512, (tb + 1) * 512)
                            ps = ps_mm.get()
                            for kc in range(16):
                                mm(ps, ps.t[:, :], w.t[:, kc, mi * 128:(mi + 1) * 128], merged.t[:, kc, tc], kc == 0, kc == 15, [w, merged])
                            act(so, so.t[:, tc], ps.t[:, :], AF.Copy, [ps])
                            sq = sq_p.get()
                            act(sq, sq.t[:, :], ps.t[:, :], AF.Square, [ps])
                            A_ = ACC0 if tb == 0 else ACC1
                            mm(A_, A_.t[:, :], ones_bf.t[:, :], sq.t[:, :], m == 0, m == 15, [ones_bf, sq])
                        c.dma("sp", fbuf.t[m * 128:(m + 1) * 128, :], so.t[:, :], reads=[so], writes=[fbuf])
                rs = rstd_p.get()
                rstd_from_ssq(ACC0, ACC0.t[:, :], rs, rs.t[:, 0:512], 1.0 / D)
                rstd_from_ssq(ACC1, ACC1.t[:, :], rs, rs.t[:, 512:1024], 1.0 / D)
                def ldC(m):
                    xl = xl_p.get()
                    c.dma("sp", xl.t[:, :], xs_ap[m * 128:(m + 1) * 128, c0:c0 + 1024], reads=[xs_dep], writes=[xl])
                    fl = fl_p.get()
                    c.dma("sp", fl.t[:, :], fbuf.t[m * 128:(m + 1) * 128, :], reads=[fbuf], writes=[fl])
                    return xl, fl
                nxt = ldC(0)
                for m in range(16):
                    xl, fl = nxt
                    if m + 1 < 16:
                        nxt = ldC(m + 1)
                    so = stg_f.get()
                    stt(so, so.t[:, :], fl.t[:, :], P.t[:, 2, m:m + 1], rs.t[:, :], ALU.mult, ALU.mult, [fl, rs] + A_reads)
                    tt(so, so.t[:, :], so.t[:, :], xl.t[:, :], ALU.add, [so, xl])
                    c.dma("sp", x1T.t[m * 128:(m + 1) * 128, c0:c0 + 1024], so.t[:, :], reads=[so], writes=[x1T])
            if DEBUG and is_p and l == 0:
                c.dma("sp", dbg_x1, x1T.t[:, 0:1024], reads=[x1T], writes=[d_out])
            AR.reset()
            NCOL = 520
            h2 = AR.tile([128, 16, NCOL], BF16)
            aT = AR.tile([128, 44, 512], BF16)
            xblk_pool = Pool([AR.tile([128, 16, 128], F32) for _ in range(1)])
            xblk_sq = Pool([AR.tile([128, 16, 128], BF16) for _ in range(1)])
            uv_p = Pool([AR.tile([128, NCOL], F32) for _ in range(2)])
            ug_p = Pool([AR.tile([128, NCOL], F32) for _ in range(2)])
            cv_p = Pool([AR.tile([128, 512], F32) for _ in range(4)])
            wD = Pool([AR.tile([128, 44, 128], BF16) for _ in range(2)])
            wU = Pool([AR.tile([128, 16, 256], BF16) for _ in range(4)])
            yo_ap, yo_dep = (x2T.t, x2T.d) if l == 0 else (G["yout"], d_out)
            nsegD = ntok // 512
            for sg in range(nsegD):
                c0 = sg * 512
                if is_p:
                    c.op("pool", lambda h, h2=h2: h.memset(h2.t[:, :, :], 0.0), writes=[h2])
                    for sq_i in range(2):
                        for q0 in range(0, 256, 128):
                            norm_mod(x1T.t, x1T.d, c0 + sq_i * 256 + q0, 128, h2, sq_i * 258 + 1 + q0, P.t[:, 3, :], P.t[:, 4, :], xblk_pool)
                    mmblocks = [(0, 258), (258, 258)]
                else:
                    lo = c0 - 1 if sg > 0 else c0
                    hi = c0 + 513 if sg < nsegD - 1 else c0 + 512
                    if sg == 0 or sg == nsegD - 1:
                        c.op("pool", lambda h, h2=h2: h.memset(h2.t[:, :, :], 0.0), writes=[h2])
                    q = lo
                    while q < hi:
                        n = min(128, hi - q)
                        norm_mod(x1T.t, x1T.d, q, n, h2, q - (c0 - 1), P.t[:, 3, :], P.t[:, 4, :], xblk_pool)
                        q += n
                    mmblocks = [(0, 512), (512, 2)]
                for hb in range(22):
                    wv_ = wU.get()
                    firstD = is_p and sg == 0
                    load_w(wv_, 16, 256, w_up[l, :, hb * 256:(hb + 1) * 256], wcU[l], hb, firstD)
                    wg_ = wU.get()
                    load_w(wg_, 16, 256, w_up[l, :, DFF + hb * 256:DFF + (hb + 1) * 256], wcU[l], 22 + hb, firstD)
                    for mi in range(2):
                        hm = hb * 2 + mi
                        uv, ug = uv_p.get(), ug_p.get()
                        for (wt, ut, eng) in ((wv_, uv, "act"), (wg_, ug, "dve")):
                            for (b0, bn) in mmblocks:
                                ps = ps_mm.get()
                                for kc in range(16):
                                    mm(ps, ps.t[:, 0:bn], wt.t[:, kc, mi * 128:(mi + 1) * 128], h2.t[:, kc, b0:b0 + bn], kc == 0, kc == 15, [wt, h2])
                                if eng == "act":
                                    act(ut, ut.t[:, b0:b0 + bn], ps.t[:, 0:bn], AF.Copy, [ps])
                                else:
                                    c.op("dve", lambda h, ut=ut, ps=ps, b0=b0, bn=bn: h.tensor_copy(ut.t[:, b0:b0 + bn], ps.t[:, 0:bn]), reads=[ps], writes=[ut])
                        cvs = []
                        for (ut, fm_) in ((uv, hm), (ug, 44 + hm)):
                            cv = cv_p.get()
                            w0, w1, w2 = (cvw[l].t[:, fm_, i:i + 1] for i in range(3))
                            bb = cvb[l].t[:, fm_:fm_ + 1]
                            if is_p:
                                u3 = ut.t[:, 0:516].rearrange("p (a b) -> p a b", b=258)
                                o3 = cv.t[:, :].rearrange("p (a b) -> p a b", b=256)
                                i0, i1, i2 = u3[:, :, 0:256], u3[:, :, 1:257], u3[:, :, 2:258]
                            else:
                                o3 = cv.t[:, :]
                                i0, i1, i2 = ut.t[:, 0:512], ut.t[:, 1:513], ut.t[:, 2:514]
                            ts(cv, o3, i1, w1, bb, ALU.mult, ALU.add, [ut, cvw[l], cvb[l]])
                            stt(cv, o3, i0, w0, o3, ALU.mult, ALU.add, [ut, cvw[l], cv])
                            stt(cv, o3, i2, w2, o3, ALU.mult, ALU.add, [ut, cvw[l], cv])
                            cvs.append(cv)
                        act(cvs[1], cvs[1].t[:, :], cvs[1].t[:, :], AF.Silu, [cvs[1]])
                        tt(aT, aT.t[:, hm, :], cvs[1].t[:, :], cvs[0].t[:, :], ALU.mult, [cvs[0], cvs[1]])
                for m in range(16):
                    w = wD.get()
                    load_w(w, 44, 128, w_down[l, :, m * 128:(m + 1) * 128], wcD[l], m, is_p and sg == 0)
                    so = stg_f.get()
                    ps = ps_mm.get()
                    for kc in range(44):
                        mm(ps, ps.t[:, :], w.t[:, kc, :], aT.t[:, kc, :], kc == 0, kc == 43, [w, aT])
                    act(so, so.t[:, 0:512], ps.t[:, :], AF.Copy, [ps])
                    sq = sq_p.get()
                    act(sq, sq.t[:, :], ps.t[:, :], AF.Square, [ps])
                    mm(ACC0, ACC0.t[:, :], ones_bf.t[:, :], sq.t[:, :], m == 0, m == 15, [ones_bf, sq])
                    c.dma("sp", fbuf.t[m * 128:(m + 1) * 128, 0:512], so.t[:, 0:512], reads=[so], writes=[fbuf])
                rs = rstd_p.get()
                rstd_from_ssq(ACC0, ACC0.t[:, :], rs, rs.t[:, 0:512], 1.0 / D)
                def ldD(m):
                    xl = cv_p.get()
                    c.dma("sp", xl.t[:, :], x1T.t[m * 128:(m + 1) * 128, c0:c0 + 512], reads=[x1T], writes=[xl])
                    fl = cv_p.get()
                    c.dma("sp", fl.t[:, :], fbuf.t[m * 128:(m + 1) * 128, 0:512], reads=[fbuf], writes=[fl])
                    return xl, fl
                nxt = ldD(0)
                for m in range(16):
                    xl, fl = nxt
                    if m + 1 < 16:
                        nxt = ldD(m + 1)
                    so = stg_f.get()
                    stt(so, so.t[:, 0:512], fl.t[:, :], P.t[:, 5, m:m + 1], rs.t[:, 0:512], ALU.mult, ALU.mult, [fl, rs] + A_reads)
                    tt(so, so.t[:, 0:512], so.t[:, 0:512], xl.t[:, :], ALU.add, [so, xl])
                    c.dma("sp", yo_ap[m * 128:(m + 1) * 128, c0:c0 + 512], so.t[:, 0:512], reads=[so], writes=[yo_dep])
            if DEBUG and is_p and l == 0:
                c.dma("sp", dbg_x2, x2T.t[:, 0:1024], reads=[x2T], writes=[d_out])
    c.finish()
    return nc, c


def host_inputs(I, core):
    f32 = np.float32
    b = core % 2
    A = lambda x: np.ascontiguousarray(x, dtype=f32)
    m = {}
    m["xTp"] = A(I["x_prompt"][4 * core:4 * core + 4].reshape(1024, D).T)
    m["xTs"] = A(I["x_sample"][b].T)
    conds = np.stack([I["c_ctx"], I["c"][b]], axis=-1)
    m["condT"] = A(conds.reshape(16, 128, 2).transpose(1, 0, 2))
    m["ada_w"] = A(I["ada_w"])
    m["ada_bT"] = A(I["ada_b"].reshape(NL, 96, 128).transpose(0, 2, 1))
    m["norm_gT"] = A(I["norm_g"].reshape(NL, 4, 16, 128).transpose(0, 3, 1, 2))
    m["w_in"] = A(I["w_in"])
    kcols = 6400 + np.concatenate([np.arange(0, 64, 2), np.arange(1, 64, 2)])
    m["w_in_kr"] = A(I["w_in"][:, :, kcols])

    def qj(a):
        sh = a.shape[:-2]
        return a.reshape(sh + (32, 2, 64)).reshape(sh + (32, 128)).swapaxes(-1, -2)
    ldt = np.broadcast_to(I["s5_log_dt"][..., None], I["s5_lam_re"].shape)
    m["s5_lam"] = A(np.stack([qj(I["s5_lam_re"]), qj(I["s5_lam_im"]), qj(ldt)], axis=2))
    nb = np.zeros((NL, 2, 128, 32, 32), f32)
    cbk = np.zeros((NL, 2, 128, 32, 32), f32)
    for r, (bsrc, csrc) in enumerate(((I["s5_b_re"], I["s5_c_re"]), (I["s5_b_im"], I["s5_c_im"]))):
        bb = bsrc.reshape(NL, 32, 2, 64, 16)
        cc = csrc.reshape(NL, 32, 2, 16, 64)
        for g2 in range(2):
            nb[:, r, g2 * 64:(g2 + 1) * 64, :, g2 * 16:(g2 + 1) * 16] = bb[:, :, g2].transpose(0, 2, 1, 3)
            cbk[:, r, g2 * 64:(g2 + 1) * 64, :, g2 * 16:(g2 + 1) * 16] = cc[:, :, g2].transpose(0, 3, 1, 2)
    m["s5_nb"] = nb
    m["s5_cb"] = cbk
    m["s5_dT"] = A(I["s5_d"].reshape(NL, 32, 32).transpose(0, 2, 1))
    s0 = np.stack([I["state_s5_fwd"][b], I["state_s5_bwd"][b]], axis=1)
    m["s5_s0"] = A(s0.reshape(NL, 2, 32, 128, 2).transpose(0, 1, 3, 2, 4))
    m["glu_w"] = A(I["s5_glu_w"])
    m["glu_bT"] = A(I["s5_glu_b"].reshape(NL, 8, 128).transpose(0, 2, 1))
    m["ret_dec"] = A(np.broadcast_to(I["ret_decay"].reshape(NL, 1, 16), (NL, 128, 16)))
    m["ret_ngT"] = A(I["ret_norm_g"].reshape(NL, 8, 128).transpose(0, 2, 1))
    m["ret_s0"] = A(np.stack([I["state_ret_fwd"][b], I["state_ret_bwd"][b]], axis=1))
    jj = np.arange(128)[:, None].astype(f32)
    ii = np.arange(128)[None, :].astype(f32)
    tab = np.stack([np.maximum(ii - jj, 0), (ii >= jj).astype(f32), np.maximum(jj - ii, 0), (jj >= ii).astype(f32),
                    np.broadcast_to(ii + 1, (128, 128)), np.broadcast_to(128 - ii, (128, 128))], axis=1)
    m["ret_tab"] = A(tab)
    pp = np.arange(128).astype(f32)
    m["ret_col"] = A(np.stack([127 - pp, pp, np.full(128, 128.0, f32)], axis=1))
    m["q_normT"] = A(I["mla_q_norm"].reshape(NL, 6, 128).transpose(0, 2, 1))
    m["kv_normT"] = A(I["mla_kv_norm"].reshape(NL, 4, 128).transpose(0, 2, 1))
    m["kv_norm_rep"] = A(np.broadcast_to(I["mla_kv_norm"][:, None, :], (NL, 128, 512)))
    hcols = np.concatenate([np.arange(128), 128 + np.arange(0, 64, 2), 128 + np.arange(1, 64, 2)])
    allc = np.concatenate([h * 192 + hcols for h in range(8)])
    m["w_uq"] = A(I["mla_w_uq"][:, :, allc])
    m["w_uk"] = A(I["mla_w_uk"])
    m["w_uv"] = A(I["mla_w_uv"])
    t = np.arange(LS)
    row = (t // 64).astype(f32)
    col = (t % 64).astype(f32)
    inv = (np.float32(10000.0) ** (-np.arange(16, dtype=f32) / np.float32(16))).astype(f32)
    ang = np.concatenate([row[:, None] * inv, col[:, None] * inv], axis=-1).astype(f32)
    m["rope_cs"] = A(np.stack([np.cos(ang).T, np.sin(ang).T]))
    m["cache_ckvT"] = A(I["cache_mla_ckv"][b].transpose(0, 2, 1))
    ck = I["cache_mla_krope"][b]
    m["cache_kr"] = A(np.stack([ck[:, :, 0::2], ck[:, :, 1::2]], axis=1).transpose(0, 1, 3, 2))
    m["w_branch"] = A(I["w_branch"])
    m["w_out"] = A(I["w_out"])
    m["w_up"] = A(I["ffn_w_up"])
    m["conv_wT"] = A(I["ffn_conv_w"].reshape(NL, 3, 88, 128).transpose(0, 3, 2, 1))
    m["conv_bT"] = A(I["ffn_conv_b"].reshape(NL, 88, 128).transpose(0, 2, 1))
    m["w_down"] = A(I["ffn_w_down"])
    return m


_CACHE = {}


def kernel(**inputs):
    I = {k: np.asarray(v) for k, v in inputs.items()}
    if "nc" not in _CACHE:
        _CACHE["nc"] = build()[0]
    nc = _CACHE["nc"]
    in_maps = [host_inputs(I, core) for core in range(8)]
    res = run_bass_kernel_spmd(nc, in_maps, core_ids=list(range(8)))
    R = res.results
    _CACHE["last"] = R
    y_prompt = np.concatenate([R[cidx]["yTp"].T.reshape(4, LP, D) for cidx in range(8)], axis=0)
    y_sample = np.stack([R[0]["yTs"].T, R[1]["yTs"].T], axis=0)
    ckv = np.concatenate([R[cidx]["ckv_o"] for cidx in range(8)], axis=0)
    krp = np.concatenate([R[cidx]["kr_o"] for cidx in range(8)], axis=0)
    retf = np.concatenate([R[cidx]["ret_o"][0] for cidx in range(8)], axis=0)
    retb = np.concatenate([R[cidx]["ret_o"][1] for cidx in range(8)], axis=0)

    def s5fix(a):
        a = a.transpose(0, 1, 4, 3, 2)
        return np.ascontiguousarray(a.reshape(a.shape[0], NL, 64, 64, 2))
    s5f = np.concatenate([s5fix(R[cidx]["s5_o"][0]) for cidx in range(8)], axis=0)
    s5b = np.concatenate([s5fix(R[cidx]["s5_o"][1]) for cidx in range(8)], axis=0)
    f = lambda a: np.ascontiguousarray(a, dtype=np.float32)
    return (f(y_prompt), f(y_sample), f(ckv), f(krp), f(retf), f(retb), f(s5f), f(s5b))
```

```python
import math
import numpy as np
import concourse.bass as bass
import concourse.mybir as mybir
from concourse.bass_utils import run_bass_kernel_spmd

F32 = mybir.dt.float32
BF16 = mybir.dt.bfloat16
AF = mybir.ActivationFunctionType
ALU = mybir.AluOpType

RING = 8
DEBUG = False
EPS = 1e-6
D = 2048
DM = 1024
NL = 2
LP = 256
NPS = 4
LS = 4096
PAST = 256
DFF = 5632
ZROWS = 11584
R_U, R_Q, R_K, R_G, R_CQ, R_CKV, R_KR, R_GATE = 0, 1024, 2048, 3072, 4096, 4864, 5376, 5440
ATT_SCALE = (128 + 64) ** -0.5
RET_SCALE = 128 ** -0.5
MAGIC = 12582912.0
TWO_PI = 2.0 * math.pi


class Dep:
    __slots__ = ("w", "r")

    def __init__(self):
        self.w = {}
        self.r = {}


class T:
    __slots__ = ("t", "d")

    def __init__(self, t, d=None):
        self.t = t
        self.d = d if d is not None else Dep()


class Ctx:
    def __init__(self, nc):
        self.nc = nc
        self.enames = ["pe", "act", "dve", "pool", "sp"]
        self.ops = {e: [] for e in self.enames}
        self.cnt = {e: 0 for e in self.enames}
        self.sem = {e: nc.alloc_semaphore("s_" + e) for e in self.enames}
        self.waited = {e: {} for e in self.enames}
        self.ring = {q: [nc.alloc_semaphore("r_%s%d" % (q, i)) for i in range(RING)] for q in ("sp", "pool", "act")}
        self.dman = {q: 0 for q in self.ring}
        self.nalloc = 0
        self.ninstr = 0

    def sb(self, shape, dt, name=None):
        self.nalloc += 1
        return T(self.nc.alloc_sbuf_tensor(name or ("sb%d" % self.nalloc), list(shape), dt))

    def semof(self, key):
        if key[0] == "E":
            return self.sem[key[1]]
        return self.ring[key[1]][key[2]]

    def _collect(self, e, reads, writes, extra=None):
        need = {}
        toks = []
        for d in reads:
            toks += list(d.w.items())
        for d in writes:
            toks += list(d.w.items())
            toks += list(d.r.items())
        if extra:
            toks += extra
        for key, (val, src) in toks:
            if src == "pe" and e == "pe" and key[0] == "E":
                continue
            if self.waited[e].get(key, 0) >= val:
                continue
            if need.get(key, 0) < val:
                need[key] = val
        for key, val in need.items():
            self.waited[e][key] = val
        return [(self.semof(k), v) for k, v in need.items()]

    def op(self, e, fn, reads=(), writes=()):
        reads = [x.d if isinstance(x, T) else x for x in reads]
        writes = [x.d if isinstance(x, T) else x for x in writes]
        wl = self._collect(e, reads, writes)
        self.cnt[e] += 1
        n = self.cnt[e]
        key = ("E", e)
        sem = self.sem[e]

        def emit(h):
            for sm, v in wl:
                h.wait_ge(sm, v)
            fn(h).then_inc(sem, 1)
        self.ops[e].append(emit)
        self.ninstr += 1 + len(wl)
        for d in reads:
            d.r[key] = (n, e)
        for d in writes:
            d.w[key] = (n, e)

    def dma(self, q, out, in_, reads=(), writes=(), **kw):
        reads = [x.d if isinstance(x, T) else x for x in reads]
        writes = [x.d if isinstance(x, T) else x for x in writes]
        i = self.dman[q]
        self.dman[q] += 1
        ri = i % RING
        val = 16 * (i // RING + 1)
        prev = 16 * (i // RING)
        key = ("D", q, ri)
        extra = [(key, (prev, None))] if prev > 0 else None
        wl = self._collect(q, reads, writes, extra)
        sem = self.ring[q][ri]

        def emit(h):
            for sm, v in wl:
                h.wait_ge(sm, v)
            h.dma_start(out=out, in_=in_, **kw).then_inc(sem, 16)
        self.ops[q].append(emit)
        self.ninstr += 1 + len(wl)
        for d in reads:
            d.r[key] = (val, None)
        for d in writes:
            d.w[key] = (val, None)

    def finish(self):
        finals = []
        for q in self.ring:
            n = self.dman[q]
            for ri in range(RING):
                cntr = (n - ri + RING - 1) // RING if n > ri else 0
                if cntr > 0:
                    finals.append((self.ring[q][ri], 16 * cntr))
        efinal = [(self.sem[e], self.cnt[e]) for e in self.enames if self.cnt[e] > 0]
        ops = self.ops
        with self.nc.Block() as block:
            @block.tensor
            def _(h):
                for f in ops["pe"]:
                    f(h)

            @block.scalar
            def _(h):
                for f in ops["act"]:
                    f(h)

            @block.vector
            def _(h):
                for f in ops["dve"]:
                    f(h)

            @block.gpsimd
            def _(h):
                for f in ops["pool"]:
                    f(h)

            @block.sync
            def _(h):
                for f in ops["sp"]:
                    f(h)
                for sm, v in efinal:
                    h.wait_ge(sm, v)
                for sm, v in finals:
                    h.wait_ge(sm, v)


class Pool:
    def __init__(self, tiles):
        self.tiles = tiles
        self.i = 0

    def get(self):
        t = self.tiles[self.i % len(self.tiles)]
        self.i += 1
        return t


def pat(a):
    return a.tensor, a.offset, [list(x) for x in a.ap]


def bc_mid(a, n):
    t, o, p = pat(a)
    return bass.AP(t, o, [p[0], [0, n]] + p[1:])


def bc_last(a, n):
    t, o, p = pat(a)
    return bass.AP(t, o, p + [[0, n]])


def rev(a):
    t, o, p = pat(a)
    st, n = p[1]
    return bass.AP(t, o + st * (n - 1), [p[0], [-st, n]])


def build(debug_stop=None):
    nc = bass.Bass("TRN2", target_bir_lowering=False)
    c = Ctx(nc)

    def din(name, shape, dt=F32):
        return nc.dram_tensor(name, list(shape), dt, kind="ExternalInput").ap()

    def dout(name, shape, dt=F32):
        return nc.dram_tensor(name, list(shape), dt, kind="ExternalOutput").ap()

    def dscr(name, shape, dt):
        return T(nc.dram_tensor(name, list(shape), dt, kind="Internal").ap())

    xTp = din("xTp", [D, 1024])
    xTs = din("xTs", [D, LS])
    condT = din("condT", [128, 16, 2])
    ada_w = din("ada_w", [NL, D, 6 * D])
    ada_bT = din("ada_bT", [NL, 128, 96])
    norm_gT = din("norm_gT", [NL, 128, 4, 16])
    w_in = din("w_in", [NL, D, 12608])
    w_in_kr = din("w_in_kr", [NL, D, 64])
    s5_lam = din("s5_lam", [NL, 2, 3, 128, 32])
    s5_nb = din("s5_nb", [NL, 2, 128, 32, 32])
    s5_cb = din("s5_cb", [NL, 2, 128, 32, 32])
    s5_dT = din("s5_dT", [NL, 32, 32])
    s5_s0 = din("s5_s0", [NL, 2, 128, 32, 2])
    glu_w = din("glu_w", [NL, DM, DM])
    glu_bT = din("glu_bT", [NL, 128, 8])
    ret_dec = din("ret_dec", [NL, 128, 16])
    ret_ngT = din("ret_ngT", [NL, 128, 8])
    ret_s0 = din("ret_s0", [NL, 2, 8, 128, 128])
    ret_tab = din("ret_tab", [128, 6, 128])
    ret_col = din("ret_col", [128, 3])
    q_normT = din("q_normT", [NL, 128, 6])
    kv_normT = din("kv_normT", [NL, 128, 4])
    kv_norm_rep = din("kv_norm_rep", [NL, 128, 512])
    w_uq = din("w_uq", [NL, 768, 1536])
    w_uk = din("w_uk", [NL, 512, 1024])
    w_uv = din("w_uv", [NL, 512, 1024])
    rope_cs = din("rope_cs", [2, 32, LS])
    cache_ckvT = din("cache_ckvT", [NL, 512, PAST])
    cache_kr = din("cache_kr", [NL, 2, 32, PAST])
    w_branch = din("w_branch", [NL, 3, DM, D])
    w_out = din("w_out", [NL, D, D])
    w_up = din("w_up", [NL, D, 2 * DFF])
    conv_wT = din("conv_wT", [NL, 128, 88, 3])
    conv_bT = din("conv_bT", [NL, 128, 88])
    w_down = din("w_down", [NL, DFF, D])
    yTp = dout("yTp", [D, 1024])
    yTs = dout("yTs", [D, LS])
    ckv_o = dout("ckv_o", [NPS, NL, LP, 512])
    kr_o = dout("kr_o", [NPS, NL, LP, 64])
    ret_o = dout("ret_o", [2, NPS, NL, 8, 128, 128])
    s5_o = dout("s5_o", [2, NPS, NL, 2, 128, 32])
    if DEBUG:
        dbg_y = dout("dbg_y", [3 * DM, 1024], BF16)
        dbg_x1 = dout("dbg_x1", [D, 1024])
        dbg_x2 = dout("dbg_x2", [D, 1024])
    zT = dscr("zT", [ZROWS, LS], BF16)
    ktok = dscr("ktok", [LS, DM], BF16)
    vtok = dscr("vtok", [LS, DM], BF16)
    ygT = dscr("ygT", [DM, LS], BF16)
    yT = dscr("yT", [3 * DM, LS], BF16)
    x1b = dscr("x1b", [D, LS], F32)
    x2b = dscr("x2b", [D, LS], F32)
    fbuf = dscr("fbuf", [D, 1024], F32)
    qscr = dscr("qscr", [8, 192, LS], BF16)
    d_xTp, d_xTs = Dep(), Dep()
    d_out = Dep()

    wcA = [dscr("wcA%d" % l, [32, 128, 16 * 512], BF16) for l in range(NL)]
    wcC = [dscr("wcC%d" % l, [16, 128, 16 * 512], BF16) for l in range(NL)]
    wcU = [dscr("wcU%d" % l, [44, 128, 16 * 256], BF16) for l in range(NL)]
    wcD = [dscr("wcD%d" % l, [16, 128, 44 * 128], BF16) for l in range(NL)]

    def load_w(w, nk, ncols, src_ap, cache, bi, first):
        cview = cache.t[bi, :, 0:nk * ncols].rearrange("p (k n) -> p k n", k=nk)
        if first:
            c.dma("pool", w.t[:, 0:nk, 0:ncols], src_ap.rearrange("(k p) n -> p k n", p=128), writes=[w])
            c.dma("sp", cview, w.t[:, 0:nk, 0:ncols], reads=[w], writes=[cache])
        else:
            c.dma("pool", w.t[:, 0:nk, 0:ncols], cview, reads=[cache], writes=[w])

    def dump(name, ap, shape, dt, reads):
        if DEBUG:
            o_ = dout("dbg_" + name, shape, dt)
            c.dma("sp", o_, ap, reads=reads, writes=[d_out])

    PS = [T(nc.alloc_psum_tensor("psb%d" % i, [128, 512], F32)) for i in range(8)]
    ps_mm = Pool(PS[0:4])
    ps_aux = Pool(PS[6:8])
    ACC0, ACC1 = PS[4], PS[5]

    ident = c.sb([128, 128], F32, "ident")
    c.op("pool", lambda h: h.memset(ident.t[:, :], 0.0), writes=[ident])
    c.op("pool", lambda h: h.affine_select(ident.t[:, :], ident.t[:, :], pattern=[[-1, 128]], compare_op=ALU.not_equal,
                                           fill=1.0, base=0, channel_multiplier=1), reads=[ident], writes=[ident])
    ones_bf = c.sb([128, 128], BF16, "ones_bf")
    c.op("pool", lambda h: h.memset(ones_bf.t[:, :], 1.0), writes=[ones_bf])
    ones128 = c.sb([128, 128], BF16, "ones128")
    c.op("pool", lambda h: h.memset(ones128.t[:, :], 1.0 / 128.0), writes=[ones128])
    eps_t = c.sb([128, 1], F32, "eps_t")
    c.op("pool", lambda h: h.memset(eps_t.t[:, :], EPS), writes=[eps_t])

    def barrier():
        toks = [(("E", e), (c.cnt[e], e)) for e in c.enames if c.cnt[e] > 0]
        for q in c.ring:
            n = c.dman[q]
            for ri in range(RING):
                cntr = (n - ri + RING - 1) // RING if n > ri else 0
                if cntr > 0:
                    toks.append((("D", q, ri), (16 * cntr, None)))
        d = Dep()
        for k, v in toks:
            d.w[k] = v
        for e in c.enames:
            if e in ("sp",):
                wl = c._collect(e, [d], [])

                def emit(h, wl=wl):
                    for sm, v in wl:
                        h.wait_ge(sm, v)
                c.ops[e].append(emit)
            elif e == "pe":
                wl = c._collect(e, [d], [])

                def emit(h, wl=wl):
                    for sm, v in wl:
                        h.wait_ge(sm, v)
                c.ops[e].append(emit)
            else:
                wl = c._collect(e, [d], [])

                def emit(h, wl=wl):
                    for sm, v in wl:
                        h.wait_ge(sm, v)
                c.ops[e].append(emit)

    def mm(ps, ps_ap, lhsT, rhs, start, stop, reads):
        c.op("pe", lambda h: h.matmul(ps_ap, lhsT, rhs, start=start, stop=stop), reads=reads, writes=[ps])

    def act(out_t, out_ap, in_ap, func, reads, bias=None, scale=None):
        kw = {}
        if bias is not None:
            kw["bias"] = bias
        if scale is not None:
            kw["scale"] = scale
        c.op("act", lambda h: h.activation(out_ap, in_ap, func, **kw), reads=reads, writes=[out_t])

    def tt(out_t, out_ap, a, b, op, reads, eng="dve"):
        c.op(eng, lambda h: h.tensor_tensor(out_ap, a, b, op), reads=reads, writes=[out_t])

    def ts(out_t, out_ap, a, s1, s2, op0, op1, reads, eng="dve"):
        if op1 is None:
            c.op(eng, lambda h: h.tensor_scalar(out_ap, a, s1, None, op0), reads=reads, writes=[out_t])
        else:
            c.op(eng, lambda h: h.tensor_scalar(out_ap, a, s1, s2, op0, op1), reads=reads, writes=[out_t])

    def stt(out_t, out_ap, a, s, b, op0, op1, reads):
        c.op("dve", lambda h: h.scalar_tensor_tensor(out_ap, a, s, b, op0, op1), reads=reads, writes=[out_t])

    def recip(out_t, out_ap, a, reads):
        c.op("dve", lambda h: h.reciprocal(out_ap, a), reads=reads, writes=[out_t])

    def rstd_from_ssq(ps, ps_ap, out_t, out_ap, inv_n):
        act(out_t, out_ap, ps_ap, AF.Sqrt, [ps, eps_t], bias=eps_t.t[:, 0:1], scale=inv_n)
        recip(out_t, out_ap, out_ap, [out_t])

    ARENA_BYTES = 146 * 1024
    arena = nc.alloc_sbuf_tensor("arena", [128, ARENA_BYTES], mybir.dt.uint8)

    class Arena:
        def __init__(self):
            self.off = 0

        def reset(self):
            barrier()
            self.off = 0

        def tile(self, shape, dt):
            nb = int(np.prod(shape[1:])) * mybir.dt.size(dt)
            nb_al = (nb + 31) // 32 * 32
            assert self.off + nb_al <= ARENA_BYTES, ("arena overflow", self.off, nb_al)
            P = shape[0]
            v = arena[0:P, self.off:self.off + nb].bitcast(dt)
            self.off += nb_al
            if len(shape) == 3:
                v = v.rearrange("p (a b) -> p a b", a=shape[1])
            elif len(shape) == 4:
                v = v.rearrange("p (a b c) -> p a b c", a=shape[1], b=shape[2])
            return T(v)

    AR = Arena()
    stg_bf = Pool([c.sb([128, 1024], BF16, "stgbf%d" % i) for i in range(3)])
    stg_f = Pool([c.sb([128, 1024], F32, "stgf%d" % i) for i in range(2)])
    sq_p = Pool([c.sb([128, 512], BF16, "sqp%d" % i) for i in range(3)])
    rstd_p = Pool([c.sb([128, 1024], F32, "rstd%d" % i) for i in range(2)])

    cond_f = c.sb([128, 16, 2], F32, "cond_f")
    cond_b = c.sb([128, 16, 2], BF16, "cond_b")
    c.dma("sp", cond_f.t[:, :, :], condT, writes=[cond_f])
    act(cond_b, cond_b.t[:, :, :], cond_f.t[:, :, :], AF.Silu, [cond_f])
    mods = [c.sb([128, 96, 2], F32, "mods%d" % l) for l in range(NL)]
    adab = [c.sb([128, 96], F32, "adab%d" % l) for l in range(NL)]
    ngs = [c.sb([128, 4, 16], F32, "ng%d" % l) for l in range(NL)]
    PR = [[c.sb([128, 6, 16], F32, "pr%d_%d" % (l, j)) for j in range(2)] for l in range(NL)]
    AR.reset()
    wA = Pool([AR.tile([128, 16, 512], BF16) for i in range(2)])
    for l in range(NL):
        c.dma("sp", adab[l].t[:, :], ada_bT[l], writes=[adab[l]])
        c.dma("sp", ngs[l].t[:, :, :], norm_gT[l], writes=[ngs[l]])
        for blk in range(24):
            w = wA.get()
            c.dma("pool", w.t[:, :, :], ada_w[l, :, blk * 512:(blk + 1) * 512].rearrange("(k p) n -> p k n", p=128), writes=[w])
            ps = ps_mm.get()
            for mi in range(4):
                for kc in range(16):
                    mm(ps, ps.t[:, mi * 2:mi * 2 + 2], w.t[:, kc, mi * 128:(mi + 1) * 128], cond_b.t[:, kc, :], kc == 0, kc == 15, [w, cond_b])
            for mi in range(4):
                m = blk * 4 + mi
                act(mods[l], mods[l].t[:, m, :], ps.t[:, mi * 2:mi * 2 + 2], AF.Identity, [ps, adab[l]], bias=adab[l].t[:, m:m + 1])
        for j in range(2):
            P = PR[l][j]
            M = mods[l]
            stt(P, P.t[:, 0, :], M.t[:, 16:32, j], 1.0, ngs[l].t[:, 0, :], ALU.add, ALU.mult, [M, ngs[l]])
            c.op("dve", lambda h, P=P, M=M, j=j: h.tensor_copy(P.t[:, 1, :], M.t[:, 0:16, j]), reads=[M], writes=[P])
            tt(P, P.t[:, 2, :], M.t[:, 32:48, j], ngs[l].t[:, 1, :], ALU.mult, [M, ngs[l]])
            stt(P, P.t[:, 3, :], M.t[:, 64:80, j], 1.0, ngs[l].t[:, 2, :], ALU.add, ALU.mult, [M, ngs[l]])
            c.op("dve", lambda h, P=P, M=M, j=j: h.tensor_copy(P.t[:, 4, :], M.t[:, 48:64, j]), reads=[M], writes=[P])
            tt(P, P.t[:, 5, :], M.t[:, 80:96, j], ngs[l].t[:, 3, :], ALU.mult, [M, ngs[l]])

    s5_mag = [[c.sb([128, 32], F32, "s5mag%d_%d" % (l, d)) for d in range(2)] for l in range(NL)]
    s5_dc = [[c.sb([128, 13, 32], F32, "s5dc%d_%d" % (l, d)) for d in range(2)] for l in range(NL)]
    s5_ds = [[c.sb([128, 13, 32], F32, "s5ds%d_%d" % (l, d)) for d in range(2)] for l in range(NL)]
    AR.reset()
    wb_d = dscr("wb_d", [NL, 2, 2, 32, 32, 128], BF16)
    wc_d = dscr("wc_d", [NL, 2, 128, 32, 32], BF16)
    wb_stage = Pool([AR.tile([32, 32, 128], BF16) for _ in range(2)])
    wc_stage = Pool([AR.tile([128, 32, 32], BF16) for _ in range(2)])
    s5_dsk = [c.sb([32, 32], F32, "s5dsk%d" % l) for l in range(NL)]
    s5_init = [[c.sb([128, 32, 2], F32, "s5init%d_%d" % (l, d)) for d in range(2)] for l in range(NL)]
    tmpP = Pool([AR.tile([128, 32], F32) for i in range(64)])
    nbig = Pool([AR.tile([128, 32, 32], F32) for i in range(12)])

    def range_reduce_sin(out_t, x_t, shift):
        a = tmpP.get()
        ts(a, a.t[:, :], x_t.t[:, :], shift, None, ALU.add, None, [x_t])
        n = tmpP.get()
        ts(n, n.t[:, :], a.t[:, :], 1.0 / TWO_PI, MAGIC, ALU.mult, ALU.add, [a])
        ts(n, n.t[:, :], n.t[:, :], MAGIC, None, ALU.subtract, None, [n])
        stt(a, a.t[:, :], n.t[:, :], -TWO_PI, a.t[:, :], ALU.mult, ALU.add, [n, a])
        ts(a, a.t[:, :], a.t[:, :], math.pi, -math.pi, ALU.min, ALU.max, [a])
        act(out_t, out_t.t[:, :], a.t[:, :], AF.Sin, [a])

    for l in range(NL):
        c.dma("sp", s5_dsk[l].t[:, :], s5_dT[l], writes=[s5_dsk[l]])
        nbr, nbi = nbig.get(), nbig.get()
        c.dma("sp", nbr.t[:, :, :], s5_nb[l, 0], writes=[nbr])
        c.dma("sp", nbi.t[:, :, :], s5_nb[l, 1], writes=[nbi])
        cr_, ci_ = nbig.get(), nbig.get()
        c.dma("sp", cr_.t[:, :, :], s5_cb[l, 0], writes=[cr_])
        c.dma("sp", ci_.t[:, :, :], s5_cb[l, 1], writes=[ci_])
        for r, (srcc, scl) in enumerate(((cr_, 1.0), (ci_, -1.0))):
            wcs = wc_stage.get()
            act(wcs, wcs.t[:, :, :], srcc.t[:, :, :], AF.Copy, [srcc], scale=scl)
            c.dma("sp", wc_d.t[l, r], wcs.t[:, :, :], reads=[wcs], writes=[wc_d])
        for d in range(2):
            c.dma("sp", s5_init[l][d].t[:, :, :], s5_s0[l, d], writes=[s5_init[l][d]])
            lr, li, ldt = tmpP.get(), tmpP.get(), tmpP.get()
            c.dma("sp", lr.t[:, :], s5_lam[l, d, 0], writes=[lr])
            c.dma("sp", li.t[:, :], s5_lam[l, d, 1], writes=[li])
            c.dma("sp", ldt.t[:, :], s5_lam[l, d, 2], writes=[ldt])
            dt_ = tmpP.get()
            act(dt_, dt_.t[:, :], ldt.t[:, :], AF.Exp, [ldt])
            ar, ai = tmpP.get(), tmpP.get()
            tt(ar, ar.t[:, :], lr.t[:, :], dt_.t[:, :], ALU.mult, [lr, dt_])
            tt(ai, ai.t[:, :], li.t[:, :], dt_.t[:, :], ALU.mult, [li, dt_])
            mag = s5_mag[l][d]
            act(mag, mag.t[:, :], ar.t[:, :], AF.Exp, [ar])
            dc, ds = s5_dc[l][d], s5_ds[l][d]
            cs, sn = tmpP.get(), tmpP.get()
            range_reduce_sin(cs, ai, math.pi / 2)
            range_reduce_sin(sn, ai, 0.0)
            c.op("dve", lambda h, dc=dc, cs=cs: h.tensor_copy(dc.t[:, 0, :], cs.t[:, :]), reads=[cs], writes=[dc])
            c.op("dve", lambda h, ds=ds, sn=sn: h.tensor_copy(ds.t[:, 0, :], sn.t[:, :]), reads=[sn], writes=[ds])
            for k in range(12):
                t1, t2 = tmpP.get(), tmpP.get()
                tt(t1, t1.t[:, :], dc.t[:, k, :], dc.t[:, k, :], ALU.mult, [dc])
                tt(t2, t2.t[:, :], ds.t[:, k, :], ds.t[:, k, :], ALU.mult, [ds])
                tt(dc, dc.t[:, k + 1, :], t1.t[:, :], t2.t[:, :], ALU.subtract, [t1, t2])
                stt(ds, ds.t[:, k + 1, :], dc.t[:, k, :], 2.0, ds.t[:, k, :], ALU.mult, ALU.mult, [dc, ds])
            abr, abi = tmpP.get(), tmpP.get()
            tt(abr, abr.t[:, :], mag.t[:, :], cs.t[:, :], ALU.mult, [mag, cs])
            tt(abi, abi.t[:, :], mag.t[:, :], sn.t[:, :], ALU.mult, [mag, sn])
            ts(abr, abr.t[:, :], abr.t[:, :], -1.0, None, ALU.add, None, [abr])
            den, t1, t2 = tmpP.get(), tmpP.get(), tmpP.get()
            tt(den, den.t[:, :], lr.t[:, :], lr.t[:, :], ALU.mult, [lr])
            tt(t1, t1.t[:, :], li.t[:, :], li.t[:, :], ALU.mult, [li])
            tt(den, den.t[:, :], den.t[:, :], t1.t[:, :], ALU.add, [den, t1])
            recip(den, den.t[:, :], den.t[:, :], [den])
            cre, cim = tmpP.get(), tmpP.get()
            tt(t1, t1.t[:, :], abr.t[:, :], lr.t[:, :], ALU.mult, [abr, lr])
            tt(t2, t2.t[:, :], abi.t[:, :], li.t[:, :], ALU.mult, [abi, li])
            tt(t1, t1.t[:, :], t1.t[:, :], t2.t[:, :], ALU.add, [t1, t2])
            tt(cre, cre.t[:, :], t1.t[:, :], den.t[:, :], ALU.mult, [t1, den])
            tt(t1, t1.t[:, :], abi.t[:, :], lr.t[:, :], ALU.mult, [abi, lr])
            tt(t2, t2.t[:, :], abr.t[:, :], li.t[:, :], ALU.mult, [abr, li])
            tt(t1, t1.t[:, :], t1.t[:, :], t2.t[:, :], ALU.subtract, [t1, t2])
            tt(cim, cim.t[:, :], t1.t[:, :], den.t[:, :], ALU.mult, [t1, den])
            bpr, bpi = nbig.get(), nbig.get()
            cre_b, cim_b = bc_last(cre.t[:, :], 32), bc_last(cim.t[:, :], 32)
            tt(bpr, bpr.t[:, :, :], nbr.t[:, :, :], cre_b, ALU.mult, [nbr, cre])
            tt(bpi, bpi.t[:, :, :], nbi.t[:, :, :], cim_b, ALU.mult, [nbi, cim])
            tt(bpr, bpr.t[:, :, :], bpr.t[:, :, :], bpi.t[:, :, :], ALU.subtract, [bpr, bpi])
            tt(bpi, bpi.t[:, :, :], nbi.t[:, :, :], cre_b, ALU.mult, [nbi, cre])
            t3 = nbig.get()
            tt(t3, t3.t[:, :, :], nbr.t[:, :, :], cim_b, ALU.mult, [nbr, cim])
            tt(bpi, bpi.t[:, :, :], bpi.t[:, :, :], t3.t[:, :, :], ALU.add, [bpi, t3])
            for r, src in ((0, bpr), (1, bpi)):
                wb = wb_stage.get()
                for j4 in range(8):
                    ps = ps_mm.get()
                    for jj in range(4):
                        j = j4 * 4 + jj
                        c.op("pe", lambda h, ps=ps, src=src, j=j, jj=jj: h.transpose(ps.t[0:32, jj * 128:(jj + 1) * 128], src.t[:, j, :], ident.t[:, :]),
                             reads=[src, ident], writes=[ps])
                    act(wb, wb.t[:, j4 * 4:(j4 + 1) * 4, :], ps.t[0:32, :].rearrange("p (a b) -> p a b", a=4), AF.Copy, [ps])
                c.dma("sp", wb_d.t[l, d, r], wb.t[:, :, :], reads=[wb], writes=[wb_d])

    rtab = c.sb([128, 6, 128], F32, "rtab")
    rcol = c.sb([128, 3], F32, "rcol")
    c.dma("sp", rtab.t[:, :, :], ret_tab, writes=[rtab])
    c.dma("sp", rcol.t[:, :], ret_col, writes=[rcol])
    r_lg = [c.sb([128, 16], F32, "rlg%d" % l) for l in range(NL)]
    r_ng = [c.sb([128, 8], F32, "rng%d" % l) for l in range(NL)]
    for l in range(NL):
        c.dma("sp", r_lg[l].t[:, :], ret_dec[l], writes=[r_lg[l]])
        c.dma("sp", r_ng[l].t[:, :], ret_ngT[l], writes=[r_ng[l]])
        act(r_lg[l], r_lg[l].t[:, :], r_lg[l].t[:, :], AF.Exp, [r_lg[l]])
        ts(r_lg[l], r_lg[l].t[:, :], r_lg[l].t[:, :], -1.0, None, ALU.mult, None, [r_lg[l]])

    def ret_tables(l):
        DT = AR.tile([128, 8, 128], F32)
        qp = AR.tile([128, 2, 8, 128], F32)
        kp = AR.tile([128, 2, 8], F32)
        dsc = AR.tile([128, 2, 8], F32)
        e1 = AR.tile([128, 128], F32)
        e2 = AR.tile([128, 128], F32)
        for hh in range(8):
            lgf = r_lg[l].t[:, hh:hh + 1]
            lgb = r_lg[l].t[:, 8 + hh:9 + hh]
            act(e1, e1.t[:, :], rtab.t[:, 0, :], AF.Exp, [rtab, r_lg[l]], scale=lgf)
            tt(e1, e1.t[:, :], e1.t[:, :], rtab.t[:, 1, :], ALU.mult, [e1, rtab])
            act(e2, e2.t[:, :], rtab.t[:, 2, :], AF.Exp, [rtab, r_lg[l]], scale=lgb)
            tt(e2, e2.t[:, :], e2.t[:, :], rtab.t[:, 3, :], ALU.mult, [e2, rtab])
            tt(e1, e1.t[:, :], e1.t[:, :], e2.t[:, :], ALU.add, [e1, e2])
            ts(DT, DT.t[:, hh, :], e1.t[:, :], RET_SCALE, None, ALU.mult, None, [e1])
            act(qp, qp.t[:, 0, hh, :], rtab.t[:, 4, :], AF.Exp, [rtab, r_lg[l]], scale=lgf)
            act(qp, qp.t[:, 1, hh, :], rtab.t[:, 5, :], AF.Exp, [rtab, r_lg[l]], scale=lgb)
            act(kp, kp.t[:, 0, hh:hh + 1], rcol.t[:, 0:1], AF.Exp, [rcol, r_lg[l]], scale=lgf)
            act(kp, kp.t[:, 1, hh:hh + 1], rcol.t[:, 1:2], AF.Exp, [rcol, r_lg[l]], scale=lgb)
            act(dsc, dsc.t[:, 0, hh:hh + 1], rcol.t[:, 2:3], AF.Exp, [rcol, r_lg[l]], scale=lgf)
            act(dsc, dsc.t[:, 1, hh:hh + 1], rcol.t[:, 2:3], AF.Exp, [rcol, r_lg[l]], scale=lgb)
        ts(kp, kp.t[:, :, :], kp.t[:, :, :], RET_SCALE, None, ALU.mult, None, [kp])
        return DT, qp, kp, dsc

    glub = [c.sb([128, 8], F32, "glub%d" % l) for l in range(NL)]
    qn_t = [c.sb([128, 6], F32, "qn%d" % l) for l in range(NL)]
    kvn_t = [c.sb([128, 4], F32, "kvn%d" % l) for l in range(NL)]
    kvn_rep = [c.sb([128, 512], F32, "kvnr%d" % l) for l in range(NL)]
    cvw = [c.sb([128, 88, 3], F32, "cvw%d" % l) for l in range(NL)]
    cvb = [c.sb([128, 88], F32, "cvb%d" % l) for l in range(NL)]
    for l in range(NL):
        c.dma("sp", glub[l].t[:, :], glu_bT[l], writes=[glub[l]])
        c.dma("sp", qn_t[l].t[:, :], q_normT[l], writes=[qn_t[l]])
        c.dma("sp", kvn_t[l].t[:, :], kv_normT[l], writes=[kvn_t[l]])
        c.dma("sp", kvn_rep[l].t[:, :], kv_norm_rep[l], writes=[kvn_rep[l]])
        c.dma("sp", cvw[l].t[:, :, :], conv_wT[l], writes=[cvw[l]])
        c.dma("sp", cvb[l].t[:, :], conv_bT[l], writes=[cvb[l]])

    def norm_mod(xsrc, xdep, src0, n, hbuf, dst0, Aap, Bap, xblk_pool):
        xb = xblk_pool.get()
        kw = {"allow_slow_non_contiguous": True} if n == 1 else {}
        c.dma("sp", xb.t[:, :, 0:n], xsrc[:, src0:src0 + n].rearrange("(k p) n -> p k n", p=128), reads=[xdep], writes=[xb], **kw)
        sq = xblk_sq.get()
        act(sq, sq.t[:, :, 0:n], xb.t[:, :, 0:n], AF.Square, [xb])
        ps = ps_aux.get()
        for kc in range(16):
            mm(ps, ps.t[:, 0:n], ones_bf.t[:, :], sq.t[:, kc, 0:n], kc == 0, kc == 15, [ones_bf, sq])
        rs = rstd_p.get()
        rstd_from_ssq(ps, ps.t[:, 0:n], rs, rs.t[:, 0:n], 1.0 / D)
        tt(xb, xb.t[:, :, 0:n], xb.t[:, :, 0:n], bc_last(Aap, n), ALU.mult, [xb] + A_reads)
        tt(xb, xb.t[:, :, 0:n], xb.t[:, :, 0:n], bc_mid(rs.t[:, 0:n], 16), ALU.mult, [xb, rs])
        tt(hbuf, hbuf.t[:, :, dst0:dst0 + n], xb.t[:, :, 0:n], bc_last(Bap, n), ALU.add, [xb] + A_reads)

    A_reads = [PR[l][j] for l in range(NL) for j in range(2)]

    groups = [
        dict(name="p", cond=0, xin=xTp, xin_dep=d_xTp, ntok=1024, L=LP, nseq=NPS, yout=yTp),
        dict(name="s", cond=1, xin=xTs, xin_dep=d_xTs, ntok=LS, L=LS, nseq=1, yout=yTs),
    ]
    x1T, x2T = x1b, x2b

    for G in groups:
        is_p = G["name"] == "p"
        L, nseq, ntok, cj = G["L"], G["nseq"], G["ntok"], G["cond"]
        nseg = ntok // 1024
        for l in range(NL):
            P = PR[l][cj]
            if l == 0:
                xs_ap, xs_dep = G["xin"], G["xin_dep"]
            else:
                xs_ap, xs_dep = x2T.t, x2T.d
            AR.reset()
            hT = AR.tile([128, 16, 1024], BF16)
            xblk_pool = Pool([AR.tile([128, 16, 256], F32) for _ in range(2)])
            xblk_sq = Pool([AR.tile([128, 16, 256], BF16) for _ in range(2)])
            stok = Pool([AR.tile([128, 512], BF16) for _ in range(3)])
            stokf = Pool([AR.tile([128, 512], F32) for _ in range(3)])
            small = Pool([AR.tile([128, 2], F32) for _ in range(4)])
            wA = Pool([AR.tile([128, 16, 512], BF16) for _ in range(2)])
            for sg in range(nseg):
                c0 = sg * 1024
                for b4 in range(4):
                    norm_mod(xs_ap, xs_dep, c0 + b4 * 256, 256, hT, b4 * 256, P.t[:, 0, :], P.t[:, 1, :], xblk_pool)
                fm = []
                for cb in range(6):
                    fm.append((w_in[l], cb * 512, 512, cb * 512, "copy", True))
                for cb in range(4):
                    fm.append((w_in[l], 4096 + cb * 512, 512, R_G + cb * 512, "copy", True))
                fm.append((w_in[l], 4096 + 2048, 256, R_G + 2048, "copy", True))
                fm.append((w_in_kr[l], 0, 64, R_KR, "copy", False))
                for cb in range(12):
                    fm.append((w_in[l], 6464 + cb * 512, 512, R_GATE + cb * 512, "sig", True))
                tokm = [(2048, 512, ktok, 0), (2560, 512, ktok, 512), (3072, 512, vtok, 0), (3584, 512, vtok, 512)]
                if is_p:
                    tokm += [(5888, 512, "ckv", 0), (6400, 64, "kr", 0)]
                tok_by_col = {t[0]: t for t in tokm}
                for bi, (wsrc, col0, ncols, zr0, epi, is_main) in enumerate(fm):
                    w = wA.get()
                    load_w(w, 16, ncols, wsrc[:, col0:col0 + ncols], wcA[l], bi, is_p)
                    nm = (ncols + 127) // 128
                    for mi in range(nm):
                        mw = min(128, ncols - mi * 128)
                        st = stg_bf.get()
                        for tb in range(2):
                            ps = ps_mm.get()
                            for kc in range(16):
                                mm(ps, ps.t[0:mw, :], w.t[:, kc, mi * 128:mi * 128 + mw], hT.t[:, kc, tb * 512:(tb + 1) * 512], kc == 0, kc == 15, [w, hT])
                            if epi == "sig":
                                act(st, st.t[0:mw, tb * 512:(tb + 1) * 512], ps.t[0:mw, :], AF.Sigmoid, [ps])
                            elif (mi + tb) % 2 == 0:
                                act(st, st.t[0:mw, tb * 512:(tb + 1) * 512], ps.t[0:mw, :], AF.Copy, [ps])
                            else:
                                c.op("dve", lambda h, st=st, ps=ps, mw=mw, tb=tb: h.tensor_copy(st.t[0:mw, tb * 512:(tb + 1) * 512], ps.t[0:mw, :]), reads=[ps], writes=[st])
                        c.dma("sp", zT.t[zr0 + mi * 128:zr0 + mi * 128 + mw, c0:c0 + 1024], st.t[0:mw, :], reads=[st], writes=[zT])
                    if is_main and col0 in tok_by_col:
                        _, _, dst, dcol = tok_by_col.pop(col0)
                        for tti in range(8):
                            ps = ps_mm.get()
                            for kc in range(16):
                                mm(ps, ps.t[:, 0:ncols], hT.t[:, kc, tti * 128:(tti + 1) * 128], w.t[:, kc, 0:ncols], kc == 0, kc == 15, [w, hT])
                            so = stok.get()
                            act(so, so.t[:, 0:ncols], ps.t[:, 0:ncols], AF.Copy, [ps])
                            c.dma("sp", dst.t[c0 + tti * 128:c0 + (tti + 1) * 128, dcol:dcol + ncols], so.t[:, 0:ncols], reads=[so], writes=[dst])
                for li_, (col0, ncols, dst, dcol) in enumerate(list(tok_by_col.values())):
                    w = wA.get()
                    if isinstance(dst, str):
                        c.dma("pool", w.t[:, :, 0:ncols], w_in[l, :, col0:col0 + ncols].rearrange("(k p) n -> p k n", p=128), writes=[w])
                    else:
                        load_w(w, 16, ncols, w_in[l, :, col0:col0 + ncols], wcA[l], 29 + li_, is_p)
                    for tti in range(8):
                        ps = ps_mm.get()
                        for kc in range(16):
                            mm(ps, ps.t[:, 0:ncols], hT.t[:, kc, tti * 128:(tti + 1) * 128], w.t[:, kc, 0:ncols], kc == 0, kc == 15, [w, hT])
                        if dst == "ckv":
                            sqf = stokf.get()
                            ss = small.get()
                            c.op("act", lambda h, sqf=sqf, ps=ps, ss=ss: h.activation(sqf.t[:, :], ps.t[:, :], AF.Square, accum_out=ss.t[:, 0:1]),
                                 reads=[ps], writes=[sqf, ss])
                            act(ss, ss.t[:, 1:2], ss.t[:, 0:1], AF.Sqrt, [ss, eps_t], bias=eps_t.t[:, 0:1], scale=1.0 / 512)
                            recip(ss, ss.t[:, 1:2], ss.t[:, 1:2], [ss])
                            so = stokf.get()
                            stt(so, so.t[:, :], ps.t[:, :], ss.t[:, 1:2], kvn_rep[l].t[:, :], ALU.mult, ALU.mult, [ps, ss, kvn_rep[l]])
                            sq_i, t0 = (tti * 128) // LP, (tti * 128) % LP
                            c.dma("sp", ckv_o[sq_i, l, t0:t0 + 128, :], so.t[:, :], reads=[so], writes=[d_out])
                        elif dst == "kr":
                            so = stokf.get()
                            act(so, so.t[:, 0:64], ps.t[:, 0:64], AF.Copy, [ps])
                            sq_i, t0 = (tti * 128) // LP, (tti * 128) % LP
                            c.dma("sp", kr_o[sq_i, l, t0:t0 + 128, :], so.t[:, 0:64], reads=[so], writes=[d_out])
                        else:
                            so = stok.get()
                            act(so, so.t[:, 0:ncols], ps.t[:, 0:ncols], AF.Copy, [ps])
                            c.dma("sp", dst.t[c0 + tti * 128:c0 + (tti + 1) * 128, dcol:dcol + ncols], so.t[:, 0:ncols], reads=[so], writes=[dst])
            if debug_stop == "A":
                break
            AR.reset()
            HL = L // 2
            Ec = AR.tile([128, L], F32)
            Es = AR.tile([128, L], F32)
            tq = AR.tile([128, HL], F32)
            btil = [AR.tile([128, L], F32) for _ in range(2)]
            sbf = [[AR.tile([128, ntok], BF16) for _ in range(2)] for _ in range(2)]
            u_pool = Pool([AR.tile([32, ntok], BF16) for _ in range(1)])
            tmp32 = Pool([AR.tile([128, 512], F32) for _ in range(4)])
            yj_p = Pool([AR.tile([32, 512], F32) for _ in range(2)])
            g_p = Pool([AR.tile([32, 512], F32) for _ in range(3)])
            wbj = Pool([AR.tile([32, 128], BF16) for _ in range(8)])
            wcj = Pool([AR.tile([128, 32], BF16) for _ in range(4)])
            fin = [[AR.tile([128, 2, 32], F32) for _ in range(2)] for _ in range(nseq)] if is_p else None
            fin_tmp = Pool([AR.tile([128, 2], F32) for _ in range(4)])
            BLK = min(512, L)
            br_t, bi_t = btil
            for j in range(32):
                uj = u_pool.get()
                c.dma("sp", uj.t[:, :], zT.t[R_U + j * 32:R_U + (j + 1) * 32, 0:ntok], reads=[zT], writes=[uj])
                for d in range(2):
                    wbr, wbi = wbj.get(), wbj.get()
                    c.dma("sp", wbr.t[:, :], wb_d.t[l, d, 0, :, j, :], reads=[wb_d], writes=[wbr])
                    c.dma("sp", wbi.t[:, :], wb_d.t[l, d, 1, :, j, :], reads=[wb_d], writes=[wbi])
                    dc, ds = s5_dc[l][d], s5_ds[l][d]
                    c.op("dve", lambda h, dc=dc, j=j, Ec=Ec: h.tensor_copy(Ec.t[:, 0:1], dc.t[:, 0, j:j + 1]), reads=[dc], writes=[Ec])
                    c.op("dve", lambda h, ds=ds, j=j, Es=Es: h.tensor_copy(Es.t[:, 0:1], ds.t[:, 0, j:j + 1]), reads=[ds], writes=[Es])
                    n = 1
                    k = 0
                    while n < L:
                        Ck, Sk = dc.t[:, k, j:j + 1], ds.t[:, k, j:j + 1]
                        ts(tq, tq.t[:, 0:n], Es.t[:, 0:n], Sk, None, ALU.mult, None, [Es, ds])
                        stt(Ec, Ec.t[:, n:2 * n], Ec.t[:, 0:n], Ck, tq.t[:, 0:n], ALU.mult, ALU.subtract, [Ec, dc, tq])
                        ts(tq, tq.t[:, 0:n], Ec.t[:, 0:n], Sk, None, ALU.mult, None, [Ec, ds])
                        stt(Es, Es.t[:, n:2 * n], Es.t[:, 0:n], Ck, tq.t[:, 0:n], ALU.mult, ALU.add, [Es, dc, tq])
                        n *= 2
                        k += 1
                    mg = s5_mag[l][d].t[:, j:j + 1]
                    magb = bass.AP(mg.tensor, mg.offset, [list(mg.ap[0]), [0, L]])
                    sre, sim = sbf[d]
                    for s in range(nseq):
                        s0 = s * L
                        for blk in range(L // BLK):
                            cols = slice(s0 + blk * BLK, s0 + (blk + 1) * BLK)
                            if d == 0:
                                tcols = slice(blk * BLK, (blk + 1) * BLK)
                            else:
                                tcols = slice(L - (blk + 1) * BLK, L - blk * BLK)
                            pr, pi = ps_mm.get(), ps_mm.get()
                            mm(pr, pr.t[:, 0:BLK], wbr.t[:, :], uj.t[:, cols], True, True, [wbr, uj])
                            mm(pi, pi.t[:, 0:BLK], wbi.t[:, :], uj.t[:, cols], True, True, [wbi, uj])
                            prv = pr.t[:, 0:BLK] if d == 0 else rev(pr.t[:, 0:BLK])
                            piv = pi.t[:, 0:BLK] if d == 0 else rev(pi.t[:, 0:BLK])
                            t1, t2 = tmp32.get(), tmp32.get()
                            tt(t1, t1.t[:, 0:BLK], prv, Ec.t[:, tcols], ALU.mult, [pr, Ec])
                            tt(t2, t2.t[:, 0:BLK], piv, Es.t[:, tcols], ALU.mult, [pi, Es])
                            tt(br_t, br_t.t[:, tcols], t1.t[:, 0:BLK], t2.t[:, 0:BLK], ALU.add, [t1, t2])
                            t3, t4 = tmp32.get(), tmp32.get()
                            tt(t3, t3.t[:, 0:BLK], piv, Ec.t[:, tcols], ALU.mult, [pi, Ec])
                            tt(t4, t4.t[:, 0:BLK], prv, Es.t[:, tcols], ALU.mult, [pr, Es])
                            tt(bi_t, bi_t.t[:, tcols], t3.t[:, 0:BLK], t4.t[:, 0:BLK], ALU.subtract, [t3, t4])
                        if is_p and l == 0 and j == 0 and s == 0:
                            dump("bt_r%d" % d, br_t.t[:, :], [128, L], F32, [br_t])
                            dump("bt_i%d" % d, bi_t.t[:, :], [128, L], F32, [bi_t])
                        for ri, bt in enumerate(btil):
                            if is_p:
                                init = 0.0
                                rd = [bt, s5_mag[l][d]]
                            else:
                                init = s5_init[l][d].t[:, j, ri:ri + 1]
                                rd = [bt, s5_mag[l][d], s5_init[l][d]]
                            c.op("dve", lambda h, bt=bt, magb=magb, init=init: h.tensor_tensor_scan(bt.t[:, :], magb, bt.t[:, :], init, ALU.mult, ALU.add),
                                 reads=rd, writes=[bt])
                        if is_p and l == 0 and j == 0 and s == 0:
                            dump("Ec%d" % d, Ec.t[:, :], [128, L], F32, [Ec])
                            dump("Es%d" % d, Es.t[:, :], [128, L], F32, [Es])
                            dump("sr%d" % d, br_t.t[:, :], [128, L], F32, [br_t])
                            dump("si%d" % d, bi_t.t[:, :], [128, L], F32, [bi_t])
                        for q0 in range(0, L, 512):
                            qn_ = min(512, L - q0)
                            tc = slice(q0, q0 + qn_)
                            if d == 0:
                                ore, oim = sre.t[:, s0 + q0:s0 + q0 + qn_], sim.t[:, s0 + q0:s0 + q0 + qn_]
                            else:
                                ore = rev(sre.t[:, s0 + L - q0 - qn_:s0 + L - q0])
                                oim = rev(sim.t[:, s0 + L - q0 - qn_:s0 + L - q0])
                            t1, t2 = tmp32.get(), tmp32.get()
                            tt(t1, t1.t[:, 0:qn_], br_t.t[:, tc], Ec.t[:, tc], ALU.mult, [br_t, Ec])
                            tt(t2, t2.t[:, 0:qn_], bi_t.t[:, tc], Es.t[:, tc], ALU.mult, [bi_t, Es])
                            tt(sre, ore, t1.t[:, 0:qn_], t2.t[:, 0:qn_], ALU.subtract, [t1, t2])
                            t3, t4 = tmp32.get(), tmp32.get()
                            tt(t3, t3.t[:, 0:qn_], bi_t.t[:, tc], Ec.t[:, tc], ALU.mult, [bi_t, Ec])
                            tt(t4, t4.t[:, 0:qn_], br_t.t[:, tc], Es.t[:, tc], ALU.mult, [br_t, Es])
                            tt(sim, oim, t3.t[:, 0:qn_], t4.t[:, 0:qn_], ALU.add, [t3, t4])
                        if is_p:
                            ft = fin_tmp.get()
                            f_ = fin[s][d]
                            e_c, e_s = Ec.t[:, L - 1:L], Es.t[:, L - 1:L]
                            tt(ft, ft.t[:, 0:1], br_t.t[:, L - 1:L], e_c, ALU.mult, [br_t, Ec])
                            tt(ft, ft.t[:, 1:2], bi_t.t[:, L - 1:L], e_s, ALU.mult, [bi_t, Es])
                            tt(f_, f_.t[:, 0, j:j + 1], ft.t[:, 0:1], ft.t[:, 1:2], ALU.subtract, [ft])
                            ft = fin_tmp.get()
                            tt(ft, ft.t[:, 0:1], bi_t.t[:, L - 1:L], e_c, ALU.mult, [bi_t, Ec])
                            tt(ft, ft.t[:, 1:2], br_t.t[:, L - 1:L], e_s, ALU.mult, [br_t, Es])
                            tt(f_, f_.t[:, 1, j:j + 1], ft.t[:, 0:1], ft.t[:, 1:2], ALU.add, [ft])
                wcr, wci = wcj.get(), wcj.get()
                c.dma("sp", wcr.t[:, :], wc_d.t[l, 0, :, j, :], reads=[wc_d], writes=[wcr])
                c.dma("sp", wci.t[:, :], wc_d.t[l, 1, :, j, :], reads=[wc_d], writes=[wci])
                for blk in range(ntok // 512):
                    cols = slice(blk * 512, (blk + 1) * 512)
                    ps = ps_mm.get()
                    mm(ps, ps.t[0:32, :], wcr.t[:, :], sbf[0][0].t[:, cols], True, False, [wcr, sbf[0][0]])
                    mm(ps, ps.t[0:32, :], wci.t[:, :], sbf[0][1].t[:, cols], False, False, [wci, sbf[0][1]])
                    mm(ps, ps.t[0:32, :], wcr.t[:, :], sbf[1][0].t[:, cols], False, False, [wcr, sbf[1][0]])
                    mm(ps, ps.t[0:32, :], wci.t[:, :], sbf[1][1].t[:, cols], False, True, [wci, sbf[1][1]])
                    yj = yj_p.get()
                    stt(yj, yj.t[:, :], uj.t[:, cols], s5_dsk[l].t[:, j:j + 1], ps.t[0:32, :], ALU.mult, ALU.add, [uj, s5_dsk[l], ps])
                    g1_ = g_p.get()
                    tt(g1_, g1_.t[:, :], yj.t[:, :], yj.t[:, :], ALU.mult, [yj])
                    ts(g1_, g1_.t[:, :], g1_.t[:, :], 0.044715, 1.0, ALU.mult, ALU.add, [g1_])
                    tt(g1_, g1_.t[:, :], g1_.t[:, :], yj.t[:, :], ALU.mult, [g1_, yj])
                    act(g1_, g1_.t[:, :], g1_.t[:, :], AF.Sigmoid, [g1_], scale=2.0 * math.sqrt(2.0 / math.pi))
                    so = stg_bf.get()
                    tt(so, so.t[0:32, 0:512], g1_.t[:, :], yj.t[:, :], ALU.mult, [g1_, yj])
                    c.dma("sp", ygT.t[j * 32:(j + 1) * 32, cols], so.t[0:32, 0:512], reads=[so], writes=[ygT])
            if is_p:
                for s in range(nseq):
                    for d in range(2):
                        c.dma("sp", s5_o[d, s, l].rearrange("r q j -> q r j"), fin[s][d].t[:, :, :], reads=[fin[s][d]], writes=[d_out])
            AR.reset()
            wg = AR.tile([128, 8, 1024], BF16)
            c.dma("pool", wg.t[:, :, :], glu_w[l].rearrange("(k p) n -> p k n", p=128), writes=[wg])
            yg_p = Pool([AR.tile([128, 8, 512], BF16) for _ in range(2)])
            sg_p = Pool([AR.tile([128, 512], F32) for _ in range(3)])
            for tb in range(ntok // 512):
                cols = slice(tb * 512, (tb + 1) * 512)
                yg = yg_p.get()
                c.dma("sp", yg.t[:, :, :], ygT.t[:, cols].rearrange("(k p) n -> p k n", p=128), reads=[ygT], writes=[yg])
                for m in range(8):
                    ps = ps_mm.get()
                    for kc in range(8):
                        mm(ps, ps.t[:, :], wg.t[:, kc, m * 128:(m + 1) * 128], yg.t[:, kc, :], kc == 0, kc == 7, [wg, yg])
                    sg = sg_p.get()
                    act(sg, sg.t[:, :], ps.t[:, :], AF.Sigmoid, [ps, glub[l]], bias=glub[l].t[:, m:m + 1])
                    so = stg_bf.get()
                    tt(so, so.t[:, 0:512], sg.t[:, :], yg.t[:, m, :], ALU.mult, [sg, yg])
                    c.dma("sp", yT.t[m * 128:(m + 1) * 128, cols], so.t[:, 0:512], reads=[so], writes=[yT])
            AR.reset()
            NCH = L // 128
            rDT, rqp, rkp, rds = ret_tables(l)
            qT_ = AR.tile([128, ntok], BF16)
            kT_ = AR.tile([128, ntok], BF16)
            qf_ = AR.tile([128, ntok], BF16)
            qb_ = AR.tile([128, ntok], BF16)
            kt_ = AR.tile([128, ntok // 128, 128], BF16)
            vt_ = AR.tile([128, ntok // 128, 128], BF16)
            kf_ = AR.tile([128, ntok // 128, 128], BF16)
            kb_ = AR.tile([128, ntok // 128, 128], BF16)
            acc = AR.tile([128, ntok], F32)
            Sst = [[AR.tile([128, 128], F32) for _ in range(2)] for _ in range(2)]
            Sbf = Pool([AR.tile([128, 128], BF16) for _ in range(3)])
            pt_p = Pool([AR.tile([128, 128], BF16) for _ in range(3)])
            gt_p = Pool([AR.tile([128, 512], BF16) for _ in range(2)])
            w32 = Pool([AR.tile([128, 512], F32) for _ in range(6)])
            obf_p = Pool([AR.tile([128, 512], BF16) for _ in range(2)])
            for hh in range(8):
                c.dma("sp", qT_.t[:, :], zT.t[R_Q + hh * 128:R_Q + (hh + 1) * 128, 0:ntok], reads=[zT], writes=[qT_])
                c.dma("sp", kT_.t[:, :], zT.t[R_K + hh * 128:R_K + (hh + 1) * 128, 0:ntok], reads=[zT], writes=[kT_])
                c.dma("sp", kt_.t[:, :, :], ktok.t[0:ntok, hh * 128:(hh + 1) * 128].rearrange("(c j) d -> j c d", j=128), reads=[ktok], writes=[kt_])
                c.dma("sp", vt_.t[:, :, :], vtok.t[0:ntok, hh * 128:(hh + 1) * 128].rearrange("(c j) d -> j c d", j=128), reads=[vtok], writes=[vt_])
                q3 = qT_.t[:, :].rearrange("p (a b) -> p a b", b=128)
                tt(qf_, qf_.t[:, :].rearrange("p (a b) -> p a b", b=128), q3, bc_mid(rqp.t[:, 0, hh, :], ntok // 128), ALU.mult, [qT_, rqp])
                tt(qb_, qb_.t[:, :].rearrange("p (a b) -> p a b", b=128), q3, bc_mid(rqp.t[:, 1, hh, :], ntok // 128), ALU.mult, [qT_, rqp])
                ts(kf_, kf_.t[:, :, :], kt_.t[:, :, :], rkp.t[:, 0, hh:hh + 1], None, ALU.mult, None, [kt_, rkp])
                ts(kb_, kb_.t[:, :, :], kt_.t[:, :, :], rkp.t[:, 1, hh:hh + 1], None, ALU.mult, None, [kt_, rkp])
                for s in range(nseq):
                    for d in range(2):
                        Sd = Sst[d]
                        S = Sd[0]
                        if is_p:
                            c.op("pool", lambda h, S=S: h.memset(S.t[:, :], 0.0), writes=[S])
                        else:
                            c.dma("sp", S.t[:, :], ret_s0[l, d, hh], writes=[S])
                        order = range(NCH) if d == 0 else range(NCH - 1, -1, -1)
                        qd = qf_ if d == 0 else qb_
                        kd = kf_ if d == 0 else kb_
                        for it_, ch in enumerate(order):
                            gc = s * NCH + ch
                            cs_ = slice(gc * 128, (gc + 1) * 128)
                            cur, nxt = Sd[it_ % 2], Sd[(it_ + 1) % 2]
                            pS = ps_mm.get()
                            mm(pS, pS.t[:, 0:128], kd.t[:, gc, :], vt_.t[:, gc, :], True, True, [kd, vt_])
                            stt(nxt, nxt.t[:, :], cur.t[:, :], rds.t[:, d, hh:hh + 1], pS.t[:, 0:128], ALU.mult, ALU.add, [cur, rds, pS])
                            sb_ = Sbf.get()
                            act(sb_, sb_.t[:, :], cur.t[:, :], AF.Copy, [cur])
                            po = ps_mm.get()
                            if d == 0:
                                pa = ps_mm.get()
                                mm(pa, pa.t[:, 0:128], kT_.t[:, cs_], qT_.t[:, cs_], True, True, [kT_, qT_])
                                pt = pt_p.get()
                                tt(pt, pt.t[:, :], pa.t[:, 0:128], rDT.t[:, hh, :], ALU.mult, [pa, rDT])
                                mm(po, po.t[:, 0:128], vt_.t[:, gc, :], pt.t[:, :], True, False, [vt_, pt])
                                mm(po, po.t[:, 0:128], sb_.t[:, :], qd.t[:, cs_], False, True, [sb_, qd])
                                act(acc, acc.t[:, cs_], po.t[:, 0:128], AF.Copy, [po])
                            else:
                                mm(po, po.t[:, 0:128], sb_.t[:, :], qd.t[:, cs_], True, True, [sb_, qd])
                                tt(acc, acc.t[:, cs_], acc.t[:, cs_], po.t[:, 0:128], ALU.add, [acc, po])
                        if is_p:
                            Sfin = Sd[NCH % 2]
                            c.dma("sp", ret_o[d, s, l, hh], Sfin.t[:, :], reads=[Sfin], writes=[d_out])
                for tb in range(ntok // 512):
                    cols = slice(tb * 512, (tb + 1) * 512)
                    ob = obf_p.get()
                    act(ob, ob.t[:, :], acc.t[:, cols], AF.Copy, [acc])
                    sq = sq_p.get()
                    act(sq, sq.t[:, :], acc.t[:, cols], AF.Square, [acc])
                    pm, pv = ps_aux.get(), ps_aux.get()
                    mm(pm, pm.t[:, :], ones128.t[:, :], ob.t[:, :], True, True, [ones128, ob])
                    mm(pv, pv.t[:, :], ones128.t[:, :], sq.t[:, :], True, True, [ones128, sq])
                    mean = w32.get()
                    act(mean, mean.t[:, :], pm.t[:, :], AF.Copy, [pm])
                    var = w32.get()
                    tt(var, var.t[:, :], mean.t[:, :], mean.t[:, :], ALU.mult, [mean])
                    tt(var, var.t[:, :], pv.t[:, :], var.t[:, :], ALU.subtract, [pv, var])
                    ts(var, var.t[:, :], var.t[:, :], 0.0, None, ALU.max, None, [var])
                    act(var, var.t[:, :], var.t[:, :], AF.Sqrt, [var, eps_t], bias=eps_t.t[:, 0:1])
                    recip(var, var.t[:, :], var.t[:, :], [var])
                    cen = w32.get()
                    tt(cen, cen.t[:, :], acc.t[:, cols], mean.t[:, :], ALU.subtract, [acc, mean])
                    tt(cen, cen.t[:, :], cen.t[:, :], var.t[:, :], ALU.mult, [cen, var])
                    gt = gt_p.get()
                    c.dma("sp", gt.t[:, :], zT.t[R_G + hh * 128:R_G + (hh + 1) * 128, cols], reads=[zT], writes=[gt])
                    sgt = w32.get()
                    act(sgt, sgt.t[:, :], gt.t[:, :], AF.Silu, [gt])
                    so = stg_bf.get()
                    stt(so, so.t[:, 0:512], cen.t[:, :], r_ng[l].t[:, hh:hh + 1], sgt.t[:, :], ALU.mult, ALU.mult, [cen, r_ng[l], sgt])
                    c.dma("sp", yT.t[DM + hh * 128:DM + (hh + 1) * 128, cols], so.t[:, 0:512], reads=[so], writes=[yT])
            AR.reset()
            cqn = AR.tile([128, 6, 512], BF16)
            wq_all = AR.tile([128, 6, 1536], BF16)
            c.dma("pool", wq_all.t[:, :, :], w_uq[l].rearrange("(k p) n -> p k n", p=128), writes=[wq_all])
            ld_p = Pool([AR.tile([128, 6, 512], BF16) for _ in range(2)])
            ldsq = Pool([AR.tile([128, 6, 512], BF16) for _ in range(2)])
            k32 = Pool([AR.tile([32, 512], F32) for _ in range(6)])
            rope_p = Pool([AR.tile([32, 512], F32) for _ in range(4)])
            qst = Pool([AR.tile([128, 512], BF16) for _ in range(3)])
            qst32 = Pool([AR.tile([32, 512], BF16) for _ in range(4)])
            for tb in range(ntok // 512):
                cols = slice(tb * 512, (tb + 1) * 512)
                ld = ld_p.get()
                c.dma("sp", ld.t[:, :, :], zT.t[R_CQ:R_CQ + 768, cols].rearrange("(k p) n -> p k n", p=128), reads=[zT], writes=[ld])
                sq = ldsq.get()
                act(sq, sq.t[:, :, :], ld.t[:, :, :], AF.Square, [ld])
                ps = ps_aux.get()
                for kc in range(6):
                    mm(ps, ps.t[:, :], ones_bf.t[:, :], sq.t[:, kc, :], kc == 0, kc == 5, [ones_bf, sq])
                rs = rstd_p.get()
                rstd_from_ssq(ps, ps.t[:, :], rs, rs.t[:, 0:512], 1.0 / 768)
                for kc in range(6):
                    stt(cqn, cqn.t[:, kc, :], ld.t[:, kc, :], qn_t[l].t[:, kc:kc + 1], rs.t[:, 0:512], ALU.mult, ALU.mult, [ld, qn_t[l], rs])
                if not is_p:
                    rc, rsn = rope_p.get(), rope_p.get()
                    c.dma("sp", rc.t[:, :], rope_cs[0, :, cols], writes=[rc])
                    c.dma("sp", rsn.t[:, :], rope_cs[1, :, cols], writes=[rsn])
                for hh in range(8):
                    ps = ps_mm.get()
                    for kc in range(6):
                        mm(ps, ps.t[:, :], wq_all.t[:, kc, hh * 192:hh * 192 + 128], cqn.t[:, kc, :], kc == 0, kc == 5, [wq_all, cqn])
                    so = qst.get()
                    act(so, so.t[:, :], ps.t[:, :], AF.Copy, [ps])
                    c.dma("sp", qscr.t[hh, 0:128, cols], so.t[:, :], reads=[so], writes=[qscr])
                    pr = []
                    for hf in range(2):
                        p_ = ps_mm.get()
                        for kc in range(6):
                            mm(p_, p_.t[0:32, :], wq_all.t[:, kc, hh * 192 + 128 + hf * 32:hh * 192 + 160 + hf * 32], cqn.t[:, kc, :], kc == 0, kc == 5, [wq_all, cqn])
                        pr.append(p_)
                    o0, o1 = qst32.get(), qst32.get()
                    if is_p:
                        act(o0, o0.t[:, :], pr[0].t[0:32, :], AF.Copy, [pr[0]])
                        act(o1, o1.t[:, :], pr[1].t[0:32, :], AF.Copy, [pr[1]])
                    else:
                        a1_, a2_ = k32.get(), k32.get()
                        tt(a1_, a1_.t[:, :], pr[0].t[0:32, :], rc.t[:, :], ALU.mult, [pr[0], rc])
                        tt(a2_, a2_.t[:, :], pr[1].t[0:32, :], rsn.t[:, :], ALU.mult, [pr[1], rsn])
                        tt(o0, o0.t[:, :], a1_.t[:, :], a2_.t[:, :], ALU.subtract, [a1_, a2_])
                        a3_, a4_ = k32.get(), k32.get()
                        tt(a3_, a3_.t[:, :], pr[0].t[0:32, :], rsn.t[:, :], ALU.mult, [pr[0], rsn])
                        tt(a4_, a4_.t[:, :], pr[1].t[0:32, :], rc.t[:, :], ALU.mult, [pr[1], rc])
                        tt(o1, o1.t[:, :], a3_.t[:, :], a4_.t[:, :], ALU.add, [a3_, a4_])
                    c.dma("sp", qscr.t[hh, 128:160, cols], o0.t[:, :], reads=[o0], writes=[qscr])
                    c.dma("sp", qscr.t[hh, 160:192, cols], o1.t[:, :], reads=[o1], writes=[qscr])
            AR.reset()
            SK = L if is_p else L + PAST
            NKT = SK // 128
            ckv = AR.tile([128, 4, nseq * SK], BF16)
            kr = [AR.tile([32, nseq * SK], BF16) for _ in range(2)]
            qn_h = AR.tile([128, ntok], BF16)
            qr_h = [AR.tile([32, ntok], BF16) for _ in range(2)]
            kn_h = AR.tile([128, nseq * SK], BF16)
            v_h = AR.tile([128, nseq * NKT, 128], BF16)
            wk_h = AR.tile([128, 4, 128], BF16)
            wv_h = AR.tile([128, 4, 128], BF16)
            ld_p = Pool([AR.tile([128, 4, 512], BF16) for _ in range(2)])
            ldsq = Pool([AR.tile([128, 4, 512], BF16) for _ in range(1)])
            pt_p = Pool([AR.tile([128, 512], BF16) for _ in range(4)])
            w32 = Pool([AR.tile([128, 512], F32) for _ in range(3)])
            k32 = Pool([AR.tile([32, 512], F32) for _ in range(4)])
            rope_p = Pool([AR.tile([32, 512], F32) for _ in range(2)])
            kraw = Pool([AR.tile([32, 512], BF16) for _ in range(2)])
            for tb in range(ntok // 512):
                cols = slice(tb * 512, (tb + 1) * 512)
                ld = ld_p.get()
                c.dma("sp", ld.t[:, :, :], zT.t[R_CKV:R_CKV + 512, cols].rearrange("(k p) n -> p k n", p=128), reads=[zT], writes=[ld])
                sq = ldsq.get()
                act(sq, sq.t[:, :, :], ld.t[:, :, :], AF.Square, [ld])
                ps = ps_aux.get()
                for kc in range(4):
                    mm(ps, ps.t[:, :], ones_bf.t[:, :], sq.t[:, kc, :], kc == 0, kc == 3, [ones_bf, sq])
                rs = rstd_p.get()
                rstd_from_ssq(ps, ps.t[:, :], rs, rs.t[:, 0:512], 1.0 / 512)
                for kc in range(4):
                    stt(ckv, ckv.t[:, kc, cols], ld.t[:, kc, :], kvn_t[l].t[:, kc:kc + 1], rs.t[:, 0:512], ALU.mult, ALU.mult, [ld, kvn_t[l], rs])
                kw0, kw1 = kraw.get(), kraw.get()
                c.dma("sp", kw0.t[:, :], zT.t[R_KR:R_KR + 32, cols], reads=[zT], writes=[kw0])
                c.dma("sp", kw1.t[:, :], zT.t[R_KR + 32:R_KR + 64, cols], reads=[zT], writes=[kw1])
                if is_p:
                    c.op("dve", lambda h, kw0=kw0, cols=cols, k0_=kr[0]: h.tensor_copy(k0_.t[:, cols], kw0.t[:, :]), reads=[kw0], writes=[kr[0]])
                    c.op("dve", lambda h, kw1=kw1, cols=cols, k1_=kr[1]: h.tensor_copy(k1_.t[:, cols], kw1.t[:, :]), reads=[kw1], writes=[kr[1]])
                else:
                    rc, rsn = rope_p.get(), rope_p.get()
                    c.dma("sp", rc.t[:, :], rope_cs[0, :, cols], writes=[rc])
                    c.dma("sp", rsn.t[:, :], rope_cs[1, :, cols], writes=[rsn])
                    a1_, a2_ = k32.get(), k32.get()
                    tt(a1_, a1_.t[:, :], kw0.t[:, :], rc.t[:, :], ALU.mult, [kw0, rc])
                    tt(a2_, a2_.t[:, :], kw1.t[:, :], rsn.t[:, :], ALU.mult, [kw1, rsn])
                    tt(kr[0], kr[0].t[:, cols], a1_.t[:, :], a2_.t[:, :], ALU.subtract, [a1_, a2_])
                    a3_, a4_ = k32.get(), k32.get()
                    tt(a3_, a3_.t[:, :], kw0.t[:, :], rsn.t[:, :], ALU.mult, [kw0, rsn])
                    tt(a4_, a4_.t[:, :], kw1.t[:, :], rc.t[:, :], ALU.mult, [kw1, rc])
                    tt(kr[1], kr[1].t[:, cols], a3_.t[:, :], a4_.t[:, :], ALU.add, [a3_, a4_])
            if not is_p:
                c.dma("pool", ckv.t[:, :, L:L + PAST], cache_ckvT[l].rearrange("(k p) n -> p k n", p=128), writes=[ckv])
                for hf in range(2):
                    c.dma("pool", kr[hf].t[:, L:L + PAST], cache_kr[l, hf], writes=[kr[hf]])
            for hh in range(8):
                c.dma("pool", wk_h.t[:, :, :], w_uk[l, :, hh * 128:(hh + 1) * 128].rearrange("(k p) n -> p k n", p=128), writes=[wk_h])
                c.dma("pool", wv_h.t[:, :, :], w_uv[l, :, hh * 128:(hh + 1) * 128].rearrange("(k p) n -> p k n", p=128), writes=[wv_h])
                c.dma("sp", qn_h.t[:, :], qscr.t[hh, 0:128, 0:ntok], reads=[qscr], writes=[qn_h])
                c.dma("sp", qr_h[0].t[:, :], qscr.t[hh, 128:160, 0:ntok], reads=[qscr], writes=[qr_h[0]])
                c.dma("sp", qr_h[1].t[:, :], qscr.t[hh, 160:192, 0:ntok], reads=[qscr], writes=[qr_h[1]])
                nk_tot = nseq * SK
                for k0 in range(0, nk_tot, 512):
                    kn_ = min(512, nk_tot - k0)
                    ps = ps_mm.get()
                    for kc in range(4):
                        mm(ps, ps.t[:, 0:kn_], wk_h.t[:, kc, :], ckv.t[:, kc, k0:k0 + kn_], kc == 0, kc == 3, [wk_h, ckv])
                    act(kn_h, kn_h.t[:, k0:k0 + kn_], ps.t[:, 0:kn_], AF.Copy, [ps])
                for kt4 in range(0, nseq * NKT, 4):
                    nn = min(4, nseq * NKT - kt4)
                    ps = ps_mm.get()
                    for i in range(nn):
                        kt = kt4 + i
                        for kc in range(4):
                            mm(ps, ps.t[:, i * 128:(i + 1) * 128], ckv.t[:, kc, kt * 128:(kt + 1) * 128], wv_h.t[:, kc, :], kc == 0, kc == 3, [ckv, wv_h])
                    c.op("dve", lambda h, ps=ps, kt4=kt4, nn=nn, v_h=v_h: h.tensor_copy(v_h.t[:, kt4:kt4 + nn, :], ps.t[:, 0:nn * 128].rearrange("p (a b) -> p a b", b=128)),
                         reads=[ps], writes=[v_h])
                if is_p and l == 0 and hh == 0:
                    dump("qn", qn_h.t[:, :], [128, ntok], BF16, [qn_h])
                    dump("qr0", qr_h[0].t[:, :], [32, ntok], BF16, [qr_h[0]])
                    dump("kn", kn_h.t[:, :], [128, nseq * SK], BF16, [kn_h])
                    dump("kr0", kr[0].t[:, :], [32, nseq * SK], BF16, [kr[0]])
                    dump("vh", v_h.t[:, :, :], [128, nseq * NKT, 128], BF16, [v_h])
                    dump("ckv", ckv.t[:, :, :], [128, 4, nseq * SK], BF16, [ckv])
                QB = min(512, L)
                for s in range(nseq):
                    for qb in range(L // QB):
                        qc = slice(s * L + qb * QB, s * L + (qb + 1) * QB)
                        pend = []
                        for kt in range(NKT + 2):
                            if kt < NKT:
                                kc_ = slice(s * SK + kt * 128, s * SK + (kt + 1) * 128)
                                ps = ps_mm.get()
                                mm(ps, ps.t[:, 0:QB], kn_h.t[:, kc_], qn_h.t[:, qc], True, False, [kn_h, qn_h])
                                mm(ps, ps.t[:, 0:QB], kr[0].t[:, kc_], qr_h[0].t[:, qc], False, False, [kr[0], qr_h[0]])
                                mm(ps, ps.t[:, 0:QB], kr[1].t[:, kc_], qr_h[1].t[:, qc], False, True, [kr[1], qr_h[1]])
                                pt = pt_p.get()
                                act(pt, pt.t[:, 0:QB], ps.t[:, 0:QB], AF.Exp, [ps], scale=ATT_SCALE)
                                pend.append((kt, pt))
                            if kt >= 2:
                                k0, pt0 = pend.pop(0)
                                mm(ACC0, ACC0.t[:, 0:QB], v_h.t[:, s * NKT + k0, :], pt0.t[:, 0:QB], k0 == 0, k0 == NKT - 1, [v_h, pt0])
                                mm(ACC1, ACC1.t[:, 0:QB], ones_bf.t[:, :], pt0.t[:, 0:QB], k0 == 0, k0 == NKT - 1, [ones_bf, pt0])
                        rd = w32.get()
                        recip(rd, rd.t[:, 0:QB], ACC1.t[:, 0:QB], [ACC1])
                        if is_p and l == 0 and hh == 0 and s == 0:
                            dump("rden", rd.t[:, 0:QB], [128, QB], F32, [rd])
                            on_ = w32.get()
                            act(on_, on_.t[:, 0:QB], ACC0.t[:, 0:QB], AF.Copy, [ACC0])
                            dump("onum", on_.t[:, 0:QB], [128, QB], F32, [on_])
                        so = stg_bf.get()
                        tt(so, so.t[:, 0:QB], ACC0.t[:, 0:QB], rd.t[:, 0:QB], ALU.mult, [ACC0, rd])
                        c.dma("sp", yT.t[2 * DM + hh * 128:2 * DM + (hh + 1) * 128, qc], so.t[:, 0:QB], reads=[so], writes=[yT])
            if DEBUG and is_p and l == 0:
                c.dma("sp", dbg_y, yT.t[:, 0:1024], reads=[yT], writes=[d_out])
            AR.reset()
            merged = AR.tile([128, 16, 1024], BF16)
            yb_p = Pool([AR.tile([128, 8, 1024], BF16) for _ in range(2)])
            gt_p = Pool([AR.tile([128, 1024], BF16) for _ in range(2)])
            t32 = Pool([AR.tile([128, 512], F32) for _ in range(3)])
            xl_p = Pool([AR.tile([128, 1024], F32) for _ in range(2)])
            fl_p = Pool([AR.tile([128, 1024], F32) for _ in range(2)])
            wA = Pool([AR.tile([128, 16, 512], BF16) for _ in range(2)])
            for sg in range(nseg):
                c0 = sg * 1024
                for b in range(3):
                    yb = yb_p.get()
                    c.dma("sp", yb.t[:, :, :], yT.t[b * DM:(b + 1) * DM, c0:c0 + 1024].rearrange("(k p) n -> p k n", p=128), reads=[yT], writes=[yb])
                    for blk in range(4):
                        w = wA.get()
                        load_w(w, 8, 512, w_branch[l, b, :, blk * 512:(blk + 1) * 512], wcC[l], b * 4 + blk, is_p)
                        for mi in range(4):
                            m = blk * 4 + mi
                            gt = gt_p.get()
                            r0 = R_GATE + b * D + m * 128
                            c.dma("sp", gt.t[:, :], zT.t[r0:r0 + 128, c0:c0 + 1024], reads=[zT], writes=[gt])
                            for tb in range(2):
                                tc = slice(tb * 512, (tb + 1) * 512)
                                ps = ps_mm.get()
                                for kc in range(8):
                                    mm(ps, ps.t[:, :], w.t[:, kc, mi * 128:(mi + 1) * 128], yb.t[:, kc, tc], kc == 0, kc == 7, [w, yb])
                                if b == 0:
                                    tt(merged, merged.t[:, m, tc], ps.t[:, :], gt.t[:, tc], ALU.mult, [ps, gt])
                                else:
                                    tq_ = t32.get()
                                    tt(tq_, tq_.t[:, :], ps.t[:, :], gt.t[:, tc], ALU.mult, [ps, gt])
                                    tt(merged, merged.t[:, m, tc], merged.t[:, m, tc], tq_.t[:, :], ALU.add, [merged, tq_])
                for blk in range(4):
                    w = wA.get()
                    load_w(w, 16, 512, w_out[l, :, blk * 512:(blk + 1) * 512], wcC[l], 12 + blk, is_p)
                    for mi in range(4):
                        m = blk * 4 + mi
                        so = stg_f.get()
                        for tb in range(2):
                            tc = slice(tb * 512, (tb + 1) * 512)
                            ps = ps_mm.get()
                            for kc in range(16):
                                mm(ps, ps.t[:, :], w.t[:, kc, mi * 128:(mi + 1) * 128], merged.t[:, kc, tc], kc == 0, kc == 15, [w, merged])
                            act(so, so.t[:, tc], ps.t[:, :], AF.Copy, [ps])
                            sq = sq_p.get()
                            act(sq, sq.t[:, :], ps.t[:, :], AF.Square, [ps])
                            A_ = ACC0 if tb == 0 else ACC1
                            mm(A_, A_.t[:, :], ones_bf.t[:, :], sq.t[:, :], m == 0, m == 15, [ones_bf, sq])
                        c.dma("sp", fbuf.t[m * 128:(m + 1) * 128, :], so.t[:, :], reads=[so], writes=[fbuf])
                rs = rstd_p.get()
                rstd_from_ssq(ACC0, ACC0.t[:, :], rs, rs.t[:, 0:512], 1.0 / D)
                rstd_from_ssq(ACC1, ACC1.t[:, :], rs, rs.t[:, 512:1024], 1.0 / D)
                def ldC(m):
                    xl = xl_p.get()
                    c.dma("sp", xl.t[:, :], xs_ap[m * 128:(m + 1) * 128, c0:c0 + 1024], reads=[xs_dep], writes=[xl])
                    fl = fl_p.get()
                    c.dma("sp", fl.t[:, :], fbuf.t[m * 128:(m + 1) * 128, :], reads=[fbuf], writes=[fl])
                    return xl, fl
                nxt = ldC(0)
                for m in range(16):
                    xl, fl = nxt
                    if m + 1 < 16:
                        nxt = ldC(m + 1)
                    so = stg_f.get()
                    stt(so, so.t[:, :], fl.t[:, :], P.t[:, 2, m:m + 1], rs.t[:, :], ALU.mult, ALU.mult, [fl, rs] + A_reads)
                    tt(so, so.t[:, :], so.t[:, :], xl.t[:, :], ALU.add, [so, xl])
                    c.dma("sp", x1T.t[m * 128:(m + 1) * 128, c0:c0 + 1024], so.t[:, :], reads=[so], writes=[x1T])
            if DEBUG and is_p and l == 0:
                c.dma("sp", dbg_x1, x1T.t[:, 0:1024], reads=[x1T], writes=[d_out])
            AR.reset()
            NCOL = 520
            h2 = AR.tile([128, 16, NCOL], BF16)
            aT = AR.tile([128, 44, 512], BF16)
            xblk_pool = Pool([AR.tile([128, 16, 128], F32) for _ in range(1)])
            xblk_sq = Pool([AR.tile([128, 16, 128], BF16) for _ in range(1)])
            uv_p = Pool([AR.tile([128, NCOL], F32) for _ in range(2)])
            ug_p = Pool([AR.tile([128, NCOL], F32) for _ in range(2)])
            cv_p = Pool([AR.tile([128, 512], F32) for _ in range(4)])
            wD = Pool([AR.tile([128, 44, 128], BF16) for _ in range(2)])
            wU = Pool([AR.tile([128, 16, 256], BF16) for _ in range(4)])
            yo_ap, yo_dep = (x2T.t, x2T.d) if l == 0 else (G["yout"], d_out)
            nsegD = ntok // 512
            for sg in range(nsegD):
                c0 = sg * 512
                if is_p:
                    c.op("pool", lambda h, h2=h2: h.memset(h2.t[:, :, :], 0.0), writes=[h2])
                    for sq_i in range(2):
                        for q0 in range(0, 256, 128):
                            norm_mod(x1T.t, x1T.d, c0 + sq_i * 256 + q0, 128, h2, sq_i * 258 + 1 + q0, P.t[:, 3, :], P.t[:, 4, :], xblk_pool)
                    mmblocks = [(0, 258), (258, 258)]
                else:
                    lo = c0 - 1 if sg > 0 else c0
                    hi = c0 + 513 if sg < nsegD - 1 else c0 + 512
                    if sg == 0 or sg == nsegD - 1:
                        c.op("pool", lambda h, h2=h2: h.memset(h2.t[:, :, :], 0.0), writes=[h2])
                    q = lo
                    while q < hi:
                        n = min(128, hi - q)
                        norm_mod(x1T.t, x1T.d, q, n, h2, q - (c0 - 1), P.t[:, 3, :], P.t[:, 4, :], xblk_pool)
                        q += n
                    mmblocks = [(0, 512), (512, 2)]
                for hb in range(22):
                    wv_ = wU.get()
                    firstD = is_p and sg == 0
                    load_w(wv_, 16, 256, w_up[l, :, hb * 256:(hb + 1) * 256], wcU[l], hb, firstD)
                    wg_ = wU.get()
                    load_w(wg_, 16, 256, w_up[l, :, DFF + hb * 256:DFF + (hb + 1) * 256], wcU[l], 22 + hb, firstD)
                    for mi in range(2):
                        hm = hb * 2 + mi
                        uv, ug = uv_p.get(), ug_p.get()
                        for (wt, ut, eng) in ((wv_, uv, "act"), (wg_, ug, "dve")):
                            for (b0, bn) in mmblocks:
                                ps = ps_mm.get()
                                for kc in range(16):
                                    mm(ps, ps.t[:, 0:bn], wt.t[:, kc, mi * 128:(mi + 1) * 128], h2.t[:, kc, b0:b0 + bn], kc == 0, kc == 15, [wt, h2])
                                if eng == "act":
                                    act(ut, ut.t[:, b0:b0 + bn], ps.t[:, 0:bn], AF.Copy, [ps])
                                else:
                                    c.op("dve", lambda h, ut=ut, ps=ps, b0=b0, bn=bn: h.tensor_copy(ut.t[:, b0:b0 + bn], ps.t[:, 0:bn]), reads=[ps], writes=[ut])
                        cvs = []
                        for (ut, fm_) in ((uv, hm), (ug, 44 + hm)):
                            cv = cv_p.get()
                            w0, w1, w2 = (cvw[l].t[:, fm_, i:i + 1] for i in range(3))
                            bb = cvb[l].t[:, fm_:fm_ + 1]
                            if is_p:
                                u3 = ut.t[:, 0:516].rearrange("p (a b) -> p a b", b=258)
                                o3 = cv.t[:, :].rearrange("p (a b) -> p a b", b=256)
                                i0, i1, i2 = u3[:, :, 0:256], u3[:, :, 1:257], u3[:, :, 2:258]
                            else:
                                o3 = cv.t[:, :]
                                i0, i1, i2 = ut.t[:, 0:512], ut.t[:, 1:513], ut.t[:, 2:514]
                            ts(cv, o3, i1, w1, bb, ALU.mult, ALU.add, [ut, cvw[l], cvb[l]])
                            stt(cv, o3, i0, w0, o3, ALU.mult, ALU.add, [ut, cvw[l], cv])
                            stt(cv, o3, i2, w2, o3, ALU.mult, ALU.add, [ut, cvw[l], cv])
                            cvs.append(cv)
                        act(cvs[1], cvs[1].t[:, :], cvs[1].t[:, :], AF.Silu, [cvs[1]])
                        tt(aT, aT.t[:, hm, :], cvs[1].t[:, :], cvs[0].t[:, :], ALU.mult, [cvs[0], cvs[1]])
                for m in range(16):
                    w = wD.get()
                    load_w(w, 44, 128, w_down[l, :, m * 128:(m + 1) * 128], wcD[l], m, is_p and sg == 0)
                    so = stg_f.get()
                    ps = ps_mm.get()
                    for kc in range(44):
                        mm(ps, ps.t[:, :], w.t[:, kc, :], aT.t[:, kc, :], kc == 0, kc == 43, [w, aT])
                    act(so, so.t[:, 0:512], ps.t[:, :], AF.Copy, [ps])
                    sq = sq_p.get()
                    act(sq, sq.t[:, :], ps.t[:, :], AF.Square, [ps])
                    mm(ACC0, ACC0.t[:, :], ones_bf.t[:, :], sq.t[:, :], m == 0, m == 15, [ones_bf, sq])
                    c.dma("sp", fbuf.t[m * 128:(m + 1) * 128, 0:512], so.t[:, 0:512], reads=[so], writes=[fbuf])
                rs = rstd_p.get()
                rstd_from_ssq(ACC0, ACC0.t[:, :], rs, rs.t[:, 0:512], 1.0 / D)
                def ldD(m):
                    xl = cv_p.get()
                    c.dma("sp", xl.t[:, :], x1T.t[m * 128:(m + 1) * 128, c0:c0 + 512], reads=[x1T], writes=[xl])
                    fl = cv_p.get()
                    c.dma("sp", fl.t[:, :], fbuf.t[m * 128:(m + 1) * 128, 0:512], reads=[fbuf], writes=[fl])
                    return xl, fl
                nxt = ldD(0)
                for m in range(16):
                    xl, fl = nxt
                    if m + 1 < 16:
                        nxt = ldD(m + 1)
                    so = stg_f.get()
                    stt(so, so.t[:, 0:512], fl.t[:, :], P.t[:, 5, m:m + 1], rs.t[:, 0:512], ALU.mult, ALU.mult, [fl, rs] + A_reads)
                    tt(so, so.t[:, 0:512], so.t[:, 0:512], xl.t[:, :], ALU.add, [so, xl])
                    c.dma("sp", yo_ap[m * 128:(m + 1) * 128, c0:c0 + 512], so.t[:, 0:512], reads=[so], writes=[yo_dep])
            if DEBUG and is_p and l == 0:
                c.dma("sp", dbg_x2, x2T.t[:, 0:1024], reads=[x2T], writes=[d_out])
    c.finish()
    return nc, c


def host_inputs(I, core):
    f32 = np.float32
    b = core % 2
    A = lambda x: np.ascontiguousarray(x, dtype=f32)
    m = {}
    m["xTp"] = A(I["x_prompt"][4 * core:4 * core + 4].reshape(1024, D).T)
    m["xTs"] = A(I["x_sample"][b].T)
    conds = np.stack([I["c_ctx"], I["c"][b]], axis=-1)
    m["condT"] = A(conds.reshape(16, 128, 2).transpose(1, 0, 2))
    m["ada_w"] = A(I["ada_w"])
    m["ada_bT"] = A(I["ada_b"].reshape(NL, 96, 128).transpose(0, 2, 1))
    m["norm_gT"] = A(I["norm_g"].reshape(NL, 4, 16, 128).transpose(0, 3, 1, 2))
    m["w_in"] = A(I["w_in"])
    kcols = 6400 + np.concatenate([np.arange(0, 64, 2), np.arange(1, 64, 2)])
    m["w_in_kr"] = A(I["w_in"][:, :, kcols])

    def qj(a):
        sh = a.shape[:-2]
        return a.reshape(sh + (32, 2, 64)).reshape(sh + (32, 128)).swapaxes(-1, -2)
    ldt = np.broadcast_to(I["s5_log_dt"][..., None], I["s5_lam_re"].shape)
    m["s5_lam"] = A(np.stack([qj(I["s5_lam_re"]), qj(I["s5_lam_im"]), qj(ldt)], axis=2))
    nb = np.zeros((NL, 2, 128, 32, 32), f32)
    cbk = np.zeros((NL, 2, 128, 32, 32), f32)
    for r, (bsrc, csrc) in enumerate(((I["s5_b_re"], I["s5_c_re"]), (I["s5_b_im"], I["s5_c_im"]))):
        bb = bsrc.reshape(NL, 32, 2, 64, 16)
        cc = csrc.reshape(NL, 32, 2, 16, 64)
        for g2 in range(2):
            nb[:, r, g2 * 64:(g2 + 1) * 64, :, g2 * 16:(g2 + 1) * 16] = bb[:, :, g2].transpose(0, 2, 1, 3)
            cbk[:, r, g2 * 64:(g2 + 1) * 64, :, g2 * 16:(g2 + 1) * 16] = cc[:, :, g2].transpose(0, 3, 1, 2)
    m["s5_nb"] = nb
    m["s5_cb"] = cbk
    m["s5_dT"] = A(I["s5_d"].reshape(NL, 32, 32).transpose(0, 2, 1))
    s0 = np.stack([I["state_s5_fwd"][b], I["state_s5_bwd"][b]], axis=1)
    m["s5_s0"] = A(s0.reshape(NL, 2, 32, 128, 2).transpose(0, 1, 3, 2, 4))
    m["glu_w"] = A(I["s5_glu_w"])
    m["glu_bT"] = A(I["s5_glu_b"].reshape(NL, 8, 128).transpose(0, 2, 1))
    m["ret_dec"] = A(np.broadcast_to(I["ret_decay"].reshape(NL, 1, 16), (NL, 128, 16)))
    m["ret_ngT"] = A(I["ret_norm_g"].reshape(NL, 8, 128).transpose(0, 2, 1))
    m["ret_s0"] = A(np.stack([I["state_ret_fwd"][b], I["state_ret_bwd"][b]], axis=1))
    jj = np.arange(128)[:, None].astype(f32)
    ii = np.arange(128)[None, :].astype(f32)
    tab = np.stack([np.maximum(ii - jj, 0), (ii >= jj).astype(f32), np.maximum(jj - ii, 0), (jj >= ii).astype(f32),
                    np.broadcast_to(ii + 1, (128, 128)), np.broadcast_to(128 - ii, (128, 128))], axis=1)
    m["ret_tab"] = A(tab)
    pp = np.arange(128).astype(f32)
    m["ret_col"] = A(np.stack([127 - pp, pp, np.full(128, 128.0, f32)], axis=1))
    m["q_normT"] = A(I["mla_q_norm"].reshape(NL, 6, 128).transpose(0, 2, 1))
    m["kv_normT"] = A(I["mla_kv_norm"].reshape(NL, 4, 128).transpose(0, 2, 1))
    m["kv_norm_rep"] = A(np.broadcast_to(I["mla_kv_norm"][:, None, :], (NL, 128, 512)))
    hcols = np.concatenate([np.arange(128), 128 + np.arange(0, 64, 2), 128 + np.arange(1, 64, 2)])
    allc = np.concatenate([h * 192 + hcols for h in range(8)])
    m["w_uq"] = A(I["mla_w_uq"][:, :, allc])
    m["w_uk"] = A(I["mla_w_uk"])
    m["w_uv"] = A(I["mla_w_uv"])
    t = np.arange(LS)
    row = (t // 64).astype(f32)
    col = (t % 64).astype(f32)
    inv = (np.float32(10000.0) ** (-np.arange(16, dtype=f32) / np.float32(16))).astype(f32)
    ang = np.concatenate([row[:, None] * inv, col[:, None] * inv], axis=-1).astype(f32)
    m["rope_cs"] = A(np.stack([np.cos(ang).T, np.sin(ang).T]))
    m["cache_ckvT"] = A(I["cache_mla_ckv"][b].transpose(0, 2, 1))
    ck = I["cache_mla_krope"][b]
    m["cache_kr"] = A(np.stack([ck[:, :, 0::2], ck[:, :, 1::2]], axis=1).transpose(0, 1, 3, 2))
    m["w_branch"] = A(I["w_branch"])
    m["w_out"] = A(I["w_out"])
    m["w_up"] = A(I["ffn_w_up"])
    m["conv_wT"] = A(I["ffn_conv_w"].reshape(NL, 3, 88, 128).transpose(0, 3, 2, 1))
    m["conv_bT"] = A(I["ffn_conv_b"].reshape(NL, 88, 128).transpose(0, 2, 1))
    m["w_down"] = A(I["ffn_w_down"])
    return m


_CACHE = {}


def kernel(**inputs):
    I = {k: np.asarray(v) for k, v in inputs.items()}
    if "nc" not in _CACHE:
        _CACHE["nc"] = build()[0]
    nc = _CACHE["nc"]
    in_maps = [host_inputs(I, core) for core in range(8)]
    res = run_bass_kernel_spmd(nc, in_maps, core_ids=list(range(8)))
    R = res.results
    _CACHE["last"] = R
    y_prompt = np.concatenate([R[cidx]["yTp"].T.reshape(4, LP, D) for cidx in range(8)], axis=0)
    y_sample = np.stack([R[0]["yTs"].T, R[1]["yTs"].T], axis=0)
    ckv = np.concatenate([R[cidx]["ckv_o"] for cidx in range(8)], axis=0)
    krp = np.concatenate([R[cidx]["kr_o"] for cidx in range(8)], axis=0)
    retf = np.concatenate([R[cidx]["ret_o"][0] for cidx in range(8)], axis=0)
    retb = np.concatenate([R[cidx]["ret_o"][1] for cidx in range(8)], axis=0)

    def s5fix(a):
        a = a.transpose(0, 1, 4, 3, 2)
        return np.ascontiguousarray(a.reshape(a.shape[0], NL, 64, 64, 2))
    s5f = np.concatenate([s5fix(R[cidx]["s5_o"][0]) for cidx in range(8)], axis=0)
    s5b = np.concatenate([s5fix(R[cidx]["s5_o"][1]) for cidx in range(8)], axis=0)
    f = lambda a: np.ascontiguousarray(a, dtype=np.float32)
    return (f(y_prompt), f(y_sample), f(ckv), f(krp), f(retf), f(retb), f(s5f), f(s5b))
```

```python
import math
import numpy as np
import concourse.bass as bass
import concourse.mybir as mybir
from concourse.bass_utils import run_bass_kernel_spmd

F32 = mybir.dt.float32
BF16 = mybir.dt.bfloat16
AF = mybir.ActivationFunctionType
ALU = mybir.AluOpType

RING = 8
DEBUG = False
EPS = 1e-6
D = 2048
DM = 1024
NL = 2
LP = 256
NPS = 4
LS = 4096
PAST = 256
DFF = 5632
ZROWS = 11584
R_U, R_Q, R_K, R_G, R_CQ, R_CKV, R_KR, R_GATE = 0, 1024, 2048, 3072, 4096, 4864, 5376, 5440
ATT_SCALE = (128 + 64) ** -0.5
RET_SCALE = 128 ** -0.5
MAGIC = 12582912.0
TWO_PI = 2.0 * math.pi


class Dep:
    __slots__ = ("w", "r")

    def __init__(self):
        self.w = {}
        self.r = {}


class T:
    __slots__ = ("t", "d")

    def __init__(self, t, d=None):
        self.t = t
        self.d = d if d is not None else Dep()


class Ctx:
    def __init__(self, nc):
        self.nc = nc
        self.enames = ["pe", "act", "dve", "pool", "sp"]
        self.ops = {e: [] for e in self.enames}
        self.cnt = {e: 0 for e in self.enames}
        self.sem = {e: nc.alloc_semaphore("s_" + e) for e in self.enames}
        self.waited = {e: {} for e in self.enames}
        self.ring = {q: [nc.alloc_semaphore("r_%s%d" % (q, i)) for i in range(RING)] for q in ("sp", "pool", "act")}
        self.dman = {q: 0 for q in self.ring}
        self.nalloc = 0
        self.ninstr = 0

    def sb(self, shape, dt, name=None):
        self.nalloc += 1
        return T(self.nc.alloc_sbuf_tensor(name or ("sb%d" % self.nalloc), list(shape), dt))

    def semof(self, key):
        if key[0] == "E":
            return self.sem[key[1]]
        return self.ring[key[1]][key[2]]

    def _collect(self, e, reads, writes, extra=None):
        need = {}
        toks = []
        for d in reads:
            toks += list(d.w.items())
        for d in writes:
            toks += list(d.w.items())
            toks += list(d.r.items())
        if extra:
            toks += extra
        for key, (val, src) in toks:
            if src == "pe" and e == "pe" and key[0] == "E":
                continue
            if self.waited[e].get(key, 0) >= val:
                continue
            if need.get(key, 0) < val:
                need[key] = val
        for key, val in need.items():
            self.waited[e][key] = val
        return [(self.semof(k), v) for k, v in need.items()]

    def op(self, e, fn, reads=(), writes=()):
        reads = [x.d if isinstance(x, T) else x for x in reads]
        writes = [x.d if isinstance(x, T) else x for x in writes]
        wl = self._collect(e, reads, writes)
        self.cnt[e] += 1
        n = self.cnt[e]
        key = ("E", e)
        sem = self.sem[e]

        def emit(h):
            for sm, v in wl:
                h.wait_ge(sm, v)
            fn(h).then_inc(sem, 1)
        self.ops[e].append(emit)
        self.ninstr += 1 + len(wl)
        for d in reads:
            d.r[key] = (n, e)
        for d in writes:
            d.w[key] = (n, e)

    def dma(self, q, out, in_, reads=(), writes=(), **kw):
        reads = [x.d if isinstance(x, T) else x for x in reads]
        writes = [x.d if isinstance(x, T) else x for x in writes]
        i = self.dman[q]
        self.dman[q] += 1
        ri = i % RING
        val = 16 * (i // RING + 1)
        prev = 16 * (i // RING)
        key = ("D", q, ri)
        extra = [(key, (prev, None))] if prev > 0 else None
        wl = self._collect(q, reads, writes, extra)
        sem = self.ring[q][ri]

        def emit(h):
            for sm, v in wl:
                h.wait_ge(sm, v)
            h.dma_start(out=out, in_=in_, **kw).then_inc(sem, 16)
        self.ops[q].append(emit)
        self.ninstr += 1 + len(wl)
        for d in reads:
            d.r[key] = (val, None)
        for d in writes:
            d.w[key] = (val, None)

    def finish(self):
        finals = []
        for q in self.ring:
            n = self.dman[q]
            for ri in range(RING):
                cntr = (n - ri + RING - 1) // RING if n > ri else 0
                if cntr > 0:
                    finals.append((self.ring[q][ri], 16 * cntr))
        efinal = [(self.sem[e], self.cnt[e]) for e in self.enames if self.cnt[e] > 0]
        ops = self.ops
        with self.nc.Block() as block:
            @block.tensor
            def _(h):
                for f in ops["pe"]:
                    f(h)

            @block.scalar
            def _(h):
                for f in ops["act"]:
                    f(h)

            @block.vector
            def _(h):
                for f in ops["dve"]:
                    f(h)

            @block.gpsimd
            def _(h):
                for f in ops["pool"]:
                    f(h)

            @block.sync
            def _(h):
                for f in ops["sp"]:
                    f(h)
                for sm, v in efinal:
                    h.wait_ge(sm, v)
                for sm, v in finals:
                    h.wait_ge(sm, v)


class Pool:
    def __init__(self, tiles):
        self.tiles = tiles
        self.i = 0

    def get(self):
        t = self.tiles[self.i % len(self.tiles)]
        self.i += 1
        return t


def pat(a):
    return a.tensor, a.offset, [list(x) for x in a.ap]


def bc_mid(a, n):
    t, o, p = pat(a)
    return bass.AP(t, o, [p[0], [0, n]] + p[1:])


def bc_last(a, n):
    t, o, p = pat(a)
    return bass.AP(t, o, p + [[0, n]])


def rev(a):
    t, o, p = pat(a)
    st, n = p[1]
    return bass.AP(t, o + st * (n - 1), [p[0], [-st, n]])


def build(debug_stop=None):
    nc = bass.Bass("TRN2", target_bir_lowering=False)
    c = Ctx(nc)

    def din(name, shape, dt=F32):
        return nc.dram_tensor(name, list(shape), dt, kind="ExternalInput").ap()

    def dout(name, shape, dt=F32):
        return nc.dram_tensor(name, list(shape), dt, kind="ExternalOutput").ap()

    def dscr(name, shape, dt):
        return T(nc.dram_tensor(name, list(shape), dt, kind="Internal").ap())

    xTp = din("xTp", [D, 1024])
    xTs = din("xTs", [D, LS])
    condT = din("condT", [128, 16, 2])
    ada_w = din("ada_w", [NL, D, 6 * D])
    ada_bT = din("ada_bT", [NL, 128, 96])
    norm_gT = din("norm_gT", [NL, 128, 4, 16])
    w_in = din("w_in", [NL, D, 12608])
    w_in_kr = din("w_in_kr", [NL, D, 64])
    s5_lam = din("s5_lam", [NL, 2, 3, 128, 32])
    s5_nb = din("s5_nb", [NL, 2, 128, 32, 32])
    s5_cb = din("s5_cb", [NL, 2, 128, 32, 32])
    s5_dT = din("s5_dT", [NL, 32, 32])
    s5_s0 = din("s5_s0", [NL, 2, 128, 32, 2])
    glu_w = din("glu_w", [NL, DM, DM])
    glu_bT = din("glu_bT", [NL, 128, 8])
    ret_dec = din("ret_dec", [NL, 128, 16])
    ret_ngT = din("ret_ngT", [NL, 128, 8])
    ret_s0 = din("ret_s0", [NL, 2, 8, 128, 128])
    ret_tab = din("ret_tab", [128, 6, 128])
    ret_col = din("ret_col", [128, 3])
    q_normT = din("q_normT", [NL, 128, 6])
    kv_normT = din("kv_normT", [NL, 128, 4])
    kv_norm_rep = din("kv_norm_rep", [NL, 128, 512])
    w_uq = din("w_uq", [NL, 768, 1536])
    w_uk = din("w_uk", [NL, 512, 1024])
    w_uv = din("w_uv", [NL, 512, 1024])
    rope_cs = din("rope_cs", [2, 32, LS])
    cache_ckvT = din("cache_ckvT", [NL, 512, PAST])
    cache_kr = din("cache_kr", [NL, 2, 32, PAST])
    w_branch = din("w_branch", [NL, 3, DM, D])
    w_out = din("w_out", [NL, D, D])
    w_up = din("w_up", [NL, D, 2 * DFF])
    conv_wT = din("conv_wT", [NL, 128, 88, 3])
    conv_bT = din("conv_bT", [NL, 128, 88])
    w_down = din("w_down", [NL, DFF, D])
    yTp = dout("yTp", [D, 1024])
    yTs = dout("yTs", [D, LS])
    ckv_o = dout("ckv_o", [NPS, NL, LP, 512])
    kr_o = dout("kr_o", [NPS, NL, LP, 64])
    ret_o = dout("ret_o", [2, NPS, NL, 8, 128, 128])
    s5_o = dout("s5_o", [2, NPS, NL, 2, 128, 32])
    if DEBUG:
        dbg_y = dout("dbg_y", [3 * DM, 1024], BF16)
        dbg_x1 = dout("dbg_x1", [D, 1024])
        dbg_x2 = dout("dbg_x2", [D, 1024])
    zT = dscr("zT", [ZROWS, LS], BF16)
    ktok = dscr("ktok", [LS, DM], BF16)
    vtok = dscr("vtok", [LS, DM], BF16)
    ygT = dscr("ygT", [DM, LS], BF16)
    yT = dscr("yT", [3 * DM, LS], BF16)
    x1b = dscr("x1b", [D, LS], F32)
    x2b = dscr("x2b", [D, LS], F32)
    fbuf = dscr("fbuf", [D, 1024], F32)
    qscr = dscr("qscr", [8, 192, LS], BF16)
    d_xTp, d_xTs = Dep(), Dep()
    d_out = Dep()

    wcA = [dscr("wcA%d" % l, [32, 128, 16 * 512], BF16) for l in range(NL)]
    wcC = [dscr("wcC%d" % l, [16, 128, 16 * 512], BF16) for l in range(NL)]
    wcU = [dscr("wcU%d" % l, [44, 128, 16 * 256], BF16) for l in range(NL)]
    wcD = [dscr("wcD%d" % l, [16, 128, 44 * 128], BF16) for l in range(NL)]

    def load_w(w, nk, ncols, src_ap, cache, bi, first):
        cview = cache.t[bi, :, 0:nk * ncols].rearrange("p (k n) -> p k n", k=nk)
        if first:
            c.dma("pool", w.t[:, 0:nk, 0:ncols], src_ap.rearrange("(k p) n -> p k n", p=128), writes=[w])
            c.dma("sp", cview, w.t[:, 0:nk, 0:ncols], reads=[w], writes=[cache])
        else:
            c.dma("pool", w.t[:, 0:nk, 0:ncols], cview, reads=[cache], writes=[w])

    def dump(name, ap, shape, dt, reads):
        if DEBUG:
            o_ = dout("dbg_" + name, shape, dt)
            c.dma("sp", o_, ap, reads=reads, writes=[d_out])

    PS = [T(nc.alloc_psum_tensor("psb%d" % i, [128, 512], F32)) for i in range(8)]
    ps_mm = Pool(PS[0:4])
    ps_aux = Pool(PS[6:8])
    ACC0, ACC1 = PS[4], PS[5]

    ident = c.sb([128, 128], F32, "ident")
    c.op("pool", lambda h: h.memset(ident.t[:, :], 0.0), writes=[ident])
    c.op("pool", lambda h: h.affine_select(ident.t[:, :], ident.t[:, :], pattern=[[-1, 128]], compare_op=ALU.not_equal,
                                           fill=1.0, base=0, channel_multiplier=1), reads=[ident], writes=[ident])
    ones_bf = c.sb([128, 128], BF16, "ones_bf")
    c.op("pool", lambda h: h.memset(ones_bf.t[:, :], 1.0), writes=[ones_bf])
    ones128 = c.sb([128, 128], BF16, "ones128")
    c.op("pool", lambda h: h.memset(ones128.t[:, :], 1.0 / 128.0), writes=[ones128])
    eps_t = c.sb([128, 1], F32, "eps_t")
    c.op("pool", lambda h: h.memset(eps_t.t[:, :], EPS), writes=[eps_t])

    def barrier():
        toks = [(("E", e), (c.cnt[e], e)) for e in c.enames if c.cnt[e] > 0]
        for q in c.ring:
            n = c.dman[q]
            for ri in range(RING):
                cntr = (n - ri + RING - 1) // RING if n > ri else 0
                if cntr > 0:
                    toks.append((("D", q, ri), (16 * cntr, None)))
        d = Dep()
        for k, v in toks:
            d.w[k] = v
        for e in c.enames:
            if e in ("sp",):
                wl = c._collect(e, [d], [])

                def emit(h, wl=wl):
                    for sm, v in wl:
                        h.wait_ge(sm, v)
                c.ops[e].append(emit)
            elif e == "pe":
                wl = c._collect(e, [d], [])

                def emit(h, wl=wl):
                    for sm, v in wl:
                        h.wait_ge(sm, v)
                c.ops[e].append(emit)
            else:
                wl = c._collect(e, [d], [])

                def emit(h, wl=wl):
                    for sm, v in wl:
                        h.wait_ge(sm, v)
                c.ops[e].append(emit)

    def mm(ps, ps_ap, lhsT, rhs, start, stop, reads):
        c.op("pe", lambda h: h.matmul(ps_ap, lhsT, rhs, start=start, stop=stop), reads=reads, writes=[ps])

    def act(out_t, out_ap, in_ap, func, reads, bias=None, scale=None):
        kw = {}
        if bias is not None:
            kw["bias"] = bias
        if scale is not None:
            kw["scale"] = scale
        c.op("act", lambda h: h.activation(out_ap, in_ap, func, **kw), reads=reads, writes=[out_t])

    def tt(out_t, out_ap, a, b, op, reads, eng="dve"):
        c.op(eng, lambda h: h.tensor_tensor(out_ap, a, b, op), reads=reads, writes=[out_t])

    def ts(out_t, out_ap, a, s1, s2, op0, op1, reads, eng="dve"):
        if op1 is None:
            c.op(eng, lambda h: h.tensor_scalar(out_ap, a, s1, None, op0), reads=reads, writes=[out_t])
        else:
            c.op(eng, lambda h: h.tensor_scalar(out_ap, a, s1, s2, op0, op1), reads=reads, writes=[out_t])

    def stt(out_t, out_ap, a, s, b, op0, op1, reads):
        c.op("dve", lambda h: h.scalar_tensor_tensor(out_ap, a, s, b, op0, op1), reads=reads, writes=[out_t])

    def recip(out_t, out_ap, a, reads):
        c.op("dve", lambda h: h.reciprocal(out_ap, a), reads=reads, writes=[out_t])

    def rstd_from_ssq(ps, ps_ap, out_t, out_ap, inv_n):
        act(out_t, out_ap, ps_ap, AF.Sqrt, [ps, eps_t], bias=eps_t.t[:, 0:1], scale=inv_n)
        recip(out_t, out_ap, out_ap, [out_t])

    ARENA_BYTES = 146 * 1024
    arena = nc.alloc_sbuf_tensor("arena", [128, ARENA_BYTES], mybir.dt.uint8)

    class Arena:
        def __init__(self):
            self.off = 0

        def reset(self):
            barrier()
            self.off = 0

        def tile(self, shape, dt):
            nb = int(np.prod(shape[1:])) * mybir.dt.size(dt)
            nb_al = (nb + 31) // 32 * 32
            assert self.off + nb_al <= ARENA_BYTES, ("arena overflow", self.off, nb_al)
            P = shape[0]
            v = arena[0:P, self.off:self.off + nb].bitcast(dt)
            self.off += nb_al
            if len(shape) == 3:
                v = v.rearrange("p (a b) -> p a b", a=shape[1])
            elif len(shape) == 4:
                v = v.rearrange("p (a b c) -> p a b c", a=shape[1], b=shape[2])
            return T(v)

    AR = Arena()
    stg_bf = Pool([c.sb([128, 1024], BF16, "stgbf%d" % i) for i in range(3)])
    stg_f = Pool([c.sb([128, 1024], F32, "stgf%d" % i) for i in range(2)])
    sq_p = Pool([c.sb([128, 512], BF16, "sqp%d" % i) for i in range(3)])
    rstd_p = Pool([c.sb([128, 1024], F32, "rstd%d" % i) for i in range(2)])

    cond_f = c.sb([128, 16, 2], F32, "cond_f")
    cond_b = c.sb([128, 16, 2], BF16, "cond_b")
    c.dma("sp", cond_f.t[:, :, :], condT, writes=[cond_f])
    act(cond_b, cond_b.t[:, :, :], cond_f.t[:, :, :], AF.Silu, [cond_f])
    mods = [c.sb([128, 96, 2], F32, "mods%d" % l) for l in range(NL)]
    adab = [c.sb([128, 96], F32, "adab%d" % l) for l in range(NL)]
    ngs = [c.sb([128, 4, 16], F32, "ng%d" % l) for l in range(NL)]
    PR = [[c.sb([128, 6, 16], F32, "pr%d_%d" % (l, j)) for j in range(2)] for l in range(NL)]
    AR.reset()
    wA = Pool([AR.tile([128, 16, 512], BF16) for i in range(2)])
    for l in range(NL):
        c.dma("sp", adab[l].t[:, :], ada_bT[l], writes=[adab[l]])
        c.dma("sp", ngs[l].t[:, :, :], norm_gT[l], writes=[ngs[l]])
        for blk in range(24):
            w = wA.get()
            c.dma("pool", w.t[:, :, :], ada_w[l, :, blk * 512:(blk + 1) * 512].rearrange("(k p) n -> p k n", p=128), writes=[w])
            ps = ps_mm.get()
            for mi in range(4):
                for kc in range(16):
                    mm(ps, ps.t[:, mi * 2:mi * 2 + 2], w.t[:, kc, mi * 128:(mi + 1) * 128], cond_b.t[:, kc, :], kc == 0, kc == 15, [w, cond_b])
            for mi in range(4):
                m = blk * 4 + mi
                act(mods[l], mods[l].t[:, m, :], ps.t[:, mi * 2:mi * 2 + 2], AF.Identity, [ps, adab[l]], bias=adab[l].t[:, m:m + 1])
        for j in range(2):
            P = PR[l][j]
            M = mods[l]
            stt(P, P.t[:, 0, :], M.t[:, 16:32, j], 1.0, ngs[l].t[:, 0, :], ALU.add, ALU.mult, [M, ngs[l]])
            c.op("dve", lambda h, P=P, M=M, j=j: h.tensor_copy(P.t[:, 1, :], M.t[:, 0:16, j]), reads=[M], writes=[P])
            tt(P, P.t[:, 2, :], M.t[:, 32:48, j], ngs[l].t[:, 1, :], ALU.mult, [M, ngs[l]])
            stt(P, P.t[:, 3, :], M.t[:, 64:80, j], 1.0, ngs[l].t[:, 2, :], ALU.add, ALU.mult, [M, ngs[l]])
            c.op("dve", lambda h, P=P, M=M, j=j: h.tensor_copy(P.t[:, 4, :], M.t[:, 48:64, j]), reads=[M], writes=[P])
            tt(P, P.t[:, 5, :], M.t[:, 80:96, j], ngs[l].t[:, 3, :], ALU.mult, [M, ngs[l]])

    s5_mag = [[c.sb([128, 32], F32, "s5mag%d_%d" % (l, d)) for d in range(2)] for l in range(NL)]
    s5_dc = [[c.sb([128, 13, 32], F32, "s5dc%d_%d" % (l, d)) for d in range(2)] for l in range(NL)]
    s5_ds = [[c.sb([128, 13, 32], F32, "s5ds%d_%d" % (l, d)) for d in range(2)] for l in range(NL)]
    AR.reset()
    wb_d = dscr("wb_d", [NL, 2, 2, 32, 32, 128], BF16)
    wc_d = dscr("wc_d", [NL, 2, 128, 32, 32], BF16)
    wb_stage = Pool([AR.tile([32, 32, 128], BF16) for _ in range(2)])
    wc_stage = Pool([AR.tile([128, 32, 32], BF16) for _ in range(2)])
    s5_dsk = [c.sb([32, 32], F32, "s5dsk%d" % l) for l in range(NL)]
    s5_init = [[c.sb([128, 32, 2], F32, "s5init%d_%d" % (l, d)) for d in range(2)] for l in range(NL)]
    tmpP = Pool([AR.tile([128, 32], F32) for i in range(64)])
    nbig = Pool([AR.tile([128, 32, 32], F32) for i in range(12)])

    def range_reduce_sin(out_t, x_t, shift):
        a = tmpP.get()
        ts(a, a.t[:, :], x_t.t[:, :], shift, None, ALU.add, None, [x_t])
        n = tmpP.get()
        ts(n, n.t[:, :], a.t[:, :], 1.0 / TWO_PI, MAGIC, ALU.mult, ALU.add, [a])
        ts(n, n.t[:, :], n.t[:, :], MAGIC, None, ALU.subtract, None, [n])
        stt(a, a.t[:, :], n.t[:, :], -TWO_PI, a.t[:, :], ALU.mult, ALU.add, [n, a])
        ts(a, a.t[:, :], a.t[:, :], math.pi, -math.pi, ALU.min, ALU.max, [a])
        act(out_t, out_t.t[:, :], a.t[:, :], AF.Sin, [a])

    for l in range(NL):
        c.dma("sp", s5_dsk[l].t[:, :], s5_dT[l], writes=[s5_dsk[l]])
        nbr, nbi = nbig.get(), nbig.get()
        c.dma("sp", nbr.t[:, :, :], s5_nb[l, 0], writes=[nbr])
        c.dma("sp", nbi.t[:, :, :], s5_nb[l, 1], writes=[nbi])
        cr_, ci_ = nbig.get(), nbig.get()
        c.dma("sp", cr_.t[:, :, :], s5_cb[l, 0], writes=[cr_])
        c.dma("sp", ci_.t[:, :, :], s5_cb[l, 1], writes=[ci_])
        for r, (srcc, scl) in enumerate(((cr_, 1.0), (ci_, -1.0))):
            wcs = wc_stage.get()
            act(wcs, wcs.t[:, :, :], srcc.t[:, :, :], AF.Copy, [srcc], scale=scl)
            c.dma("sp", wc_d.t[l, r], wcs.t[:, :, :], reads=[wcs], writes=[wc_d])
        for d in range(2):
            c.dma("sp", s5_init[l][d].t[:, :, :], s5_s0[l, d], writes=[s5_init[l][d]])
            lr, li, ldt = tmpP.get(), tmpP.get(), tmpP.get()
            c.dma("sp", lr.t[:, :], s5_lam[l, d, 0], writes=[lr])
            c.dma("sp", li.t[:, :], s5_lam[l, d, 1], writes=[li])
            c.dma("sp", ldt.t[:, :], s5_lam[l, d, 2], writes=[ldt])
            dt_ = tmpP.get()
            act(dt_, dt_.t[:, :], ldt.t[:, :], AF.Exp, [ldt])
            ar, ai = tmpP.get(), tmpP.get()
            tt(ar, ar.t[:, :], lr.t[:, :], dt_.t[:, :], ALU.mult, [lr, dt_])
            tt(ai, ai.t[:, :], li.t[:, :], dt_.t[:, :], ALU.mult, [li, dt_])
            mag = s5_mag[l][d]
            act(mag, mag.t[:, :], ar.t[:, :], AF.Exp, [ar])
            dc, ds = s5_dc[l][d], s5_ds[l][d]
            cs, sn = tmpP.get(), tmpP.get()
            range_reduce_sin(cs, ai, math.pi / 2)
            range_reduce_sin(sn, ai, 0.0)
            c.op("dve", lambda h, dc=dc, cs=cs: h.tensor_copy(dc.t[:, 0, :], cs.t[:, :]), reads=[cs], writes=[dc])
            c.op("dve", lambda h, ds=ds, sn=sn: h.tensor_copy(ds.t[:, 0, :], sn.t[:, :]), reads=[sn], writes=[ds])
            for k in range(12):
                t1, t2 = tmpP.get(), tmpP.get()
                tt(t1, t1.t[:, :], dc.t[:, k, :], dc.t[:, k, :], ALU.mult, [dc])
                tt(t2, t2.t[:, :], ds.t[:, k, :], ds.t[:, k, :], ALU.mult, [ds])
                tt(dc, dc.t[:, k + 1, :], t1.t[:, :], t2.t[:, :], ALU.subtract, [t1, t2])
                stt(ds, ds.t[:, k + 1, :], dc.t[:, k, :], 2.0, ds.t[:, k, :], ALU.mult, ALU.mult, [dc, ds])
            abr, abi = tmpP.get(), tmpP.get()
            tt(abr, abr.t[:, :], mag.t[:, :], cs.t[:, :], ALU.mult, [mag, cs])
            tt(abi, abi.t[:, :], mag.t[:, :], sn.t[:, :], ALU.mult, [mag, sn])
            ts(abr, abr.t[:, :], abr.t[:, :], -1.0, None, ALU.add, None, [abr])
            den, t1, t2 = tmpP.get(), tmpP.get(), tmpP.get()
            tt(den, den.t[:, :], lr.t[:, :], lr.t[:, :], ALU.mult, [lr])
            tt(t1, t1.t[:, :], li.t[:, :], li.t[:, :], ALU.mult, [li])
            tt(den, den.t[:, :], den.t[:, :], t1.t[:, :], ALU.add, [den, t1])
            recip(den, den.t[:, :], den.t[:, :], [den])
            cre, cim = tmpP.get(), tmpP.get()
            tt(t1, t1.t[:, :], abr.t[:, :], lr.t[:, :], ALU.mult, [abr, lr])
            tt(t2, t2.t[:, :], abi.t[:, :], li.t[:, :], ALU.mult, [abi, li])
            tt(t1, t1.t[:, :], t1.t[:, :], t2.t[:, :], ALU.add, [t1, t2])
            tt(cre, cre.t[:, :], t1.t[:, :], den.t[:, :], ALU.mult, [t1, den])
            tt(t1, t1.t[:, :], abi.t[:, :], lr.t[:, :], ALU.mult, [abi, lr])
            tt(t2, t2.t[:, :], abr.t[:, :], li.t[:, :], ALU.mult, [abr, li])
            tt(t1, t1.t[:, :], t1.t[:, :], t2.t[:, :], ALU.subtract, [t1, t2])
            tt(cim, cim.t[:, :], t1.t[:, :], den.t[:, :], ALU.mult, [t1, den])
            bpr, bpi = nbig.get(), nbig.get()
            cre_b, cim_b = bc_last(cre.t[:, :], 32), bc_last(cim.t[:, :], 32)
            tt(bpr, bpr.t[:, :, :], nbr.t[:, :, :], cre_b, ALU.mult, [nbr, cre])
            tt(bpi, bpi.t[:, :, :], nbi.t[:, :, :], cim_b, ALU.mult, [nbi, cim])
            tt(bpr, bpr.t[:, :, :], bpr.t[:, :, :], bpi.t[:, :, :], ALU.subtract, [bpr, bpi])
            tt(bpi, bpi.t[:, :, :], nbi.t[:, :, :], cre_b, ALU.mult, [nbi, cre])
            t3 = nbig.get()
            tt(t3, t3.t[:, :, :], nbr.t[:, :, :], cim_b, ALU.mult, [nbr, cim])
            tt(bpi, bpi.t[:, :, :], bpi.t[:, :, :], t3.t[:, :, :], ALU.add, [bpi, t3])
            for r, src in ((0, bpr), (1, bpi)):
                wb = wb_stage.get()
                for j4 in range(8):
                    ps = ps_mm.get()
                    for jj in range(4):
                        j = j4 * 4 + jj
                        c.op("pe", lambda h, ps=ps, src=src, j=j, jj=jj: h.transpose(ps.t[0:32, jj * 128:(jj + 1) * 128], src.t[:, j, :], ident.t[:, :]),
                             reads=[src, ident], writes=[ps])
                    act(wb, wb.t[:, j4 * 4:(j4 + 1) * 4, :], ps.t[0:32, :].rearrange("p (a b) -> p a b", a=4), AF.Copy, [ps])
                c.dma("sp", wb_d.t[l, d, r], wb.t[:, :, :], reads=[wb], writes=[wb_d])

    rtab = c.sb([128, 6, 128], F32, "rtab")
    rcol = c.sb([128, 3], F32, "rcol")
    c.dma("sp", rtab.t[:, :, :], ret_tab, writes=[rtab])
    c.dma("sp", rcol.t[:, :], ret_col, writes=[rcol])
    r_lg = [c.sb([128, 16], F32, "rlg%d" % l) for l in range(NL)]
    r_ng = [c.sb([128, 8], F32, "rng%d" % l) for l in range(NL)]
    for l in range(NL):
        c.dma("sp", r_lg[l].t[:, :], ret_dec[l], writes=[r_lg[l]])
        c.dma("sp", r_ng[l].t[:, :], ret_ngT[l], writes=[r_ng[l]])
        act(r_lg[l], r_lg[l].t[:, :], r_lg[l].t[:, :], AF.Exp, [r_lg[l]])
        ts(r_lg[l], r_lg[l].t[:, :], r_lg[l].t[:, :], -1.0, None, ALU.mult, None, [r_lg[l]])

    def ret_tables(l):
        DT = AR.tile([128, 8, 128], F32)
        qp = AR.tile([128, 2, 8, 128], F32)
        kp = AR.tile([128, 2, 8], F32)
        dsc = AR.tile([128, 2, 8], F32)
        e1 = AR.tile([128, 128], F32)
        e2 = AR.tile([128, 128], F32)
        for hh in range(8):
            lgf = r_lg[l].t[:, hh:hh + 1]
            lgb = r_lg[l].t[:, 8 + hh:9 + hh]
            act(e1, e1.t[:, :], rtab.t[:, 0, :], AF.Exp, [rtab, r_lg[l]], scale=lgf)
            tt(e1, e1.t[:, :], e1.t[:, :], rtab.t[:, 1, :], ALU.mult, [e1, rtab])
            act(e2, e2.t[:, :], rtab.t[:, 2, :], AF.Exp, [rtab, r_lg[l]], scale=lgb)
            tt(e2, e2.t[:, :], e2.t[:, :], rtab.t[:, 3, :], ALU.mult, [e2, rtab])
            tt(e1, e1.t[:, :], e1.t[:, :], e2.t[:, :], ALU.add, [e1, e2])
            ts(DT, DT.t[:, hh, :], e1.t[:, :], RET_SCALE, None, ALU.mult, None, [e1])
            act(qp, qp.t[:, 0, hh, :], rtab.t[:, 4, :], AF.Exp, [rtab, r_lg[l]], scale=lgf)
            act(qp, qp.t[:, 1, hh, :], rtab.t[:, 5, :], AF.Exp, [rtab, r_lg[l]], scale=lgb)
            act(kp, kp.t[:, 0, hh:hh + 1], rcol.t[:, 0:1], AF.Exp, [rcol, r_lg[l]], scale=lgf)
            act(kp, kp.t[:, 1, hh:hh + 1], rcol.t[:, 1:2], AF.Exp, [rcol, r_lg[l]], scale=lgb)
            act(dsc, dsc.t[:, 0, hh:hh + 1], rcol.t[:, 2:3], AF.Exp, [rcol, r_lg[l]], scale=lgf)
            act(dsc, dsc.t[:, 1, hh:hh + 1], rcol.t[:, 2:3], AF.Exp, [rcol, r_lg[l]], scale=lgb)
        ts(kp, kp.t[:, :, :], kp.t[:, :, :], RET_SCALE, None, ALU.mult, None, [kp])
        return DT, qp, kp, dsc

    glub = [c.sb([128, 8], F32, "glub%d" % l) for l in range(NL)]
    qn_t = [c.sb([128, 6], F32, "qn%d" % l) for l in range(NL)]
    kvn_t = [c.sb([128, 4], F32, "kvn%d" % l) for l in range(NL)]
    kvn_rep = [c.sb([128, 512], F32, "kvnr%d" % l) for l in range(NL)]
    cvw = [c.sb([128, 88, 3], F32, "cvw%d" % l) for l in range(NL)]
    cvb = [c.sb([128, 88], F32, "cvb%d" % l) for l in range(NL)]
    for l in range(NL):
        c.dma("sp", glub[l].t[:, :], glu_bT[l], writes=[glub[l]])
        c.dma("sp", qn_t[l].t[:, :], q_normT[l], writes=[qn_t[l]])
        c.dma("sp", kvn_t[l].t[:, :], kv_normT[l], writes=[kvn_t[l]])
        c.dma("sp", kvn_rep[l].t[:, :], kv_norm_rep[l], writes=[kvn_rep[l]])
        c.dma("sp", cvw[l].t[:, :, :], conv_wT[l], writes=[cvw[l]])
        c.dma("sp", cvb[l].t[:, :], conv_bT[l], writes=[cvb[l]])

    def norm_mod(xsrc, xdep, src0, n, hbuf, dst0, Aap, Bap, xblk_pool):
        xb = xblk_pool.get()
        kw = {"allow_slow_non_contiguous": True} if n == 1 else {}
        c.dma("sp", xb.t[:, :, 0:n], xsrc[:, src0:src0 + n].rearrange("(k p) n -> p k n", p=128), reads=[xdep], writes=[xb], **kw)
        sq = xblk_sq.get()
        act(sq, sq.t[:, :, 0:n], xb.t[:, :, 0:n], AF.Square, [xb])
        ps = ps_aux.get()
        for kc in range(16):
            mm(ps, ps.t[:, 0:n], ones_bf.t[:, :], sq.t[:, kc, 0:n], kc == 0, kc == 15, [ones_bf, sq])
        rs = rstd_p.get()
        rstd_from_ssq(ps, ps.t[:, 0:n], rs, rs.t[:, 0:n], 1.0 / D)
        tt(xb, xb.t[:, :, 0:n], xb.t[:, :, 0:n], bc_last(Aap, n), ALU.mult, [xb] + A_reads)
        tt(xb, xb.t[:, :, 0:n], xb.t[:, :, 0:n], bc_mid(rs.t[:, 0:n], 16), ALU.mult, [xb, rs])
        tt(hbuf, hbuf.t[:, :, dst0:dst0 + n], xb.t[:, :, 0:n], bc_last(Bap, n), ALU.add, [xb] + A_reads)

    A_reads = [PR[l][j] for l in range(NL) for j in range(2)]

    groups = [
        dict(name="p", cond=0, xin=xTp, xin_dep=d_xTp, ntok=1024, L=LP, nseq=NPS, yout=yTp),
        dict(name="s", cond=1, xin=xTs, xin_dep=d_xTs, ntok=LS, L=LS, nseq=1, yout=yTs),
    ]
    x1T, x2T = x1b, x2b

    for G in groups:
        is_p = G["name"] == "p"
        L, nseq, ntok, cj = G["L"], G["nseq"], G["ntok"], G["cond"]
        nseg = ntok // 1024
        for l in range(NL):
            P = PR[l][cj]
            if l == 0:
                xs_ap, xs_dep = G["xin"], G["xin_dep"]
            else:
                xs_ap, xs_dep = x2T.t, x2T.d
            AR.reset()
            hT = AR.tile([128, 16, 1024], BF16)
            xblk_pool = Pool([AR.tile([128, 16, 256], F32) for _ in range(2)])
            xblk_sq = Pool([AR.tile([128, 16, 256], BF16) for _ in range(2)])
            stok = Pool([AR.tile([128, 512], BF16) for _ in range(3)])
            stokf = Pool([AR.tile([128, 512], F32) for _ in range(3)])
            small = Pool([AR.tile([128, 2], F32) for _ in range(4)])
            wA = Pool([AR.tile([128, 16, 512], BF16) for _ in range(2)])
            for sg in range(nseg):
                c0 = sg * 1024
                for b4 in range(4):
                    norm_mod(xs_ap, xs_dep, c0 + b4 * 256, 256, hT, b4 * 256, P.t[:, 0, :], P.t[:, 1, :], xblk_pool)
                fm = []
                for cb in range(6):
                    fm.append((w_in[l], cb * 512, 512, cb * 512, "copy", True))
                for cb in range(4):
                    fm.append((w_in[l], 4096 + cb * 512, 512, R_G + cb * 512, "copy", True))
                fm.append((w_in[l], 4096 + 2048, 256, R_G + 2048, "copy", True))
                fm.append((w_in_kr[l], 0, 64, R_KR, "copy", False))
                for cb in range(12):
                    fm.append((w_in[l], 6464 + cb * 512, 512, R_GATE + cb * 512, "sig", True))
                tokm = [(2048, 512, ktok, 0), (2560, 512, ktok, 512), (3072, 512, vtok, 0), (3584, 512, vtok, 512)]
                if is_p:
                    tokm += [(5888, 512, "ckv", 0), (6400, 64, "kr", 0)]
                tok_by_col = {t[0]: t for t in tokm}
                for bi, (wsrc, col0, ncols, zr0, epi, is_main) in enumerate(fm):
                    w = wA.get()
                    load_w(w, 16, ncols, wsrc[:, col0:col0 + ncols], wcA[l], bi, is_p)
                    nm = (ncols + 127) // 128
                    for mi in range(nm):
                        mw = min(128, ncols - mi * 128)
                        st = stg_bf.get()
                        for tb in range(2):
                            ps = ps_mm.get()
                            for kc in range(16):
                                mm(ps, ps.t[0:mw, :], w.t[:, kc, mi * 128:mi * 128 + mw], hT.t[:, kc, tb * 512:(tb + 1) * 512], kc == 0, kc == 15, [w, hT])
                            if epi == "sig":
                                act(st, st.t[0:mw, tb * 512:(tb + 1) * 512], ps.t[0:mw, :], AF.Sigmoid, [ps])
                            elif (mi + tb) % 2 == 0:
                                act(st, st.t[0:mw, tb * 512:(tb + 1) * 512], ps.t[0:mw, :], AF.Copy, [ps])
                            else:
                                c.op("dve", lambda h, st=st, ps=ps, mw=mw, tb=tb: h.tensor_copy(st.t[0:mw, tb * 512:(tb + 1) * 512], ps.t[0:mw, :]), reads=[ps], writes=[st])
                        c.dma("sp", zT.t[zr0 + mi * 128:zr0 + mi * 128 + mw, c0:c0 + 1024], st.t[0:mw, :], reads=[st], writes=[zT])
                    if is_main and col0 in tok_by_col:
                        _, _, dst, dcol = tok_by_col.pop(col0)
                        for tti in range(8):
                            ps = ps_mm.get()
                            for kc in range(16):
                                mm(ps, ps.t[:, 0:ncols], hT.t[:, kc, tti * 128:(tti + 1) * 128], w.t[:, kc, 0:ncols], kc == 0, kc == 15, [w, hT])
                            so = stok.get()
                            act(so, so.t[:, 0:ncols], ps.t[:, 0:ncols], AF.Copy, [ps])
                            c.dma("sp", dst.t[c0 + tti * 128:c0 + (tti + 1) * 128, dcol:dcol + ncols], so.t[:, 0:ncols], reads=[so], writes=[dst])
                for li_, (col0, ncols, dst, dcol) in enumerate(list(tok_by_col.values())):
                    w = wA.get()
                    if isinstance(dst, str):
                        c.dma("pool", w.t[:, :, 0:ncols], w_in[l, :, col0:col0 + ncols].rearrange("(k p) n -> p k n", p=128), writes=[w])
                    else:
                        load_w(w, 16, ncols, w_in[l, :, col0:col0 + ncols], wcA[l], 29 + li_, is_p)
                    for tti in range(8):
                        ps = ps_mm.get()
                        for kc in range(16):
                            mm(ps, ps.t[:, 0:ncols], hT.t[:, kc, tti * 128:(tti + 1) * 128], w.t[:, kc, 0:ncols], kc == 0, kc == 15, [w, hT])
                        if dst == "ckv":
                            sqf = stokf.get()
                            ss = small.get()
                            c.op("act", lambda h, sqf=sqf, ps=ps, ss=ss: h.activation(sqf.t[:, :], ps.t[:, :], AF.Square, accum_out=ss.t[:, 0:1]),
                                 reads=[ps], writes=[sqf, ss])
                            act(ss, ss.t[:, 1:2], ss.t[:, 0:1], AF.Sqrt, [ss, eps_t], bias=eps_t.t[:, 0:1], scale=1.0 / 512)
                            recip(ss, ss.t[:, 1:2], ss.t[:, 1:2], [ss])
                            so = stokf.get()
                            stt(so, so.t[:, :], ps.t[:, :], ss.t[:, 1:2], kvn_rep[l].t[:, :], ALU.mult, ALU.mult, [ps, ss, kvn_rep[l]])
                            sq_i, t0 = (tti * 128) // LP, (tti * 128) % LP
                            c.dma("sp", ckv_o[sq_i, l, t0:t0 + 128, :], so.t[:, :], reads=[so], writes=[d_out])
                        elif dst == "kr":
                            so = stokf.get()
                            act(so, so.t[:, 0:64], ps.t[:, 0:64], AF.Copy, [ps])
                            sq_i, t0 = (tti * 128) // LP, (tti * 128) % LP
                            c.dma("sp", kr_o[sq_i, l, t0:t0 + 128, :], so.t[:, 0:64], reads=[so], writes=[d_out])
                        else:
                            so = stok.get()
                            act(so, so.t[:, 0:ncols], ps.t[:, 0:ncols], AF.Copy, [ps])
                            c.dma("sp", dst.t[c0 + tti * 128:c0 + (tti + 1) * 128, dcol:dcol + ncols], so.t[:, 0:ncols], reads=[so], writes=[dst])
            if debug_stop == "A":
                break
            AR.reset()
            HL = L // 2
            Ec = AR.tile([128, L], F32)
            Es = AR.tile([128, L], F32)
            tq = AR.tile([128, HL], F32)
            btil = [AR.tile([128, L], F32) for _ in range(2)]
            sbf = [[AR.tile([128, ntok], BF16) for _ in range(2)] for _ in range(2)]
            u_pool = Pool([AR.tile([32, ntok], BF16) for _ in range(1)])
            tmp32 = Pool([AR.tile([128, 512], F32) for _ in range(4)])
            yj_p = Pool([AR.tile([32, 512], F32) for _ in range(2)])
            g_p = Pool([AR.tile([32, 512], F32) for _ in range(3)])
            wbj = Pool([AR.tile([32, 128], BF16) for _ in range(8)])
            wcj = Pool([AR.tile([128, 32], BF16) for _ in range(4)])
            fin = [[AR.tile([128, 2, 32], F32) for _ in range(2)] for _ in range(nseq)] if is_p else None
            fin_tmp = Pool([AR.tile([128, 2], F32) for _ in range(4)])
            BLK = min(512, L)
            br_t, bi_t = btil
            for j in range(32):
                uj = u_pool.get()
                c.dma("sp", uj.t[:, :], zT.t[R_U + j * 32:R_U + (j + 1) * 32, 0:ntok], reads=[zT], writes=[uj])
                for d in range(2):
                    wbr, wbi = wbj.get(), wbj.get()
                    c.dma("sp", wbr.t[:, :], wb_d.t[l, d, 0, :, j, :], reads=[wb_d], writes=[wbr])
                    c.dma("sp", wbi.t[:, :], wb_d.t[l, d, 1, :, j, :], reads=[wb_d], writes=[wbi])
                    dc, ds = s5_dc[l][d], s5_ds[l][d]
                    c.op("dve", lambda h, dc=dc, j=j, Ec=Ec: h.tensor_copy(Ec.t[:, 0:1], dc.t[:, 0, j:j + 1]), reads=[dc], writes=[Ec])
                    c.op("dve", lambda h, ds=ds, j=j, Es=Es: h.tensor_copy(Es.t[:, 0:1], ds.t[:, 0, j:j + 1]), reads=[ds], writes=[Es])
                    n = 1
                    k = 0
                    while n < L:
                        Ck, Sk = dc.t[:, k, j:j + 1], ds.t[:, k, j:j + 1]
                        ts(tq, tq.t[:, 0:n], Es.t[:, 0:n], Sk, None, ALU.mult, None, [Es, ds])
                        stt(Ec, Ec.t[:, n:2 * n], Ec.t[:, 0:n], Ck, tq.t[:, 0:n], ALU.mult, ALU.subtract, [Ec, dc, tq])
                        ts(tq, tq.t[:, 0:n], Ec.t[:, 0:n], Sk, None, ALU.mult, None, [Ec, ds])
                        stt(Es, Es.t[:, n:2 * n], Es.t[:, 0:n], Ck, tq.t[:, 0:n], ALU.mult, ALU.add, [Es, dc, tq])
                        n *= 2
                        k += 1
                    mg = s5_mag[l][d].t[:, j:j + 1]
                    magb = bass.AP(mg.tensor, mg.offset, [list(mg.ap[0]), [0, L]])
                    sre, sim = sbf[d]
                    for s in range(nseq):
                        s0 = s * L
                        for blk in range(L // BLK):
                            cols = slice(s0 + blk * BLK, s0 + (blk + 1) * BLK)
                            if d == 0:
                                tcols = slice(blk * BLK, (blk + 1) * BLK)
                            else:
                                tcols = slice(L - (blk + 1) * BLK, L - blk * BLK)
                            pr, pi = ps_mm.get(), ps_mm.get()
                            mm(pr, pr.t[:, 0:BLK], wbr.t[:, :], uj.t[:, cols], True, True, [wbr, uj])
                            mm(pi, pi.t[:, 0:BLK], wbi.t[:, :], uj.t[:, cols], True, True, [wbi, uj])
                            prv = pr.t[:, 0:BLK] if d == 0 else rev(pr.t[:, 0:BLK])
                            piv = pi.t[:, 0:BLK] if d == 0 else rev(pi.t[:, 0:BLK])
                            t1, t2 = tmp32.get(), tmp32.get()
                            tt(t1, t1.t[:, 0:BLK], prv, Ec.t[:, tcols], ALU.mult, [pr, Ec])
                            tt(t2, t2.t[:, 0:BLK], piv, Es.t[:, tcols], ALU.mult, [pi, Es])
                            tt(br_t, br_t.t[:, tcols], t1.t[:, 0:BLK], t2.t[:, 0:BLK], ALU.add, [t1, t2])
                            t3, t4 = tmp32.get(), tmp32.get()
                            tt(t3, t3.t[:, 0:BLK], piv, Ec.t[:, tcols], ALU.mult, [pi, Ec])
                            tt(t4, t4.t[:, 0:BLK], prv, Es.t[:, tcols], ALU.mult, [pr, Es])
                            tt(bi_t, bi_t.t[:, tcols], t3.t[:, 0:BLK], t4.t[:, 0:BLK], ALU.subtract, [t3, t4])
                        if is_p and l == 0 and j == 0 and s == 0:
                            dump("bt_r%d" % d, br_t.t[:, :], [128, L], F32, [br_t])
                            dump("bt_i%d" % d, bi_t.t[:, :], [128, L], F32, [bi_t])
                        for ri, bt in enumerate(btil):
                            if is_p:
                                init = 0.0
                                rd = [bt, s5_mag[l][d]]
                            else:
                                init = s5_init[l][d].t[:, j, ri:ri + 1]
                                rd = [bt, s5_mag[l][d], s5_init[l][d]]
                            c.op("dve", lambda h, bt=bt, magb=magb, init=init: h.tensor_tensor_scan(bt.t[:, :], magb, bt.t[:, :], init, ALU.mult, ALU.add),
                                 reads=rd, writes=[bt])
                        if is_p and l == 0 and j == 0 and s == 0:
                            dump("Ec%d" % d, Ec.t[:, :], [128, L], F32, [Ec])
                            dump("Es%d" % d, Es.t[:, :], [128, L], F32, [Es])
                            dump("sr%d" % d, br_t.t[:, :], [128, L], F32, [br_t])
                            dump("si%d" % d, bi_t.t[:, :], [128, L], F32, [bi_t])
                        for q0 in range(0, L, 512):
                            qn_ = min(512, L - q0)
                            tc = slice(q0, q0 + qn_)
                            if d == 0:
                                ore, oim = sre.t[:, s0 + q0:s0 + q0 + qn_], sim.t[:, s0 + q0:s0 + q0 + qn_]
                            else:
                                ore = rev(sre.t[:, s0 + L - q0 - qn_:s0 + L - q0])
                                oim = rev(sim.t[:, s0 + L - q0 - qn_:s0 + L - q0])
                            t1, t2 = tmp32.get(), tmp32.get()
                            tt(t1, t1.t[:, 0:qn_], br_t.t[:, tc], Ec.t[:, tc], ALU.mult, [br_t, Ec])
                            tt(t2, t2.t[:, 0:qn_], bi_t.t[:, tc], Es.t[:, tc], ALU.mult, [bi_t, Es])
                            tt(sre, ore, t1.t[:, 0:qn_], t2.t[:, 0:qn_], ALU.subtract, [t1, t2])
                            t3, t4 = tmp32.get(), tmp32.get()
                            tt(t3, t3.t[:, 0:qn_], bi_t.t[:, tc], Ec.t[:, tc], ALU.mult, [bi_t, Ec])
                            tt(t4, t4.t[:, 0:qn_], br_t.t[:, tc], Es.t[:, tc], ALU.mult, [br_t, Es])
                            tt(sim, oim, t3.t[:, 0:qn_], t4.t[:, 0:qn_], ALU.add, [t3, t4])
                        if is_p:
                            ft = fin_tmp.get()
                            f_ = fin[s][d]
                            e_c, e_s = Ec.t[:, L - 1:L], Es.t[:, L - 1:L]
                            tt(ft, ft.t[:, 0:1], br_t.t[:, L - 1:L], e_c, ALU.mult, [br_t, Ec])
                            tt(ft, ft.t[:, 1:2], bi_t.t[:, L - 1:L], e_s, ALU.mult, [bi_t, Es])
                            tt(f_, f_.t[:, 0, j:j + 1], ft.t[:, 0:1], ft.t[:, 1:2], ALU.subtract, [ft])
                            ft = fin_tmp.get()
                            tt(ft, ft.t[:, 0:1], bi_t.t[:, L - 1:L], e_c, ALU.mult, [bi_t, Ec])
                            tt(ft, ft.t[:, 1:2], br_t.t[:, L - 1:L], e_s, ALU.mult, [br_t, Es])
                            tt(f_, f_.t[:, 1, j:j + 1], ft.t[:, 0:1], ft.t[:, 1:2], ALU.add, [ft])
                wcr, wci = wcj.get(), wcj.get()
                c.dma("sp", wcr.t[:, :], wc_d.t[l, 0, :, j, :], reads=[wc_d], writes=[wcr])
                c.dma("sp", wci.t[:, :], wc_d.t[l, 1, :, j, :], reads=[wc_d], writes=[wci])
                for blk in range(ntok // 512):
                    cols = slice(blk * 512, (blk + 1) * 512)
                    ps = ps_mm.get()
                    mm(ps, ps.t[0:32, :], wcr.t[:, :], sbf[0][0].t[:, cols], True, False, [wcr, sbf[0][0]])
                    mm(ps, ps.t[0:32, :], wci.t[:, :], sbf[0][1].t[:, cols], False, False, [wci, sbf[0][1]])
                    mm(ps, ps.t[0:32, :], wcr.t[:, :], sbf[1][0].t[:, cols], False, False, [wcr, sbf[1][0]])
                    mm(ps, ps.t[0:32, :], wci.t[:, :], sbf[1][1].t[:, cols], False, True, [wci, sbf[1][1]])
                    yj = yj_p.get()
                    stt(yj, yj.t[:, :], uj.t[:, cols], s5_dsk[l].t[:, j:j + 1], ps.t[0:32, :], ALU.mult, ALU.add, [uj, s5_dsk[l], ps])
                    g1_ = g_p.get()
                    tt(g1_, g1_.t[:, :], yj.t[:, :], yj.t[:, :], ALU.mult, [yj])
                    ts(g1_, g1_.t[:, :], g1_.t[:, :], 0.044715, 1.0, ALU.mult, ALU.add, [g1_])
                    tt(g1_, g1_.t[:, :], g1_.t[:, :], yj.t[:, :], ALU.mult, [g1_, yj])
                    act(g1_, g1_.t[:, :], g1_.t[:, :], AF.Sigmoid, [g1_], scale=2.0 * math.sqrt(2.0 / math.pi))
                    so = stg_bf.get()
                    tt(so, so.t[0:32, 0:512], g1_.t[:, :], yj.t[:, :], ALU.mult, [g1_, yj])
                    c.dma("sp", ygT.t[j * 32:(j + 1) * 32, cols], so.t[0:32, 0:512], reads=[so], writes=[ygT])
            if is_p:
                for s in range(nseq):
                    for d in range(2):
                        c.dma("sp", s5_o[d, s, l].rearrange("r q j -> q r j"), fin[s][d].t[:, :, :], reads=[fin[s][d]], writes=[d_out])
            AR.reset()
            wg = AR.tile([128, 8, 1024], BF16)
            c.dma("pool", wg.t[:, :, :], glu_w[l].rearrange("(k p) n -> p k n", p=128), writes=[wg])
            yg_p = Pool([AR.tile([128, 8, 512], BF16) for _ in range(2)])
            sg_p = Pool([AR.tile([128, 512], F32) for _ in range(3)])
            for tb in range(ntok // 512):
                cols = slice(tb * 512, (tb + 1) * 512)
                yg = yg_p.get()
                c.dma("sp", yg.t[:, :, :], ygT.t[:, cols].rearrange("(k p) n -> p k n", p=128), reads=[ygT], writes=[yg])
                for m in range(8):
                    ps = ps_mm.get()
                    for kc in range(8):
                        mm(ps, ps.t[:, :], wg.t[:, kc, m * 128:(m + 1) * 128], yg.t[:, kc, :], kc == 0, kc == 7, [wg, yg])
                    sg = sg_p.get()
                    act(sg, sg.t[:, :], ps.t[:, :], AF.Sigmoid, [ps, glub[l]], bias=glub[l].t[:, m:m + 1])
                    so = stg_bf.get()
                    tt(so, so.t[:, 0:512], sg.t[:, :], yg.t[:, m, :], ALU.mult, [sg, yg])
                    c.dma("sp", yT.t[m * 128:(m + 1) * 128, cols], so.t[:, 0:512], reads=[so], writes=[yT])
            AR.reset()
            NCH = L // 128
            rDT, rqp, rkp, rds = ret_tables(l)
            qT_ = AR.tile([128, ntok], BF16)
            kT_ = AR.tile([128, ntok], BF16)
            qf_ = AR.tile([128, ntok], BF16)
            qb_ = AR.tile([128, ntok], BF16)
            kt_ = AR.tile([128, ntok // 128, 128], BF16)
            vt_ = AR.tile([128, ntok // 128, 128], BF16)
            kf_ = AR.tile([128, ntok // 128, 128], BF16)
            kb_ = AR.tile([128, ntok // 128, 128], BF16)
            acc = AR.tile([128, ntok], F32)
            Sst = [[AR.tile([128, 128], F32) for _ in range(2)] for _ in range(2)]
            Sbf = Pool([AR.tile([128, 128], BF16) for _ in range(3)])
            pt_p = Pool([AR.tile([128, 128], BF16) for _ in range(3)])
            gt_p = Pool([AR.tile([128, 512], BF16) for _ in range(2)])
            w32 = Pool([AR.tile([128, 512], F32) for _ in range(6)])
            obf_p = Pool([AR.tile([128, 512], BF16) for _ in range(2)])
            for hh in range(8):
                c.dma("sp", qT_.t[:, :], zT.t[R_Q + hh * 128:R_Q + (hh + 1) * 128, 0:ntok], reads=[zT], writes=[qT_])
                c.dma("sp", kT_.t[:, :], zT.t[R_K + hh * 128:R_K + (hh + 1) * 128, 0:ntok], reads=[zT], writes=[kT_])
                c.dma("sp", kt_.t[:, :, :], ktok.t[0:ntok, hh * 128:(hh + 1) * 128].rearrange("(c j) d -> j c d", j=128), reads=[ktok], writes=[kt_])
                c.dma("sp", vt_.t[:, :, :], vtok.t[0:ntok, hh * 128:(hh + 1) * 128].rearrange("(c j) d -> j c d", j=128), reads=[vtok], writes=[vt_])
                q3 = qT_.t[:, :].rearrange("p (a b) -> p a b", b=128)
                tt(qf_, qf_.t[:, :].rearrange("p (a b) -> p a b", b=128), q3, bc_mid(rqp.t[:, 0, hh, :], ntok // 128), ALU.mult, [qT_, rqp])
                tt(qb_, qb_.t[:, :].rearrange("p (a b) -> p a b", b=128), q3, bc_mid(rqp.t[:, 1, hh, :], ntok // 128), ALU.mult, [qT_, rqp])
                ts(kf_, kf_.t[:, :, :], kt_.t[:, :, :], rkp.t[:, 0, hh:hh + 1], None, ALU.mult, None, [kt_, rkp])
                ts(kb_, kb_.t[:, :, :], kt_.t[:, :, :], rkp.t[:, 1, hh:hh + 1], None, ALU.mult, None, [kt_, rkp])
                for s in range(nseq):
                    for d in range(2):
                        Sd = Sst[d]
                        S = Sd[0]
                        if is_p:
                            c.op("pool", lambda h, S=S: h.memset(S.t[:, :], 0.0), writes=[S])
                        else:
                            c.dma("sp", S.t[:, :], ret_s0[l, d, hh], writes=[S])
                        order = range(NCH) if d == 0 else range(NCH - 1, -1, -1)
                        qd = qf_ if d == 0 else qb_
                        kd = kf_ if d == 0 else kb_
                        order = list(order)

                        def make_pt(ch_):
                            g_ = s * NCH + ch_
                            c_ = slice(g_ * 128, (g_ + 1) * 128)
                            pa = ps_mm.get()
                            mm(pa, pa.t[:, 0:128], kT_.t[:, c_], qT_.t[:, c_], True, True, [kT_, qT_])
                            pt_ = pt_p.get()
                            tt(pt_, pt_.t[:, :], pa.t[:, 0:128], rDT.t[:, hh, :], ALU.mult, [pa, rDT])
                            return pt_
                        pt_next = make_pt(order[0]) if d == 0 else None
                        for it_, ch in enumerate(order):
                            gc = s * NCH + ch
                            cs_ = slice(gc * 128, (gc + 1) * 128)
                            cur, nxt = Sd[it_ % 2], Sd[(it_ + 1) % 2]
                            pS = ps_mm.get()
                            mm(pS, pS.t[:, 0:128], kd.t[:, gc, :], vt_.t[:, gc, :], True, True, [kd, vt_])
                            stt(nxt, nxt.t[:, :], cur.t[:, :], rds.t[:, d, hh:hh + 1], pS.t[:, 0:128], ALU.mult, ALU.add, [cur, rds, pS])
                            sb_ = Sbf.get()
                            act(sb_, sb_.t[:, :], cur.t[:, :], AF.Copy, [cur])
                            po = ps_mm.get()
                            if d == 0:
                                pt = pt_next
                                if it_ + 1 < NCH:
                                    pt_next = make_pt(order[it_ + 1])
                                mm(po, po.t[:, 0:128], vt_.t[:, gc, :], pt.t[:, :], True, False, [vt_, pt])
                                mm(po, po.t[:, 0:128], sb_.t[:, :], qd.t[:, cs_], False, True, [sb_, qd])
                                act(acc, acc.t[:, cs_], po.t[:, 0:128], AF.Copy, [po])
                            else:
                                mm(po, po.t[:, 0:128], sb_.t[:, :], qd.t[:, cs_], True, True, [sb_, qd])
                                tt(acc, acc.t[:, cs_], acc.t[:, cs_], po.t[:, 0:128], ALU.add, [acc, po])
                        if is_p:
                            Sfin = Sd[NCH % 2]
                            c.dma("sp", ret_o[d, s, l, hh], Sfin.t[:, :], reads=[Sfin], writes=[d_out])
                for tb in range(ntok // 512):
                    cols = slice(tb * 512, (tb + 1) * 512)
                    ob = obf_p.get()
                    act(ob, ob.t[:, :], acc.t[:, cols], AF.Copy, [acc])
                    sq = sq_p.get()
                    act(sq, sq.t[:, :], acc.t[:, cols], AF.Square, [acc])
                    pm, pv = ps_aux.get(), ps_aux.get()
                    mm(pm, pm.t[:, :], ones128.t[:, :], ob.t[:, :], True, True, [ones128, ob])
                    mm(pv, pv.t[:, :], ones128.t[:, :], sq.t[:, :], True, True, [ones128, sq])
                    mean = w32.get()
                    act(mean, mean.t[:, :], pm.t[:, :], AF.Copy, [pm])
                    var = w32.get()
                    tt(var, var.t[:, :], mean.t[:, :], mean.t[:, :], ALU.mult, [mean])
                    tt(var, var.t[:, :], pv.t[:, :], var.t[:, :], ALU.subtract, [pv, var])
                    ts(var, var.t[:, :], var.t[:, :], 0.0, None, ALU.max, None, [var])
                    act(var, var.t[:, :], var.t[:, :], AF.Sqrt, [var, eps_t], bias=eps_t.t[:, 0:1])
                    recip(var, var.t[:, :], var.t[:, :], [var])
                    cen = w32.get()
                    tt(cen, cen.t[:, :], acc.t[:, cols], mean.t[:, :], ALU.subtract, [acc, mean])
                    tt(cen, cen.t[:, :], cen.t[:, :], var.t[:, :], ALU.mult, [cen, var])
                    gt = gt_p.get()
                    c.dma("sp", gt.t[:, :], zT.t[R_G + hh * 128:R_G + (hh + 1) * 128, cols], reads=[zT], writes=[gt])
                    sgt = w32.get()
                    act(sgt, sgt.t[:, :], gt.t[:, :], AF.Silu, [gt])
                    so = stg_bf.get()
                    stt(so, so.t[:, 0:512], cen.t[:, :], r_ng[l].t[:, hh:hh + 1], sgt.t[:, :], ALU.mult, ALU.mult, [cen, r_ng[l], sgt])
                    c.dma("sp", yT.t[DM + hh * 128:DM + (hh + 1) * 128, cols], so.t[:, 0:512], reads=[so], writes=[yT])
            AR.reset()
            cqn = AR.tile([128, 6, 512], BF16)
            wq_all = AR.tile([128, 6, 1536], BF16)
            c.dma("pool", wq_all.t[:, :, :], w_uq[l].rearrange("(k p) n -> p k n", p=128), writes=[wq_all])
            ld_p = Pool([AR.tile([128, 6, 512], BF16) for _ in range(2)])
            ldsq = Pool([AR.tile([128, 6, 512], BF16) for _ in range(2)])
            k32 = Pool([AR.tile([32, 512], F32) for _ in range(6)])
            rope_p = Pool([AR.tile([32, 512], F32) for _ in range(4)])
            qst = Pool([AR.tile([128, 512], BF16) for _ in range(3)])
            qst32 = Pool([AR.tile([32, 512], BF16) for _ in range(4)])
            for tb in range(ntok // 512):
                cols = slice(tb * 512, (tb + 1) * 512)
                ld = ld_p.get()
                c.dma("sp", ld.t[:, :, :], zT.t[R_CQ:R_CQ + 768, cols].rearrange("(k p) n -> p k n", p=128), reads=[zT], writes=[ld])
                sq = ldsq.get()
                act(sq, sq.t[:, :, :], ld.t[:, :, :], AF.Square, [ld])
                ps = ps_aux.get()
                for kc in range(6):
                    mm(ps, ps.t[:, :], ones_bf.t[:, :], sq.t[:, kc, :], kc == 0, kc == 5, [ones_bf, sq])
                rs = rstd_p.get()
                rstd_from_ssq(ps, ps.t[:, :], rs, rs.t[:, 0:512], 1.0 / 768)
                for kc in range(6):
                    stt(cqn, cqn.t[:, kc, :], ld.t[:, kc, :], qn_t[l].t[:, kc:kc + 1], rs.t[:, 0:512], ALU.mult, ALU.mult, [ld, qn_t[l], rs])
                if not is_p:
                    rc, rsn = rope_p.get(), rope_p.get()
                    c.dma("sp", rc.t[:, :], rope_cs[0, :, cols], writes=[rc])
                    c.dma("sp", rsn.t[:, :], rope_cs[1, :, cols], writes=[rsn])
                for hh in range(8):
                    ps = ps_mm.get()
                    for kc in range(6):
                        mm(ps, ps.t[:, :], wq_all.t[:, kc, hh * 192:hh * 192 + 128], cqn.t[:, kc, :], kc == 0, kc == 5, [wq_all, cqn])
                    so = qst.get()
                    act(so, so.t[:, :], ps.t[:, :], AF.Copy, [ps])
                    c.dma("sp", qscr.t[hh, 0:128, cols], so.t[:, :], reads=[so], writes=[qscr])
                    pr = []
                    for hf in range(2):
                        p_ = ps_mm.get()
                        for kc in range(6):
                            mm(p_, p_.t[0:32, :], wq_all.t[:, kc, hh * 192 + 128 + hf * 32:hh * 192 + 160 + hf * 32], cqn.t[:, kc, :], kc == 0, kc == 5, [wq_all, cqn])
                        pr.append(p_)
                    o0, o1 = qst32.get(), qst32.get()
                    if is_p:
                        act(o0, o0.t[:, :], pr[0].t[0:32, :], AF.Copy, [pr[0]])
                        act(o1, o1.t[:, :], pr[1].t[0:32, :], AF.Copy, [pr[1]])
                    else:
                        a1_, a2_ = k32.get(), k32.get()
                        tt(a1_, a1_.t[:, :], pr[0].t[0:32, :], rc.t[:, :], ALU.mult, [pr[0], rc])
                        tt(a2_, a2_.t[:, :], pr[1].t[0:32, :], rsn.t[:, :], ALU.mult, [pr[1], rsn])
                        tt(o0, o0.t[:, :], a1_.t[:, :], a2_.t[:, :], ALU.subtract, [a1_, a2_])
                        a3_, a4_ = k32.get(), k32.get()
                        tt(a3_, a3_.t[:, :], pr[0].t[0:32, :], rsn.t[:, :], ALU.mult, [pr[0], rsn])
                        tt(a4_, a4_.t[:, :], pr[1].t[0:32, :], rc.t[:, :], ALU.mult, [pr[1], rc])
                        tt(o1, o1.t[:, :], a3_.t[:, :], a4_.t[:, :], ALU.add, [a3_, a4_])
                    c.dma("sp", qscr.t[hh, 128:160, cols], o0.t[:, :], reads=[o0], writes=[qscr])
                    c.dma("sp", qscr.t[hh, 160:192, cols], o1.t[:, :], reads=[o1], writes=[qscr])
            AR.reset()
            SK = L if is_p else L + PAST
            NKT = SK // 128
            ckv = AR.tile([128, 4, nseq * SK], BF16)
            kr = [AR.tile([32, nseq * SK], BF16) for _ in range(2)]
            qn_h = AR.tile([128, ntok], BF16)
            qr_h = [AR.tile([32, ntok], BF16) for _ in range(2)]
            kn_h = AR.tile([128, nseq * SK], BF16)
            v_h = AR.tile([128, nseq * NKT, 128], BF16)
            wk_h = AR.tile([128, 4, 128], BF16)
            wv_h = AR.tile([128, 4, 128], BF16)
            ld_p = Pool([AR.tile([128, 4, 512], BF16) for _ in range(2)])
            ldsq = Pool([AR.tile([128, 4, 512], BF16) for _ in range(1)])
            pt_p = Pool([AR.tile([128, 512], BF16) for _ in range(4)])
            w32 = Pool([AR.tile([128, 512], F32) for _ in range(3)])
            k32 = Pool([AR.tile([32, 512], F32) for _ in range(4)])
            rope_p = Pool([AR.tile([32, 512], F32) for _ in range(2)])
            kraw = Pool([AR.tile([32, 512], BF16) for _ in range(2)])
            for tb in range(ntok // 512):
                cols = slice(tb * 512, (tb + 1) * 512)
                ld = ld_p.get()
                c.dma("sp", ld.t[:, :, :], zT.t[R_CKV:R_CKV + 512, cols].rearrange("(k p) n -> p k n", p=128), reads=[zT], writes=[ld])
                sq = ldsq.get()
                act(sq, sq.t[:, :, :], ld.t[:, :, :], AF.Square, [ld])
                ps = ps_aux.get()
                for kc in range(4):
                    mm(ps, ps.t[:, :], ones_bf.t[:, :], sq.t[:, kc, :], kc == 0, kc == 3, [ones_bf, sq])
                rs = rstd_p.get()
                rstd_from_ssq(ps, ps.t[:, :], rs, rs.t[:, 0:512], 1.0 / 512)
                for kc in range(4):
                    stt(ckv, ckv.t[:, kc, cols], ld.t[:, kc, :], kvn_t[l].t[:, kc:kc + 1], rs.t[:, 0:512], ALU.mult, ALU.mult, [ld, kvn_t[l], rs])
                kw0, kw1 = kraw.get(), kraw.get()
                c.dma("sp", kw0.t[:, :], zT.t[R_KR:R_KR + 32, cols], reads=[zT], writes=[kw0])
                c.dma("sp", kw1.t[:, :], zT.t[R_KR + 32:R_KR + 64, cols], reads=[zT], writes=[kw1])
                if is_p:
                    c.op("dve", lambda h, kw0=kw0, cols=cols, k0_=kr[0]: h.tensor_copy(k0_.t[:, cols], kw0.t[:, :]), reads=[kw0], writes=[kr[0]])
                    c.op("dve", lambda h, kw1=kw1, cols=cols, k1_=kr[1]: h.tensor_copy(k1_.t[:, cols], kw1.t[:, :]), reads=[kw1], writes=[kr[1]])
                else:
                    rc, rsn = rope_p.get(), rope_p.get()
                    c.dma("sp", rc.t[:, :], rope_cs[0, :, cols], writes=[rc])
                    c.dma("sp", rsn.t[:, :], rope_cs[1, :, cols], writes=[rsn])
                    a1_, a2_ = k32.get(), k32.get()
                    tt(a1_, a1_.t[:, :], kw0.t[:, :], rc.t[:, :], ALU.mult, [kw0, rc])
                    tt(a2_, a2_.t[:, :], kw1.t[:, :], rsn.t[:, :], ALU.mult, [kw1, rsn])
                    tt(kr[0], kr[0].t[:, cols], a1_.t[:, :], a2_.t[:, :], ALU.subtract, [a1_, a2_])
                    a3_, a4_ = k32.get(), k32.get()
                    tt(a3_, a3_.t[:, :], kw0.t[:, :], rsn.t[:, :], ALU.mult, [kw0, rsn])
                    tt(a4_, a4_.t[:, :], kw1.t[:, :], rc.t[:, :], ALU.mult, [kw1, rc])
                    tt(kr[1], kr[1].t[:, cols], a3_.t[:, :], a4_.t[:, :], ALU.add, [a3_, a4_])
            if not is_p:
                c.dma("pool", ckv.t[:, :, L:L + PAST], cache_ckvT[l].rearrange("(k p) n -> p k n", p=128), writes=[ckv])
                for hf in range(2):
                    c.dma("pool", kr[hf].t[:, L:L + PAST], cache_kr[l, hf], writes=[kr[hf]])
            for hh in range(8):
                c.dma("pool", wk_h.t[:, :, :], w_uk[l, :, hh * 128:(hh + 1) * 128].rearrange("(k p) n -> p k n", p=128), writes=[wk_h])
                c.dma("pool", wv_h.t[:, :, :], w_uv[l, :, hh * 128:(hh + 1) * 128].rearrange("(k p) n -> p k n", p=128), writes=[wv_h])
                c.dma("sp", qn_h.t[:, :], qscr.t[hh, 0:128, 0:ntok], reads=[qscr], writes=[qn_h])
                c.dma("sp", qr_h[0].t[:, :], qscr.t[hh, 128:160, 0:ntok], reads=[qscr], writes=[qr_h[0]])
                c.dma("sp", qr_h[1].t[:, :], qscr.t[hh, 160:192, 0:ntok], reads=[qscr], writes=[qr_h[1]])
                nk_tot = nseq * SK
                for k0 in range(0, nk_tot, 512):
                    kn_ = min(512, nk_tot - k0)
                    ps = ps_mm.get()
                    for kc in range(4):
                        mm(ps, ps.t[:, 0:kn_], wk_h.t[:, kc, :], ckv.t[:, kc, k0:k0 + kn_], kc == 0, kc == 3, [wk_h, ckv])
                    act(kn_h, kn_h.t[:, k0:k0 + kn_], ps.t[:, 0:kn_], AF.Copy, [ps])
                for kt4 in range(0, nseq * NKT, 4):
                    nn = min(4, nseq * NKT - kt4)
                    ps = ps_mm.get()
                    for i in range(nn):
                        kt = kt4 + i
                        for kc in range(4):
                            mm(ps, ps.t[:, i * 128:(i + 1) * 128], ckv.t[:, kc, kt * 128:(kt + 1) * 128], wv_h.t[:, kc, :], kc == 0, kc == 3, [ckv, wv_h])
                    c.op("dve", lambda h, ps=ps, kt4=kt4, nn=nn, v_h=v_h: h.tensor_copy(v_h.t[:, kt4:kt4 + nn, :], ps.t[:, 0:nn * 128].rearrange("p (a b) -> p a b", b=128)),
                         reads=[ps], writes=[v_h])
                if is_p and l == 0 and hh == 0:
                    dump("qn", qn_h.t[:, :], [128, ntok], BF16, [qn_h])
                    dump("qr0", qr_h[0].t[:, :], [32, ntok], BF16, [qr_h[0]])
                    dump("kn", kn_h.t[:, :], [128, nseq * SK], BF16, [kn_h])
                    dump("kr0", kr[0].t[:, :], [32, nseq * SK], BF16, [kr[0]])
                    dump("vh", v_h.t[:, :, :], [128, nseq * NKT, 128], BF16, [v_h])
                    dump("ckv", ckv.t[:, :, :], [128, 4, nseq * SK], BF16, [ckv])
                QB = min(512, L)
                for s in range(nseq):
                    for qb in range(L // QB):
                        qc = slice(s * L + qb * QB, s * L + (qb + 1) * QB)
                        pend = []
                        for kt in range(NKT + 2):
                            if kt < NKT:
                                kc_ = slice(s * SK + kt * 128, s * SK + (kt + 1) * 128)
                                ps = ps_mm.get()
                                mm(ps, ps.t[:, 0:QB], kn_h.t[:, kc_], qn_h.t[:, qc], True, False, [kn_h, qn_h])
                                mm(ps, ps.t[:, 0:QB], kr[0].t[:, kc_], qr_h[0].t[:, qc], False, False, [kr[0], qr_h[0]])
                                mm(ps, ps.t[:, 0:QB], kr[1].t[:, kc_], qr_h[1].t[:, qc], False, True, [kr[1], qr_h[1]])
                                pt = pt_p.get()
                                act(pt, pt.t[:, 0:QB], ps.t[:, 0:QB], AF.Exp, [ps], scale=ATT_SCALE)
                                pend.append((kt, pt))
                            if kt >= 2:
                                k0, pt0 = pend.pop(0)
                                mm(ACC0, ACC0.t[:, 0:QB], v_h.t[:, s * NKT + k0, :], pt0.t[:, 0:QB], k0 == 0, k0 == NKT - 1, [v_h, pt0])
                                mm(ACC1, ACC1.t[:, 0:QB], ones_bf.t[:, :], pt0.t[:, 0:QB], k0 == 0, k0 == NKT - 1, [ones_bf, pt0])
                        rd = w32.get()
                        recip(rd, rd.t[:, 0:QB], ACC1.t[:, 0:QB], [ACC1])
                        if is_p and l == 0 and hh == 0 and s == 0:
                            dump("rden", rd.t[:, 0:QB], [128, QB], F32, [rd])
                            on_ = w32.get()
                            act(on_, on_.t[:, 0:QB], ACC0.t[:, 0:QB], AF.Copy, [ACC0])
                            dump("onum", on_.t[:, 0:QB], [128, QB], F32, [on_])
                        so = stg_bf.get()
                        tt(so, so.t[:, 0:QB], ACC0.t[:, 0:QB], rd.t[:, 0:QB], ALU.mult, [ACC0, rd])
                        c.dma("sp", yT.t[2 * DM + hh * 128:2 * DM + (hh + 1) * 128, qc], so.t[:, 0:QB], reads=[so], writes=[yT])
            if DEBUG and is_p and l == 0:
                c.dma("sp", dbg_y, yT.t[:, 0:1024], reads=[yT], writes=[d_out])
            AR.reset()
            merged = AR.tile([128, 16, 1024], BF16)
            yb_p = Pool([AR.tile([128, 8, 1024], BF16) for _ in range(2)])
            gt_p = Pool([AR.tile([128, 1024], BF16) for _ in range(2)])
            t32 = Pool([AR.tile([128, 512], F32) for _ in range(3)])
            xl_p = Pool([AR.tile([128, 1024], F32) for _ in range(2)])
            fl_p = Pool([AR.tile([128, 1024], F32) for _ in range(2)])
            wA = Pool([AR.tile([128, 16, 512], BF16) for _ in range(2)])
            for sg in range(nseg):
                c0 = sg * 1024
                for b in range(3):
                    yb = yb_p.get()
                    c.dma("sp", yb.t[:, :, :], yT.t[b * DM:(b + 1) * DM, c0:c0 + 1024].rearrange("(k p) n -> p k n", p=128), reads=[yT], writes=[yb])
                    for blk in range(4):
                        w = wA.get()
                        load_w(w, 8, 512, w_branch[l, b, :, blk * 512:(blk + 1) * 512], wcC[l], b * 4 + blk, is_p)
                        for mi in range(4):
                            m = blk * 4 + mi
                            gt = gt_p.get()
                            r0 = R_GATE + b * D + m * 128
                            c.dma("sp", gt.t[:, :], zT.t[r0:r0 + 128, c0:c0 + 1024], reads=[zT], writes=[gt])
                            for tb in range(2):
                                tc = slice(tb * 512, (tb + 1) * 512)
                                ps = ps_mm.get()
                                for kc in range(8):
                                    mm(ps, ps.t[:, :], w.t[:, kc, mi * 128:(mi + 1) * 128], yb.t[:, kc, tc], kc == 0, kc == 7, [w, yb])
                                if b == 0:
                                    tt(merged, merged.t[:, m, tc], ps.t[:, :], gt.t[:, tc], ALU.mult, [ps, gt])
                                else:
                                    tq_ = t32.get()
                                    tt(tq_, tq_.t[:, :], ps.t[:, :], gt.t[:, tc], ALU.mult, [ps, gt])
                                    tt(merged, merged.t[:, m, tc], merged.t[:, m, tc], tq_.t[:, :], ALU.add, [merged, tq_])
                for blk in range(4):
                    w = wA.get()
                    load_w(w, 16, 512, w_out[l, :, blk * 512:(blk + 1) * 512], wcC[l], 12 + blk, is_p)
                    for mi in range(4):
                        m = blk * 4 + mi
                        so = stg_f.get()
                        for tb in range(2):
                            tc = slice(tb * 512, (tb + 1) * 512)
                            ps = ps_mm.get()
                            for kc in range(16):
                                mm(ps, ps.t[:, :], w.t[:, kc, mi * 128:(mi + 1) * 128], merged.t[:, kc, tc], kc == 0, kc == 15, [w, merged])
                            act(so, so.t[:, tc], ps.t[:, :], AF.Copy, [ps])
                            sq = sq_p.get()
                            act(sq, sq.t[:, :], ps.t[:, :], AF.Square, [ps])
                            A_ = ACC0 if tb == 0 else ACC1
                            mm(A_, A_.t[:, :], ones_bf.t[:, :], sq.t[:, :], m == 0, m == 15, [ones_bf, sq])
                        c.dma("sp", fbuf.t[m * 128:(m + 1) * 128, :], so.t[:, :], reads=[so], writes=[fbuf])
                rs = rstd_p.get()
                rstd_from_ssq(ACC0, ACC0.t[:, :], rs, rs.t[:, 0:512], 1.0 / D)
                rstd_from_ssq(ACC1, ACC1.t[:, :], rs, rs.t[:, 512:1024], 1.0 / D)
                def ldC(m):
                    xl = xl_p.get()
                    c.dma("sp", xl.t[:, :], xs_ap[m * 128:(m + 1) * 128, c0:c0 + 1024], reads=[xs_dep], writes=[xl])
                    fl = fl_p.get()
                    c.dma("sp", fl.t[:, :], fbuf.t[m * 128:(m + 1) * 128, :], reads=[fbuf], writes=[fl])
                    return xl, fl
                nxt = ldC(0)
                for m in range(16):
                    xl, fl = nxt
                    if m + 1 < 16:
                        nxt = ldC(m + 1)
                    so = stg_f.get()
                    stt(so, so.t[:, :], fl.t[:, :], P.t[:, 2, m:m + 1], rs.t[:, :], ALU.mult, ALU.mult, [fl, rs] + A_reads)
                    tt(so, so.t[:, :], so.t[:, :], xl.t[:, :], ALU.add, [so, xl])
                    c.dma("sp", x1T.t[m * 128:(m + 1) * 128, c0:c0 + 1024], so.t[:, :], reads=[so], writes=[x1T])
            if DEBUG and is_p and l == 0:
                c.dma("sp", dbg_x1, x1T.t[:, 0:1024], reads=[x1T], writes=[d_out])
            AR.reset()
            NCOL = 520
            h2 = AR.tile([128, 16, NCOL], BF16)
            aT = AR.tile([128, 44, 512], BF16)
            xblk_pool = Pool([AR.tile([128, 16, 128], F32) for _ in range(1)])
            xblk_sq = Pool([AR.tile([128, 16, 128], BF16) for _ in range(1)])
            uv_p = Pool([AR.tile([128, NCOL], F32) for _ in range(2)])
            ug_p = Pool([AR.tile([128, NCOL], F32) for _ in range(2)])
            cv_p = Pool([AR.tile([128, 512], F32) for _ in range(4)])
            wD = Pool([AR.tile([128, 44, 128], BF16) for _ in range(2)])
            wU = Pool([AR.tile([128, 16, 256], BF16) for _ in range(4)])
            yo_ap, yo_dep = (x2T.t, x2T.d) if l == 0 else (G["yout"], d_out)
            nsegD = ntok // 512
            for sg in range(nsegD):
                c0 = sg * 512
                if is_p:
                    c.op("pool", lambda h, h2=h2: h.memset(h2.t[:, :, :], 0.0), writes=[h2])
                    for sq_i in range(2):
                        for q0 in range(0, 256, 128):
                            norm_mod(x1T.t, x1T.d, c0 + sq_i * 256 + q0, 128, h2, sq_i * 258 + 1 + q0, P.t[:, 3, :], P.t[:, 4, :], xblk_pool)
                    mmblocks = [(0, 258), (258, 258)]
                else:
                    lo = c0 - 1 if sg > 0 else c0
                    hi = c0 + 513 if sg < nsegD - 1 else c0 + 512
                    if sg == 0 or sg == nsegD - 1:
                        c.op("pool", lambda h, h2=h2: h.memset(h2.t[:, :, :], 0.0), writes=[h2])
                    q = lo
                    while q < hi:
                        n = min(128, hi - q)
                        norm_mod(x1T.t, x1T.d, q, n, h2, q - (c0 - 1), P.t[:, 3, :], P.t[:, 4, :], xblk_pool)
                        q += n
                    mmblocks = [(0, 512), (512, 2)]
                for hb in range(22):
                    wv_ = wU.get()
                    firstD = is_p and sg == 0
                    load_w(wv_, 16, 256, w_up[l, :, hb * 256:(hb + 1) * 256], wcU[l], hb, firstD)
                    wg_ = wU.get()
                    load_w(wg_, 16, 256, w_up[l, :, DFF + hb * 256:DFF + (hb + 1) * 256], wcU[l], 22 + hb, firstD)
                    for mi in range(2):
                        hm = hb * 2 + mi
                        uv, ug = uv_p.get(), ug_p.get()
                        for (wt, ut, eng) in ((wv_, uv, "act"), (wg_, ug, "dve")):
                            for (b0, bn) in mmblocks:
                                ps = ps_mm.get()
                                for kc in range(16):
                                    mm(ps, ps.t[:, 0:bn], wt.t[:, kc, mi * 128:(mi + 1) * 128], h2.t[:, kc, b0:b0 + bn], kc == 0, kc == 15, [wt, h2])
                                if eng == "act":
                                    act(ut, ut.t[:, b0:b0 + bn], ps.t[:, 0:bn], AF.Copy, [ps])
                                else:
                                    c.op("dve", lambda h, ut=ut, ps=ps, b0=b0, bn=bn: h.tensor_copy(ut.t[:, b0:b0 + bn], ps.t[:, 0:bn]), reads=[ps], writes=[ut])
                        cvs = []
                        for (ut, fm_) in ((uv, hm), (ug, 44 + hm)):
                            cv = cv_p.get()
                            w0, w1, w2 = (cvw[l].t[:, fm_, i:i + 1] for i in range(3))
                            bb = cvb[l].t[:, fm_:fm_ + 1]
                            if is_p:
                                u3 = ut.t[:, 0:516].rearrange("p (a b) -> p a b", b=258)
                                o3 = cv.t[:, :].rearrange("p (a b) -> p a b", b=256)
                                i0, i1, i2 = u3[:, :, 0:256], u3[:, :, 1:257], u3[:, :, 2:258]
                            else:
                                o3 = cv.t[:, :]
                                i0, i1, i2 = ut.t[:, 0:512], ut.t[:, 1:513], ut.t[:, 2:514]
                            ts(cv, o3, i1, w1, bb, ALU.mult, ALU.add, [ut, cvw[l], cvb[l]])
                            stt(cv, o3, i0, w0, o3, ALU.mult, ALU.add, [ut, cvw[l], cv])
                            stt(cv, o3, i2, w2, o3, ALU.mult, ALU.add, [ut, cvw[l], cv])
                            cvs.append(cv)
                        act(cvs[1], cvs[1].t[:, :], cvs[1].t[:, :], AF.Silu, [cvs[1]])
                        tt(aT, aT.t[:, hm, :], cvs[1].t[:, :], cvs[0].t[:, :], ALU.mult, [cvs[0], cvs[1]])
                for m in range(16):
                    w = wD.get()
                    load_w(w, 44, 128, w_down[l, :, m * 128:(m + 1) * 128], wcD[l], m, is_p and sg == 0)
                    so = stg_f.get()
                    ps = ps_mm.get()
                    for kc in range(44):
                        mm(ps, ps.t[:, :], w.t[:, kc, :], aT.t[:, kc, :], kc == 0, kc == 43, [w, aT])
                    act(so, so.t[:, 0:512], ps.t[:, :], AF.Copy, [ps])
                    sq = sq_p.get()
                    act(sq, sq.t[:, :], ps.t[:, :], AF.Square, [ps])
                    mm(ACC0, ACC0.t[:, :], ones_bf.t[:, :], sq.t[:, :], m == 0, m == 15, [ones_bf, sq])
                    c.dma("sp", fbuf.t[m * 128:(m + 1) * 128, 0:512], so.t[:, 0:512], reads=[so], writes=[fbuf])
                rs = rstd_p.get()
                rstd_from_ssq(ACC0, ACC0.t[:, :], rs, rs.t[:, 0:512], 1.0 / D)
                def ldD(m):
                    xl = cv_p.get()
                    c.dma("sp", xl.t[:, :], x1T.t[m * 128:(m + 1) * 128, c0:c0 + 512], reads=[x1T], writes=[xl])
                    fl = cv_p.get()
                    c.dma("sp", fl.t[:, :], fbuf.t[m * 128:(m + 1) * 128, 0:512], reads=[fbuf], writes=[fl])
                    return xl, fl
                nxt = ldD(0)
                for m in range(16):
                    xl, fl = nxt
                    if m + 1 < 16:
                        nxt = ldD(m + 1)
                    so = stg_f.get()
                    stt(so, so.t[:, 0:512], fl.t[:, :], P.t[:, 5, m:m + 1], rs.t[:, 0:512], ALU.mult, ALU.mult, [fl, rs] + A_reads)
                    tt(so, so.t[:, 0:512], so.t[:, 0:512], xl.t[:, :], ALU.add, [so, xl])
                    c.dma("sp", yo_ap[m * 128:(m + 1) * 128, c0:c0 + 512], so.t[:, 0:512], reads=[so], writes=[yo_dep])
            if DEBUG and is_p and l == 0:
                c.dma("sp", dbg_x2, x2T.t[:, 0:1024], reads=[x2T], writes=[d_out])
    c.finish()
    return nc, c


def host_inputs(I, core):
    f32 = np.float32
    b = core % 2
    A = lambda x: np.ascontiguousarray(x, dtype=f32)
    m = {}
    m["xTp"] = A(I["x_prompt"][4 * core:4 * core + 4].reshape(1024, D).T)
    m["xTs"] = A(I["x_sample"][b].T)
    conds = np.stack([I["c_ctx"], I["c"][b]], axis=-1)
    m["condT"] = A(conds.reshape(16, 128, 2).transpose(1, 0, 2))
    m["ada_w"] = A(I["ada_w"])
    m["ada_bT"] = A(I["ada_b"].reshape(NL, 96, 128).transpose(0, 2, 1))
    m["norm_gT"] = A(I["norm_g"].reshape(NL, 4, 16, 128).transpose(0, 3, 1, 2))
    m["w_in"] = A(I["w_in"])
    kcols = 6400 + np.concatenate([np.arange(0, 64, 2), np.arange(1, 64, 2)])
    m["w_in_kr"] = A(I["w_in"][:, :, kcols])

    def qj(a):
        sh = a.shape[:-2]
        return a.reshape(sh + (32, 2, 64)).reshape(sh + (32, 128)).swapaxes(-1, -2)
    ldt = np.broadcast_to(I["s5_log_dt"][..., None], I["s5_lam_re"].shape)
    m["s5_lam"] = A(np.stack([qj(I["s5_lam_re"]), qj(I["s5_lam_im"]), qj(ldt)], axis=2))
    nb = np.zeros((NL, 2, 128, 32, 32), f32)
    cbk = np.zeros((NL, 2, 128, 32, 32), f32)
    for r, (bsrc, csrc) in enumerate(((I["s5_b_re"], I["s5_c_re"]), (I["s5_b_im"], I["s5_c_im"]))):
        bb = bsrc.reshape(NL, 32, 2, 64, 16)
        cc = csrc.reshape(NL, 32, 2, 16, 64)
        for g2 in range(2):
            nb[:, r, g2 * 64:(g2 + 1) * 64, :, g2 * 16:(g2 + 1) * 16] = bb[:, :, g2].transpose(0, 2, 1, 3)
            cbk[:, r, g2 * 64:(g2 + 1) * 64, :, g2 * 16:(g2 + 1) * 16] = cc[:, :, g2].transpose(0, 3, 1, 2)
    m["s5_nb"] = nb
    m["s5_cb"] = cbk
    m["s5_dT"] = A(I["s5_d"].reshape(NL, 32, 32).transpose(0, 2, 1))
    s0 = np.stack([I["state_s5_fwd"][b], I["state_s5_bwd"][b]], axis=1)
    m["s5_s0"] = A(s0.reshape(NL, 2, 32, 128, 2).transpose(0, 1, 3, 2, 4))
    m["glu_w"] = A(I["s5_glu_w"])
    m["glu_bT"] = A(I["s5_glu_b"].reshape(NL, 8, 128).transpose(0, 2, 1))
    m["ret_dec"] = A(np.broadcast_to(I["ret_decay"].reshape(NL, 1, 16), (NL, 128, 16)))
    m["ret_ngT"] = A(I["ret_norm_g"].reshape(NL, 8, 128).transpose(0, 2, 1))
    m["ret_s0"] = A(np.stack([I["state_ret_fwd"][b], I["state_ret_bwd"][b]], axis=1))
    jj = np.arange(128)[:, None].astype(f32)
    ii = np.arange(128)[None, :].astype(f32)
    tab = np.stack([np.maximum(ii - jj, 0), (ii >= jj).astype(f32), np.maximum(jj - ii, 0), (jj >= ii).astype(f32),
                    np.broadcast_to(ii + 1, (128, 128)), np.broadcast_to(128 - ii, (128, 128))], axis=1)
    m["ret_tab"] = A(tab)
    pp = np.arange(128).astype(f32)
    m["ret_col"] = A(np.stack([127 - pp, pp, np.full(128, 128.0, f32)], axis=1))
    m["q_normT"] = A(I["mla_q_norm"].reshape(NL, 6, 128).transpose(0, 2, 1))
    m["kv_normT"] = A(I["mla_kv_norm"].reshape(NL, 4, 128).transpose(0, 2, 1))
    m["kv_norm_rep"] = A(np.broadcast_to(I["mla_kv_norm"][:, None, :], (NL, 128, 512)))
    hcols = np.concatenate([np.arange(128), 128 + np.arange(0, 64, 2), 128 + np.arange(1, 64, 2)])
    allc = np.concatenate([h * 192 + hcols for h in range(8)])
    m["w_uq"] = A(I["mla_w_uq"][:, :, allc])
    m["w_uk"] = A(I["mla_w_uk"])
    m["w_uv"] = A(I["mla_w_uv"])
    t = np.arange(LS)
    row = (t // 64).astype(f32)
    col = (t % 64).astype(f32)
    inv = (np.float32(10000.0) ** (-np.arange(16, dtype=f32) / np.float32(16))).astype(f32)
    ang = np.concatenate([row[:, None] * inv, col[:, None] * inv], axis=-1).astype(f32)
    m["rope_cs"] = A(np.stack([np.cos(ang).T, np.sin(ang).T]))
    m["cache_ckvT"] = A(I["cache_mla_ckv"][b].transpose(0, 2, 1))
    ck = I["cache_mla_krope"][b]
    m["cache_kr"] = A(np.stack([ck[:, :, 0::2], ck[:, :, 1::2]], axis=1).transpose(0, 1, 3, 2))
    m["w_branch"] = A(I["w_branch"])
    m["w_out"] = A(I["w_out"])
    m["w_up"] = A(I["ffn_w_up"])
    m["conv_wT"] = A(I["ffn_conv_w"].reshape(NL, 3, 88, 128).transpose(0, 3, 2, 1))
    m["conv_bT"] = A(I["ffn_conv_b"].reshape(NL, 88, 128).transpose(0, 2, 1))
    m["w_down"] = A(I["ffn_w_down"])
    return m


_CACHE = {}


def kernel(**inputs):
    I = {k: np.asarray(v) for k, v in inputs.items()}
    if "nc" not in _CACHE:
        _CACHE["nc"] = build()[0]
    nc = _CACHE["nc"]
    in_maps = [host_inputs(I, core) for core in range(8)]
    res = run_bass_kernel_spmd(nc, in_maps, core_ids=list(range(8)))
    R = res.results
    _CACHE["last"] = R
    y_prompt = np.concatenate([R[cidx]["yTp"].T.reshape(4, LP, D) for cidx in range(8)], axis=0)
    y_sample = np.stack([R[0]["yTs"].T, R[1]["yTs"].T], axis=0)
    ckv = np.concatenate([R[cidx]["ckv_o"] for cidx in range(8)], axis=0)
    krp = np.concatenate([R[cidx]["kr_o"] for cidx in range(8)], axis=0)
    retf = np.concatenate([R[cidx]["ret_o"][0] for cidx in range(8)], axis=0)
    retb = np.concatenate([R[cidx]["ret_o"][1] for cidx in range(8)], axis=0)

    def s5fix(a):
        a = a.transpose(0, 1, 4, 3, 2)
        return np.ascontiguousarray(a.reshape(a.shape[0], NL, 64, 64, 2))
    s5f = np.concatenate([s5fix(R[cidx]["s5_o"][0]) for cidx in range(8)], axis=0)
    s5b = np.concatenate([s5fix(R[cidx]["s5_o"][1]) for cidx in range(8)], axis=0)
    f = lambda a: np.ascontiguousarray(a, dtype=np.float32)
    return (f(y_prompt), f(y_sample), f(ckv), f(krp), f(retf), f(retb), f(s5f), f(s5b))
```
